# Optimizing a Trainium2 kernel written in Bass

```python
import jax
import jax.numpy as jnp
from jax import lax
import numpy as np

D_MODEL = 1024
BATCH = 16
SEQ = 2048
DEPTH = 2

GRID_W = 64
CTX_LEN = 256
HEAD_DIM = 64
MIX_HEADS = 4
MIX_W = MIX_HEADS * HEAD_DIM
N_BRANCH = 4
RWKV_LORA_W = 64
RWKV_LORA_A = 64
RWKV_LORA_G = 128
RWKV_GN_EPS = HEAD_DIM * 1e-5
RET_CHUNK = 128
HGRN_CHUNK = 64
HGRN_LB_FLOOR = 1e-20
ATTN_KV_HEADS = 2
ATTN_BLOCK = 128
ROPE_BASE = 10000.0
N_EXPERTS = 16
D_EXPERT = 1024
EC_CAPACITY = 2
LN_EPS = 1e-5
RMS_EPS = 1e-6
DEEPNORM_ALPHA = (2 * DEPTH) ** 0.25
DEEPNORM_BETA = (8 * DEPTH) ** -0.25

RWKV_COLS = (MIX_W, MIX_W, MIX_W, 2 * RWKV_LORA_W, 2 * RWKV_LORA_A, RWKV_LORA_G)
RET_COLS = (MIX_W, MIX_W, MIX_W, MIX_W)
HGRN_COLS = (MIX_W, 2 * MIX_W, MIX_W, MIX_W)
ATTN_COLS = (MIX_W, ATTN_KV_HEADS * HEAD_DIM, ATTN_KV_HEADS * HEAD_DIM)
GROUP_COLS = (sum(RWKV_COLS), sum(RET_COLS), sum(HGRN_COLS), sum(ATTN_COLS))
D_IN = sum(GROUP_COLS)

kernel_name = 'hybrid_bidir_diffusion_block'


def split_cols(z, sizes):
    return jnp.split(z, [int(s) for s in np.cumsum(sizes)[:-1]], axis=-1)


def heads(t):
    return t.reshape(t.shape[:-1] + (-1, HEAD_DIM))


def merge_heads(t):
    return t.reshape(t.shape[:-2] + (-1,))


def layer_norm(x, g, b):
    xf = x.astype(jnp.float32)
    mu = jnp.mean(xf, -1, keepdims=True)
    var = jnp.mean(jnp.square(xf - mu), -1, keepdims=True)
    return ((xf - mu) * lax.rsqrt(var + LN_EPS)).astype(x.dtype) * g + b


def rms_norm(x, g):
    xf = x.astype(jnp.float32)
    return (xf * lax.rsqrt(jnp.mean(xf * xf, -1, keepdims=True) + RMS_EPS)).astype(x.dtype) * g


def group_norm(y, g, b, eps):
    yf = y.astype(jnp.float32)
    mu = jnp.mean(yf, -1, keepdims=True)
    var = jnp.mean(jnp.square(yf - mu), -1, keepdims=True)
    return merge_heads(((yf - mu) * lax.rsqrt(var + eps)).astype(y.dtype)) * g + b


def rev_segments(t, n_ctx):
    return jnp.concatenate([jnp.flip(t[:, :n_ctx], 1), jnp.flip(t[:, n_ctx:], 1)], axis=1)


def both_dirs(t):
    return jnp.stack([t, t])


def orient(t, n_ctx):
    return jnp.stack([t[0], rev_segments(t[1], n_ctx)])


def deorient_sum(y, n_ctx):
    return y[0] + rev_segments(y[1], n_ctx)


def centred_shift(t, n_ctx):
    def nb(s):
        p = jnp.pad(s, ((0, 0), (1, 1), (0, 0)))
        return 0.5 * (p[:, :-2] + p[:, 2:])
    return jnp.concatenate([nb(t[:, :n_ctx]), nb(t[:, n_ctx:])], axis=1)


def axial_rope_tables(n_tokens, dtype):
    rows = n_tokens // GRID_W
    row = jnp.repeat(jnp.arange(rows), GRID_W)
    col = jnp.tile(jnp.arange(GRID_W), rows)
    n_freq = HEAD_DIM // 4
    inv = ROPE_BASE ** (-jnp.arange(n_freq, dtype=jnp.float32) / n_freq)
    ang = jnp.concatenate([row[:, None] * inv, col[:, None] * inv], axis=-1)
    return jnp.cos(ang).astype(dtype), jnp.sin(ang).astype(dtype)


def apply_rope(x, cos, sin):
    x1, x2 = x[..., :HEAD_DIM // 2], x[..., HEAD_DIM // 2:]
    c, s = cos[:, None], sin[:, None]
    return jnp.concatenate([x1 * c - x2 * s, x1 * s + x2 * c], axis=-1)


def rope_latent(t, n_ctx, cos, sin):
    return jnp.concatenate([t[:, :n_ctx], apply_rope(t[:, n_ctx:], cos, sin)], axis=1)


def rwkv7_scan(r, w, k, v, kk, a):
    def step(S, inp):
        r_t, w_t, k_t, v_t, kk_t, a_t = inp
        s_kk = jnp.einsum('zbhvk,zbhk->zbhv', S, kk_t)
        S = (S * w_t[..., None, :] - s_kk[..., None] * (kk_t * a_t)[..., None, :]
             + v_t[..., None] * k_t[..., None, :]).astype(S.dtype)
        return S, jnp.einsum('zbhvk,zbhk->zbhv', S, r_t)
    nz, bsz, _, nh, n = r.shape
    S0 = jnp.zeros((nz, bsz, nh, n, n), v.dtype)
    _, ys = lax.scan(step, S0, tuple(jnp.moveaxis(t, 2, 0) for t in (r, w, k, v, kk, a)))
    return jnp.moveaxis(ys, 0, 2)


def rwkv7_mixer(z, n_ctx, lo, mu, w0, w2, a0, a2, g2, k_k, k_a, r_k, ln_g, ln_b):
    z = z + mu * (centred_shift(z, n_ctx) - z)
    r, k, v, zw, za, zg = split_cols(z, RWKV_COLS)
    bsz, n_tok, _ = z.shape
    zw = zw.reshape(bsz, n_tok, 2, RWKV_LORA_W)
    za = za.reshape(bsz, n_tok, 2, RWKV_LORA_A)
    w = w0[:, None, None] + jnp.einsum('btzr,zrc->zbtc', jnp.tanh(zw), w2)
    decay = jnp.exp(-jnp.exp(-jax.nn.softplus(-w) - 0.5))
    a = jax.nn.sigmoid(a0[:, None, None] + jnp.einsum('btzr,zrc->zbtc', za, a2))
    kkf = heads(k * k_k).astype(jnp.float32)
    kk = (kkf * lax.rsqrt(jnp.sum(kkf * kkf, -1, keepdims=True) + 1e-12)).astype(z.dtype)
    k_dir = k * (1 + (a - 1) * k_a)
    ys = rwkv7_scan(orient(both_dirs(heads(r)), n_ctx), orient(heads(decay), n_ctx), orient(heads(k_dir), n_ctx),
                    orient(both_dirs(heads(v)), n_ctx), orient(both_dirs(kk), n_ctx), orient(heads(a), n_ctx))
    y = group_norm(deorient_sum(ys, n_ctx)[:, lo:], ln_g, ln_b, RWKV_GN_EPS)
    rh, kh, vh = heads(r[:, lo:]), heads(k[:, lo:]), heads(v[:, lo:])
    bonus = merge_heads(jnp.sum(rh * kh * heads(r_k), -1, keepdims=True) * vh)
    g = jax.nn.sigmoid(zg[:, lo:]) @ g2
    return (y + bonus) * g


def retention_chunkwise(q, k, v, log_gamma, chunk=RET_CHUNK):
    nz, bsz, n_tok, nh, dk = q.shape
    dv = v.shape[-1]
    n_chunk = n_tok // chunk

    def blocks(t):
        return jnp.moveaxis(t.reshape(nz, bsz, n_chunk, chunk, nh, t.shape[-1]), 2, 0)
    pos = jnp.arange(chunk, dtype=jnp.float32)
    rel = pos[:, None] - pos[None, :]
    lg = log_gamma[:, :, None, None]
    intra = jnp.where(rel >= 0, jnp.exp(lg * jnp.maximum(rel, 0.0)), 0.0).astype(q.dtype)
    lgt = log_gamma[:, None, :]
    q_decay = jnp.exp(lgt * (pos[:, None] + 1.0))[:, None, :, :, None].astype(q.dtype)
    k_decay = jnp.exp(lgt * (chunk - 1.0 - pos[:, None]))[:, None, :, :, None].astype(q.dtype)
    chunk_decay = jnp.exp(log_gamma * chunk)[:, None, :, None, None].astype(v.dtype)

    def step(R, inp):
        qc, kc, vc = inp
        s = jnp.einsum('zbthd,zbshd->zbhts', qc, kc) * intra[:, None]
        o = jnp.einsum('zbhts,zbshe->zbthe', s, vc) + jnp.einsum('zbthd,zbhde->zbthe', qc * q_decay, R)
        R = (R * chunk_decay + jnp.einsum('zbshd,zbshe->zbhde', kc * k_decay, vc)).astype(R.dtype)
        return R, o
    R0 = jnp.zeros((nz, bsz, nh, dk, dv), v.dtype)
    _, o = lax.scan(step, R0, (blocks(q), blocks(k), blocks(v)))
    return jnp.moveaxis(o, 0, 2).reshape(nz, bsz, n_tok, nh, dv)


def retention_mixer(z, n_ctx, lo, cos, sin, decay_logit, norm_g, norm_b):
    q, k, v, g = split_cols(z, RET_COLS)
    q = rope_latent(heads(q), n_ctx, cos, sin)
    k = rope_latent(heads(k), n_ctx, cos, sin) * HEAD_DIM ** -0.5
    log_gamma = jax.nn.log_sigmoid(decay_logit.astype(jnp.float32))
    y = retention_chunkwise(orient(both_dirs(q), n_ctx), orient(both_dirs(k), n_ctx),
                            orient(both_dirs(heads(v)), n_ctx), log_gamma)
    y = deorient_sum(y, n_ctx)[:, lo:]
    return group_norm(y, norm_g, norm_b, LN_EPS) * jax.nn.silu(g[:, lo:])


def gla_chunkwise(q, k, v, log_f, chunk=HGRN_CHUNK):
    nz, bsz, n_tok, nh, dk = q.shape
    dv = v.shape[-1]
    n_chunk = n_tok // chunk

    def blocks(t):
        return jnp.moveaxis(t.reshape(nz, bsz, n_chunk, chunk, nh, t.shape[-1]), 2, 0)
    causal = jnp.tril(jnp.ones((chunk, chunk), dtype=bool))

    def step(S, inp):
        qc, kc, vc, lfc = inp
        b = jnp.cumsum(lfc, axis=2)
        rel = b[:, :, :, None] - b[:, :, None, :]
        dec = jnp.where(causal[:, :, None, None], jnp.exp(jnp.minimum(rel, 0.0)), 0.0).astype(qc.dtype)
        A = jnp.einsum('zbthc,zbtshc,zbshc->zbhts', qc, dec, kc)
        o = (jnp.einsum('zbhts,zbshe->zbthe', A, vc)
             + jnp.einsum('zbthc,zbhce->zbthe', qc * jnp.exp(b).astype(qc.dtype), S))
        b_last = b[:, :, -1:]
        S = (S * jnp.exp(b_last[:, :, 0])[..., None].astype(S.dtype)
             + jnp.einsum('zbshc,zbshe->zbhce', kc * jnp.exp(b_last - b).astype(kc.dtype), vc)).astype(S.dtype)
        return S, o
    S0 = jnp.zeros((nz, bsz, nh, dk, dv), v.dtype)
    _, o = lax.scan(step, S0, (blocks(q), blocks(k), blocks(v), blocks(log_f)))
    return jnp.moveaxis(o, 0, 2).reshape(nz, bsz, n_tok, nh, dv)


def hgrn2_mixer(z, n_ctx, lo, lower, norm_g):
    q, f, i, g = split_cols(z, HGRN_COLS)
    bsz, n_tok, _ = z.shape
    f = jnp.moveaxis(f.reshape(bsz, n_tok, 2, MIX_W), 2, 0).astype(jnp.float32)
    lb = lower[:, None, None, :].astype(jnp.float32)
    log_lb = jnp.log(jnp.maximum(lb, HGRN_LB_FLOOR))
    log_f = jnp.logaddexp(jax.nn.log_sigmoid(f), log_lb + jax.nn.log_sigmoid(-f))
    k = ((1.0 - lb) * jax.nn.sigmoid(-f)).astype(z.dtype)
    y = gla_chunkwise(orient(both_dirs(heads(jax.nn.silu(q))), n_ctx), orient(heads(k), n_ctx),
                      orient(both_dirs(heads(i)), n_ctx), orient(heads(log_f), n_ctx))
    y = deorient_sum(y, n_ctx)[:, lo:]
    return merge_heads(rms_norm(y, heads(norm_g))) * jax.nn.silu(g[:, lo:])


def gqa_mixer(z, n_ctx, need_ctx, cos, sin, q_g, k_g):
    q, k, v = split_cols(z, ATTN_COLS)
    q = rope_latent(rms_norm(heads(q), q_g), n_ctx, cos, sin) * HEAD_DIM ** -0.5
    k = rope_latent(rms_norm(heads(k), k_g), n_ctx, cos, sin)
    v = heads(v)
    bsz, n_tok = z.shape[0], z.shape[1]
    n_lat = n_tok - n_ctx
    group = MIX_HEADS // ATTN_KV_HEADS
    q = q.reshape(bsz, n_tok, ATTN_KV_HEADS, group, HEAD_DIM)

    def attend(qb, kb, vb):
        s = jnp.einsum('bqhgd,bkhd->bhgqk', qb, kb).astype(jnp.float32)
        p = jax.nn.softmax(s, axis=-1).astype(vb.dtype)
        return jnp.einsum('bhgqk,bkhd->bqhgd', p, vb)
    q_blocks = jnp.moveaxis(q[:, n_ctx:].reshape(bsz, n_lat // ATTN_BLOCK, ATTN_BLOCK, ATTN_KV_HEADS, group, HEAD_DIM), 1, 0)
    o_lat = lax.map(lambda qb: attend(qb, k, v), q_blocks)
    o_lat = jnp.moveaxis(o_lat, 0, 1).reshape(bsz, n_lat, MIX_W)
    if not need_ctx:
        return o_lat
    o_ctx = attend(q[:, :n_ctx], k[:, :n_ctx], v[:, :n_ctx]).reshape(bsz, n_ctx, MIX_W)
    return jnp.concatenate([o_ctx, o_lat], axis=1)


def expert_choice_ffn(u, w_router, w_e1, w_e3, w_e2):
    bsz, n_tok, _ = u.shape
    cap = EC_CAPACITY * n_tok // N_EXPERTS
    aff = jax.nn.softmax((u @ w_router).astype(jnp.float32), axis=-1)
    gate, idx = lax.top_k(jnp.swapaxes(aff, 1, 2), cap)
    bidx = jnp.arange(bsz)[:, None, None]
    xs = u[bidx, idx]
    h = jax.nn.silu(jnp.einsum('becd,edf->becf', xs, w_e1)) * jnp.einsum('becd,edf->becf', xs, w_e3)
    y = jnp.einsum('becf,efd->becd', h, w_e2) * gate[..., None].astype(u.dtype)
    return jnp.zeros_like(u).at[bidx, idx].add(y)


def trunk_layer(x_ctx, x_lat, c, c_ctx, cos, sin, hgrn_lower, need_ctx, p):
    n_ctx = x_ctx.shape[1]
    lo = 0 if need_ctx else n_ctx
    mod_lat = jax.nn.silu(c) @ p['w_mod'] + p['b_mod']
    mod_ctx = jax.nn.silu(c_ctx) @ p['w_mod'] + p['b_mod']
    sh1, sc1, g1, sh2, sc2, g2 = jnp.split(mod_lat[:, None, :], 6, axis=-1)
    sh1c, sc1c, g1c, sh2c, sc2c, g2c = jnp.split(mod_ctx, 6, axis=-1)
    u = jnp.concatenate([x_ctx * (1 + sc1c) + sh1c, x_lat * (1 + sc1) + sh1], axis=1)
    z = u @ p['w_in']
    z_a, z_b, z_c, z_d = split_cols(z, GROUP_COLS)
    branches = (
        rwkv7_mixer(z_a, n_ctx, lo, p['rwkv_mu'], p['rwkv_w0'], p['rwkv_w2'], p['rwkv_a0'], p['rwkv_a2'],
                    p['rwkv_g2'], p['rwkv_kk'], p['rwkv_ka'], p['rwkv_rk'], p['rwkv_ln_g'], p['rwkv_ln_b']),
        retention_mixer(z_b, n_ctx, lo, cos, sin, p['ret_decay'], p['ret_norm_g'], p['ret_norm_b']),
        hgrn2_mixer(z_c, n_ctx, lo, hgrn_lower, p['hgrn_norm_g']),
        gqa_mixer(z_d, n_ctx, need_ctx, cos, sin, p['attn_q_g'], p['attn_k_g']),
    )
    u_rows = u[:, lo:]
    merged = None
    for i in range(N_BRANCH):
        term = jax.nn.sigmoid(u_rows @ p['w_gate'][i]) * (branches[i] @ p['w_branch'][i])
        merged = term if merged is None else merged + term
    mix = merged @ p['w_out']
    n0 = n_ctx - lo
    x_lat = layer_norm(DEEPNORM_ALPHA * x_lat + g1 * mix[:, n0:], p['ln1_g'], p['ln1_b'])
    ffn_lat = expert_choice_ffn(x_lat * (1 + sc2) + sh2, p['w_router'], p['w_e1'], p['w_e3'], p['w_e2'])
    x_lat = layer_norm(DEEPNORM_ALPHA * x_lat + g2 * ffn_lat, p['ln2_g'], p['ln2_b'])
    if need_ctx:
        x_ctx = layer_norm(DEEPNORM_ALPHA * x_ctx + g1c * mix[:, :n_ctx], p['ln1_g'], p['ln1_b'])
        ffn_ctx = expert_choice_ffn(x_ctx * (1 + sc2c) + sh2c, p['w_router'], p['w_e1'], p['w_e3'], p['w_e2'])
        x_ctx = layer_norm(DEEPNORM_ALPHA * x_ctx + g2c * ffn_ctx, p['ln2_g'], p['ln2_b'])
    return x_ctx, x_lat


def setup_inputs(seed: int = 0) -> dict:
    key = jax.random.key(seed)
    ks = iter(jax.random.split(key, 40))

    def nrm(shape, scale):
        return scale * jax.random.normal(next(ks), shape, jnp.float32)

    def unif(shape, lo, hi):
        return jax.random.uniform(next(ks), shape, jnp.float32, lo, hi)
    L, D = DEPTH, D_MODEL
    ret_base = jnp.log(2.0 ** (5.0 + jnp.arange(MIX_HEADS, dtype=jnp.float32)) - 1.0)
    return {
        'x': nrm((BATCH, SEQ, D), 1.0),
        'c': nrm((BATCH, D), 1.0),
        'ctx': nrm((BATCH, CTX_LEN, D), 1.0),
        'c_ctx': nrm((D,), 1.0),
        'w_mod': nrm((L, D, 6 * D), 0.5 * D ** -0.5),
        'b_mod': nrm((L, 6 * D), 0.02),
        'w_in': nrm((L, D, D_IN), D ** -0.5),
        'rwkv_mu': unif((L, GROUP_COLS[0]), 0.0, 1.0),
        'rwkv_w0': unif((L, 2, MIX_W), -5.0, 1.0),
        'rwkv_w2': nrm((L, 2, RWKV_LORA_W, MIX_W), 0.1 * RWKV_LORA_W ** -0.5),
        'rwkv_a0': nrm((L, 2, MIX_W), 0.5),
        'rwkv_a2': nrm((L, 2, RWKV_LORA_A, MIX_W), 0.5 * RWKV_LORA_A ** -0.5),
        'rwkv_g2': nrm((L, RWKV_LORA_G, MIX_W), RWKV_LORA_G ** -0.5),
        'rwkv_kk': 0.85 + nrm((L, MIX_W), 0.05),
        'rwkv_ka': 1.0 + nrm((L, MIX_W), 0.05),
        'rwkv_rk': nrm((L, MIX_W), 0.1),
        'rwkv_ln_g': 1.0 + nrm((L, MIX_W), 0.05),
        'rwkv_ln_b': nrm((L, MIX_W), 0.02),
        'ret_decay': ret_base + nrm((L, 2, MIX_HEADS), 0.1),
        'ret_norm_g': 1.0 + nrm((L, MIX_W), 0.05),
        'ret_norm_b': nrm((L, MIX_W), 0.02),
        'hgrn_lb': 1.0 + nrm((2, L, MIX_W), 0.1),
        'hgrn_norm_g': 1.0 + nrm((L, MIX_W), 0.05),
        'attn_q_g': 1.0 + nrm((L, HEAD_DIM), 0.05),
        'attn_k_g': 1.0 + nrm((L, HEAD_DIM), 0.05),
        'w_gate': nrm((L, N_BRANCH, D, D), D ** -0.5),
        'w_branch': nrm((L, N_BRANCH, MIX_W, D), MIX_W ** -0.5),
        'w_out': nrm((L, D, D), DEEPNORM_BETA * D ** -0.5),
        'ln1_g': 1.0 + nrm((L, D), 0.05),
        'ln1_b': nrm((L, D), 0.02),
        'w_router': nrm((L, D, N_EXPERTS), D ** -0.5),
        'w_e1': nrm((L, N_EXPERTS, D, D_EXPERT), D ** -0.5),
        'w_e3': nrm((L, N_EXPERTS, D, D_EXPERT), D ** -0.5),
        'w_e2': nrm((L, N_EXPERTS, D_EXPERT, D), DEEPNORM_BETA * D_EXPERT ** -0.5),
        'ln2_g': 1.0 + nrm((L, D), 0.05),
        'ln2_b': nrm((L, D), 0.02),
    }


def reference(x, c, ctx, c_ctx, w_mod, b_mod, w_in, rwkv_mu, rwkv_w0, rwkv_w2, rwkv_a0, rwkv_a2, rwkv_g2,
              rwkv_kk, rwkv_ka, rwkv_rk, rwkv_ln_g, rwkv_ln_b, ret_decay, ret_norm_g, ret_norm_b, hgrn_lb,
              hgrn_norm_g, attn_q_g, attn_k_g, w_gate, w_branch, w_out, ln1_g, ln1_b, w_router, w_e1, w_e3,
              w_e2, ln2_g, ln2_b):
    cos, sin = axial_rope_tables(x.shape[1], x.dtype)
    lb_w = jax.nn.softmax(hgrn_lb.astype(jnp.float32), axis=1)
    lower = jnp.cumsum(lb_w, axis=1) - lb_w[:, :1]
    x_ctx, x_lat = ctx, x
    for l in range(DEPTH):
        p = {
            'w_mod': w_mod[l], 'b_mod': b_mod[l], 'w_in': w_in[l],
            'rwkv_mu': rwkv_mu[l], 'rwkv_w0': rwkv_w0[l], 'rwkv_w2': rwkv_w2[l], 'rwkv_a0': rwkv_a0[l],
            'rwkv_a2': rwkv_a2[l], 'rwkv_g2': rwkv_g2[l], 'rwkv_kk': rwkv_kk[l], 'rwkv_ka': rwkv_ka[l],
            'rwkv_rk': rwkv_rk[l], 'rwkv_ln_g': rwkv_ln_g[l], 'rwkv_ln_b': rwkv_ln_b[l],
            'ret_decay': ret_decay[l], 'ret_norm_g': ret_norm_g[l], 'ret_norm_b': ret_norm_b[l],
            'hgrn_norm_g': hgrn_norm_g[l], 'attn_q_g': attn_q_g[l], 'attn_k_g': attn_k_g[l],
            'w_gate': w_gate[l], 'w_branch': w_branch[l], 'w_out': w_out[l], 'ln1_g': ln1_g[l], 'ln1_b': ln1_b[l],
            'w_router': w_router[l], 'w_e1': w_e1[l], 'w_e3': w_e3[l], 'w_e2': w_e2[l],
            'ln2_g': ln2_g[l], 'ln2_b': ln2_b[l],
        }
        x_ctx, x_lat = trunk_layer(x_ctx, x_lat, c, c_ctx, cos, sin, lower[:, l], l < DEPTH - 1, p)
    return x_lat
```

```python
import numpy as np
from contextlib import ExitStack
import concourse.bass as bass
import concourse.mybir as mybir
from concourse.bass_utils import run_bass_kernel_spmd

F32, BF16 = mybir.dt.float32, mybir.dt.bfloat16
AF = mybir.ActivationFunctionType
ALU = mybir.AluOpType
AX = mybir.AxisListType

NB = 2
NCTX = 256
NLAT = 2048
T = NCTX + NLAT
D = 1024
L = 2
DIN = 3968
NZC = DIN // 128
ALPHA = float((2 * L) ** 0.25)


class Buf:
    def __init__(self, h, dram=False):
        self.h = h
        self.dram = dram
        self.w = {}
        self.r = {}
        self._ap = h.ap() if dram else None

    def __getitem__(self, k):
        return self._ap[k] if self.dram else self.h[k]


class KB:
    def __init__(self, nc, es):
        self.nc, self.es = nc, es
        self.eng = {'pe': nc.tensor, 'act': nc.scalar, 'dve': nc.vector, 'pool': nc.gpsimd, 'sp': nc.sync}
        self.sem = {e: es.enter_context(nc.semaphore('s_' + e)) for e in ('pe', 'act', 'dve', 'pool')}
        self.cnt = {e: 0 for e in self.sem}
        self.NDS = 16
        for i in range(self.NDS):
            self.sem[i] = es.enter_context(nc.semaphore('d%d' % i))
        self.dcount = 0
        self.known = {e: {} for e in self.eng}
        self.nuniq = 0

    def sb(self, shape, dt=F32, name=None):
        self.nuniq += 1
        return Buf(self.es.enter_context(self.nc.sbuf_tensor(name or 'sb%d' % self.nuniq, list(shape), dt)))

    def ps(self, shape, dt=F32, name=None):
        self.nuniq += 1
        return Buf(self.es.enter_context(self.nc.psum_tensor(name or 'ps%d' % self.nuniq, list(shape), dt)))

    def dram(self, name, shape, dt=F32, kind='Internal'):
        return Buf(self.nc.dram_tensor(name, list(shape), dt, kind=kind), dram=True)

    def _need(self, reads, writes):
        need = {}
        for b in reads:
            for s, v in b.w.items():
                if need.get(s, 0) < v:
                    need[s] = v
        for b in writes:
            for s, v in list(b.w.items()) + list(b.r.items()):
                if need.get(s, 0) < v:
                    need[s] = v
        return need

    def _emit_waits(self, e, need):
        kn = self.known[e]
        eng = self.eng[e]
        for s, v in need.items():
            if kn.get(s, 0) >= v:
                continue
            if e == 'pe' and s == 'pe':
                continue
            eng.wait_ge(self.sem[s], v)
            kn[s] = v

    def _done(self, s, v, reads, writes):
        for b in reads:
            if b.r.get(s, 0) < v:
                b.r[s] = v
        for b in writes:
            b.w = {s: v}
            b.r = {}

    def op(self, e, fn, reads=(), writes=()):
        self._emit_waits(e, self._need(reads, writes))
        inst = fn(self.eng[e])
        self.cnt[e] += 1
        inst.then_inc(self.sem[e], 1)
        self._done(e, self.cnt[e], reads, writes)

    def dma(self, out_ap, in_ap, reads=(), writes=(), q='sp', **kw):
        i = self.dcount
        self.dcount += 1
        s = i % self.NDS
        v = 16 * (i // self.NDS + 1)
        need = self._need(reads, writes)
        if v > 16 and need.get(s, 0) < v - 16:
            need[s] = v - 16
        self._emit_waits(q, need)
        self.eng[q].dma_start(out=out_ap, in_=in_ap, **kw).then_inc(self.sem[s], 16)
        self._done(s, v, reads, writes)

    def barrier(self):
        allev = {e: self.cnt[e] for e in ('pe', 'act', 'dve', 'pool') if self.cnt[e] > 0}
        for i in range(min(self.dcount, self.NDS)):
            last = self.dcount - 1 - ((self.dcount - 1 - i) % self.NDS)
            allev[i] = 16 * (last // self.NDS + 1)
        for e in self.eng:
            need = {s: v for s, v in allev.items() if s != e}
            self._emit_waits(e, need)

    def final_wait(self, bufs):
        need = self._need(bufs, ())
        self._emit_waits('sp', need)

    def mm(self, out, lhsT, rhs, start, stop, reads, writes):
        self.op('pe', lambda e: e.matmul(out, lhsT=lhsT, rhs=rhs, start=start, stop=stop), reads, writes)

    def act(self, out, in_, func, reads, writes, bias=None, scale=None, accum_out=None, eng='act'):
        kw = {}
        if bias is not None:
            kw['bias'] = bias
        if scale is not None:
            kw['scale'] = scale
        if accum_out is not None:
            kw['accum_out'] = accum_out
        self.op('act', lambda e: e.activation(out=out, in_=in_, func=func, **kw), reads, writes)

    def tt(self, out, in0, in1, op, reads, writes, eng='dve'):
        self.op(eng, lambda e: e.tensor_tensor(out=out, in0=in0, in1=in1, op=op), reads, writes)

    def ts(self, out, in0, s1, op0, reads, writes, s2=None, op1=None, eng='dve'):
        if op1 is None:
            self.op(eng, lambda e: e.tensor_scalar(out=out, in0=in0, scalar1=s1, scalar2=None, op0=op0), reads, writes)
        else:
            self.op(eng, lambda e: e.tensor_scalar(out=out, in0=in0, scalar1=s1, scalar2=s2, op0=op0, op1=op1),
                    reads, writes)

    def cp(self, out, in_, reads, writes, eng='dve'):
        if eng == 'act':
            self.op('act', lambda e: e.copy(out=out, in_=in_), reads, writes)
        else:
            self.op(eng, lambda e: e.tensor_copy(out=out, in_=in_), reads, writes)


VC = {}
_o = 0
for _n, _c in [('mu', 9), ('w0', 4), ('a0', 4), ('kk', 2), ('ka', 2), ('rk', 2), ('lng', 2), ('lnb', 2), ('retdec', 4),
               ('retg', 2), ('retb', 2), ('hlb', 8), ('hng', 2), ('aqg', 1), ('akg', 1), ('ln1g', 8), ('ln1b', 8),
               ('ln2g', 8), ('ln2b', 8)]:
    VC[_n] = _o
    _o += _c
NVC = _o
C_BO, C_DM, C_ROT, C_ID, C_ONE, NCST = 0, 128, 192, 320, 448, 576
TB = 256
NSTEP_BLK = T // TB


def sap(buf, off, dims):
    h = buf.h
    full = h[:]
    pstride = full.ap[0][0]
    return bass.AP(h, off, [[pstride, 128]] + [list(d) for d in dims])


def mixers_phase(k, l, zT, IN, CST, nps, stop_after, dbg_out):
    vecs = CST['vecs']
    cst = CST['cst']
    cstb = CST['cstb']
    rope = CST['rope']
    SI = CST['SI']
    YC = CST['YC']
    BR = CST['BR']
    AUX = CST['AUX']

    def V(name, j=0):
        return vecs[:, l, VC[name] + j:VC[name] + j + 1]
    BO = cst[:, C_BO:C_BO + 128]
    BOb = cstb[:, C_BO:C_BO + 128]
    ph = ExitStack()
    with ph:
        k.es = ph
        der = k.sb([128, 40], F32, name='der%d' % l)
        k.ts(der[:, 0:9], vecs[:, l, VC['mu']:VC['mu'] + 9], 0.5, ALU.mult, [vecs], [der])
        k.ts(der[:, 9:18], vecs[:, l, VC['mu']:VC['mu'] + 9], -1.0, ALU.mult, [vecs], [der], s2=1.0, op1=ALU.add)
        k.ts(der[:, 18:20], vecs[:, l, VC['ka']:VC['ka'] + 2], -1.0, ALU.mult, [vecs], [der], s2=1.0, op1=ALU.add)
        k.act(der[:, 20:24], vecs[:, l, VC['retdec']:VC['retdec'] + 4], AF.Sigmoid, [vecs], [der])
        if l == 0:
            k.op('dve', lambda e: e.memset(der[:, 24:28], 0.0), [], [der])
        else:
            hl = vecs[:, l, VC['hlb']:VC['hlb'] + 8].rearrange("p (d l h) -> p d l h", d=2, l=2)
            k.tt(der[:, 24:28].rearrange("p (d h) -> p d h", d=2), hl[:, :, 1, :], hl[:, :, 0, :], ALU.subtract,
                 [vecs], [der])
            k.act(der[:, 24:28], der[:, 24:28], AF.Sigmoid, [der], [der])
        k.ts(der[:, 28:32], der[:, 24:28], -1.0, ALU.mult, [der], [der], s2=1.0, op1=ALU.add)
        W2 = k.sb([128, 256], F32, name='W2_%d' % l)
        A2 = k.sb([128, 256], F32, name='A2_%d' % l)
        G2 = k.sb([128, 256], F32, name='G2_%d' % l)
        k.dma(W2[:], IN['rwkv_w2'][l].rearrange("z r c -> (z r) c"), [IN['rwkv_w2']], [W2])
        k.dma(A2[:], IN['rwkv_a2'][l].rearrange("z r c -> (z r) c"), [IN['rwkv_a2']], [A2])
        k.dma(G2[:], IN['rwkv_g2'][l], [IN['rwkv_g2']], [G2])

        NT = 6
        tmps = [k.sb([128, 512], F32, name='ptmp%d_%d' % (l, i)) for i in range(NT)]
        ti = [0]

        def tmp():
            ti[0] = (ti[0] + 1) % NT
            return tmps[ti[0]]
        outs = [k.sb([128, 512], F32, name='pout%d_%d' % (l, i)) for i in range(6)]
        oi = [0]

        def store(dst_ap, dstbuf, fn):
            oi[0] = (oi[0] + 1) % 6
            o = outs[oi[0]]
            fn(o)
            return o

        zs = k.sb([128, 9, 514], F32, name='zs%d' % l)
        twt = k.sb([128, 512], F32, name='twt%d' % l)
        sgt = k.sb([128, 512], F32, name='sgt%d' % l)
        zz = k.sb([128, 9, 512], F32, name='zz%d' % l)
        zt2 = k.sb([128, 9, 512], F32, name='zt2%d' % l)
        zl = [k.sb([128, 512], F32, name='zl%d_%d' % (l, i)) for i in range(4)]
        zli = [0]

        def loadz(b, c, t0, n):
            zli[0] = (zli[0] + 1) % 4
            t_ = zl[zli[0]]
            k.dma(t_[:, :n], zT[b][c][:, t0:t0 + n], [zT], [t_])
            return t_

        def put(dst, name, m, d_, b, hp, t0, n, src):
            k.dma(dst[m][d_][b][hp][:, t0:t0 + n], src[:, :n], [src], [dst])

        for b in range(NB):
            for (t0, n, isctx) in tok_blocks():
                seg0, seg1 = (0, NCTX) if isctx else (NCTX, T)
                k.op('pool', lambda e: e.memset(zs[:, :, 0:1], 0.0), [], [zs])
                k.op('pool', lambda e: e.memset(zs[:, :, n + 1:n + 2], 0.0), [], [zs])
                a0_ = max(t0 - 1, seg0)
                a1_ = min(t0 + n + 1, seg1)
                k.dma(zs[:, :, 1 + (a0_ - t0):1 + (a1_ - t0)],
                      zT[b][0:9][:, :, a0_:a1_].rearrange("c p t -> p c t"), [zT], [zs])
                k.tt(zt2[:, :, :n], zs[:, :, 0:n], zs[:, :, 2:n + 2], ALU.add, [zs], [zt2])
                for c in range(9):
                    k.ts(zt2[:, c, :n], zt2[:, c, :n], der[:, c:c + 1], ALU.mult, [zt2, der], [zt2])
                    k.op('dve', lambda e, c=c: e.scalar_tensor_tensor(out=zz[:, c, :n], in0=zs[:, c, 1:n + 1],
                                                                      scalar=der[:, 9 + c:10 + c], in1=zt2[:, c, :n],
                                                                      op0=ALU.mult, op1=ALU.add), [zs, zt2, der], [zz])
                for hp in range(2):
                    k.dma(SI['q'][0][0][b][hp][:, t0:t0 + n], zz[:, hp, :n], [zz], [SI['q']])
                    k.dma(SI['v'][0][0][b][hp][:, t0:t0 + n], zz[:, 4 + hp, :n], [zz], [SI['v']])
                tw = twt
                k.act(tw[:, :n], zz[:, 6, :n], AF.Tanh, [zz], [tw])
                sg = sgt
                k.act(sg[:, :n], zz[:, 8, :n], AF.Sigmoid, [zz], [sg])
                akeep = {}
                for hp in range(2):
                    kkf = tmp()
                    k.ts(kkf[:, :n], zz[:, 2 + hp, :n], V('kk', hp), ALU.mult, [zz, vecs], [kkf])
                    sq = tmp()
                    k.tt(sq[:, :n], kkf[:, :n], kkf[:, :n], ALU.mult, [kkf], [sq])
                    p = nps()
                    k.mm(p[:, :n], BO, sq[:, :n], True, True, [cst, sq], [p])
                    rs = sq
                    k.ts(rs[:, :n], p[:, :n], 1e-12, ALU.add, [p], [rs])
                    k.act(rs[:, :n], rs[:, :n], AF.Ln, [rs], [rs])
                    k.act(rs[:, :n], rs[:, :n], AF.Exp, [rs], [rs], scale=-0.5)
                    kkn = k.sb([128, 512], F32, name='kkn%d_%d_%d_%d' % (l, b, t0, hp))
                    k.tt(kkn[:, :n], kkf[:, :n], rs[:, :n], ALU.mult, [kkf, rs], [kkn])
                    k.dma(SI['kk'][0][0][b][hp][:, t0:t0 + n], kkn[:, :n], [kkn], [SI['kk']])
                    rk_ = tmp()
                    k.op('dve', lambda e, hp=hp, rk_=rk_: e.scalar_tensor_tensor(
                        out=rk_[:, :n], in0=zz[:, hp, :n], scalar=V('rk', hp), in1=zz[:, 2 + hp, :n],
                        op0=ALU.mult, op1=ALU.mult), [zz, vecs], [rk_])
                    p = nps()
                    k.mm(p[:, :n], BO, rk_[:, :n], True, True, [cst, rk_], [p])
                    bon = tmp()
                    k.tt(bon[:, :n], p[:, :n], zz[:, 4 + hp, :n], ALU.mult, [p, zz], [bon])
                    k.dma(AUX[0][b][hp][:, t0:t0 + n], bon[:, :n], [bon], [AUX])
                    p = nps()
                    k.mm(p[:, :n], G2[:, hp * 128:(hp + 1) * 128], sg[:, :n], True, True, [G2, sg], [p])
                    gg = tmp()
                    k.cp(gg[:, :n], p[:, :n], [p], [gg], eng='act')
                    k.dma(AUX[1][b][hp][:, t0:t0 + n], gg[:, :n], [gg], [AUX])
                    for d_ in range(2):
                        ps_ = slice(64 * d_, 64 * d_ + 64)
                        p = nps()
                        k.mm(p[:, :n], W2[ps_, hp * 128:(hp + 1) * 128], tw[ps_, :n], True, True, [W2, tw], [p])
                        dec = tmp()
                        k.act(dec[:, :n], p[:, :n], AF.Sigmoid, [p, vecs], [dec], bias=V('w0', d_ * 2 + hp))
                        k.act(dec[:, :n], dec[:, :n], AF.Exp, [dec], [dec], scale=-float(np.exp(-0.5)))
                        k.dma(SI['w'][0][d_][b][hp][:, t0:t0 + n], dec[:, :n], [dec], [SI['w']])
                        p = nps()
                        k.mm(p[:, :n], A2[ps_, hp * 128:(hp + 1) * 128], zz[ps_, 7, :n], True, True, [A2, zz], [p])
                        a_ = tmp()
                        k.act(a_[:, :n], p[:, :n], AF.Sigmoid, [p, vecs], [a_], bias=V('a0', d_ * 2 + hp))
                        bb = tmp()
                        k.tt(bb[:, :n], kkn[:, :n], a_[:, :n], ALU.mult, [kkn, a_], [bb])
                        k.dma(SI['b'][0][d_][b][hp][:, t0:t0 + n], bb[:, :n], [bb], [SI['b']])
                        kd = tmp()
                        k.ts(kd[:, :n], a_[:, :n], V('ka', hp), ALU.mult, [a_, vecs, der], [kd],
                             s2=der[:, 18 + hp:19 + hp], op1=ALU.add)
                        k.tt(kd[:, :n], kd[:, :n], zz[:, 2 + hp, :n], ALU.mult, [kd, zz], [kd])
                        k.dma(SI['k'][0][d_][b][hp][:, t0:t0 + n], kd[:, :n], [kd], [SI['k']])
                for hp in range(2):
                    for which, cbase, scl in (('q', 9, 1.0), ('k', 11, 0.125)):
                        zt_ = loadz(b, cbase + hp, t0, n)
                        o = tmp()
                        if isctx:
                            k.ts(o[:, :n], zt_[:, :n], scl, ALU.mult, [zt_], [o])
                        else:
                            tl = t0 - NCTX
                            p = nps()
                            k.mm(p[:, :n], cst[:, C_ROT:C_ROT + 128], zt_[:, :n], True, True, [cst, zt_], [p])
                            o2 = tmp()
                            k.tt(o2[:, :n], p[:, :n], rope[:, 1, tl:tl + n], ALU.mult, [p, rope], [o2])
                            k.tt(o[:, :n], zt_[:, :n], rope[:, 0, tl:tl + n], ALU.mult, [zt_, rope], [o])
                            k.tt(o[:, :n], o[:, :n], o2[:, :n], ALU.add, [o, o2], [o])
                            if scl != 1.0:
                                k.ts(o[:, :n], o[:, :n], scl, ALU.mult, [o], [o])
                        k.dma(SI[which][1][0][b][hp][:, t0:t0 + n], o[:, :n], [o], [SI[which]])
                for hp in range(2):
                    zt_ = loadz(b, 17 + hp, t0, n)
                    o = tmp()
                    k.act(o[:, :n], zt_[:, :n], AF.Silu, [zt_], [o])
                    k.dma(SI['q'][2][0][b][hp][:, t0:t0 + n], o[:, :n], [o], [SI['q']])
                    for d_ in range(2):
                        zt_ = loadz(b, 19 + 2 * d_ + hp, t0, n)
                        sn = tmp()
                        k.act(sn[:, :n], zt_[:, :n], AF.Sigmoid, [zt_], [sn], scale=-1.0)
                        kk_ = tmp()
                        k.ts(kk_[:, :n], sn[:, :n], der[:, 28 + d_ * 2 + hp:29 + d_ * 2 + hp], ALU.mult, [sn, der], [kk_])
                        k.dma(SI['k'][2][d_][b][hp][:, t0:t0 + n], kk_[:, :n], [kk_], [SI['k']])
                        w_ = tmp()
                        k.ts(w_[:, :n], kk_[:, :n], -1.0, ALU.mult, [kk_], [w_], s2=1.0, op1=ALU.add)
                        k.dma(SI['w'][2][d_][b][hp][:, t0:t0 + n], w_[:, :n], [w_], [SI['w']])
        k.barrier()
    for m in range(3):
        scan_mixer(k, l, m, zT, CST, nps)
    post_mixers(k, l, zT, CST, nps)
    if stop_after == 'C':
        k.es = ExitStack()
        t_ = None


def scan_mixer(k, l, m, zT, CST, nps):
    cst, cstb, vecs = CST['cst'], CST['cstb'], CST['vecs']
    SI, YC = CST['SI'], CST['YC']
    delta = (m == 0)
    BOb = cstb[:, C_BO:C_BO + 128]
    ph = ExitStack()
    with ph:
        k.es = ph
        names = ['q', 'k', 'v'] + (['w'] if m != 1 else []) + (['kk', 'b'] if delta else [])
        A = {nm: k.sb([128, 2, 4, TB], F32, name='A%s_%d_%d' % (nm, l, m)) for nm in names}
        Wc = k.sb([128, 2, 4], F32, name='Wc_%d_%d' % (l, m))
        if m == 1:
            for b in range(NB):
                k.act(Wc[:, :, 2 * b:2 * b + 2],
                      vecs[:, l, VC['retdec']:VC['retdec'] + 4].rearrange("p (d h) -> p d h", d=2),
                      AF.Sigmoid, [vecs], [Wc])
        S = k.sb([128, 512], F32, name='S_%d_%d' % (l, m))
        k.op('pool', lambda e: e.memset(S[:], 0.0), [], [S])
        Ych = k.sb([128, 2, 4, TB], F32, name='Ych_%d_%d' % (l, m))
        NR = 3
        R = [dict(vd=k.sb([128, 512], BF16), vbs=k.sb([128, 512], F32), kv=k.sb([128, 512], F32),
                  t1=k.sb([128, 512], BF16), t2=k.sb([128, 512], F32), t3=k.sb([128, 512], BF16),
                  t5=k.sb([128, 512], F32)) for _ in range(NR)]

        def v4(ap):
            return ap.rearrange("p (d c v) -> p d c v", d=2, c=4)

        def src(nm, d_, b, st):
            if m == 0:
                dd = d_ if nm in ('k', 'w', 'b') else 0
                return SI[nm], SI[nm][0][dd][b][:, :, st:st + TB]
            if m == 1:
                if nm == 'v':
                    return zT, zT[b][13:15][:, :, st:st + TB]
                return SI[nm], SI[nm][1][0][b][:, :, st:st + TB]
            if nm == 'v':
                return zT, zT[b][23:25][:, :, st:st + TB]
            dd = d_ if nm in ('k', 'w') else 0
            return SI[nm], SI[nm][2][dd][b][:, :, st:st + TB]
        S4 = v4(S[:])
        Dm4 = sap(cst, C_DM, [[0, 2], [0, 4], [1, 64]])
        Dm3 = sap(cst, C_DM, [[0, 8], [1, 64]])
        for j in range(NSTEP_BLK):
            st = [TB * j, 0 if j == 0 else T - TB * j]
            for nm in names:
                for d_ in range(2):
                    for b in range(NB):
                        sb_, sa_ = src(nm, d_, b, st[d_])
                        k.dma(A[nm][:, d_, 2 * b:2 * b + 2, :], sa_.rearrange("h p t -> p h t"), [sb_], [A[nm]])
            for s_ in range(TB):
                c0, c1 = s_, TB - 1 - s_
                r = R[s_ % NR]

                def step(nm):
                    if nm == 'w' and m == 1:
                        return sap(Wc, 0, [[4, 2], [1, 4], [0, 64]]), Wc
                    return sap(A[nm], c0, [[4 * TB + c1 - c0, 2], [TB, 4], [0, 64]]), A[nm]
                a_, ab = step('v')
                k.tt(v4(r['vd'][:]), a_, Dm4, ALU.mult, [ab, cst], [r['vd']], eng='pool')
                pA = nps()
                k.mm(pA[:], BOb, r['vd'][:], True, True, [cstb, r['vd']], [pA])
                k.cp(r['vbs'][:], pA[:], [pA], [r['vbs']], eng='act')
                a_, ab = step('k')
                k.tt(v4(r['kv'][:]), v4(r['vbs'][:]), a_, ALU.mult, [r['vbs'], ab], [r['kv']], eng='pool')
                if delta:
                    a_, ab = step('kk')
                    k.tt(v4(r['t1'][:]), S4, a_, ALU.mult, [S, ab], [r['t1']])
                    pB = nps()
                    k.mm(pB[:], BOb, r['t1'][:], True, True, [cstb, r['t1']], [pB])
                a_, ab = step('w')
                k.tt(S4, S4, a_, ALU.mult, [S, ab], [S])
                if delta:
                    a_, ab = step('b')
                    k.tt(v4(r['t2'][:]), v4(pB[:]), a_, ALU.mult, [pB, ab], [r['t2']])
                    k.tt(S[:], S[:], r['t2'][:], ALU.subtract, [S, r['t2']], [S])
                k.tt(S[:], S[:], r['kv'][:], ALU.add, [S, r['kv']], [S])
                a_, ab = step('q')
                k.tt(v4(r['t3'][:]), S4, a_, ALU.mult, [S, ab], [r['t3']])
                pC = nps()
                k.mm(pC[:], BOb, r['t3'][:], True, True, [cstb, r['t3']], [pC])
                t5v = r['t5'][:].rearrange("p (c v) -> p c v", c=8)
                k.tt(t5v, pC[:].rearrange("p (c v) -> p c v", c=8), Dm3, ALU.mult, [pC, cst], [r['t5']])
                yo = sap(Ych, c0, [[4 * TB + c1 - c0, 2], [TB, 4]])
                k.op('dve', lambda e, yo=yo, t5v=t5v: e.tensor_reduce(out=yo, in_=t5v, axis=AX.X, op=ALU.add),
                     [r['t5']], [Ych])
            for d_ in range(2):
                for b in range(NB):
                    k.dma(YC[m][d_][b][:, :, st[d_]:st[d_] + TB].rearrange("h p t -> p h t"),
                          Ych[:, d_, 2 * b:2 * b + 2, :], [Ych], [YC])
        k.barrier()


def post_mixers(k, l, zT, CST, nps):
    cst, vecs = CST['cst'], CST['vecs']
    YC, BR, AUX = CST['YC'], CST['BR'], CST['AUX']
    BO = cst[:, C_BO:C_BO + 128]
    ph = ExitStack()
    with ph:
        k.es = ph
        NT = 10
        tm = [k.sb([128, 512], F32, name='pm%d_%d' % (l, i)) for i in range(NT)]
        ti = [0]

        def tmp():
            ti[0] = (ti[0] + 1) % NT
            return tm[ti[0]]

        def ld(buf, ap, n):
            t_ = tmp()
            k.dma(t_[:, :n], ap, [buf], [t_])
            return t_
        blocks = tok_blocks() if l < L - 1 else tok_blocks()[1:]
        for m in range(3):
            for b in range(NB):
                for hp in range(2):
                    for (t0, n, isctx) in blocks:
                        y0 = ld(YC, YC[m][0][b][hp][:, t0:t0 + n], n)
                        y1 = ld(YC, YC[m][1][b][hp][:, t0:t0 + n], n)
                        y = tmp()
                        k.tt(y[:, :n], y0[:, :n], y1[:, :n], ALU.add, [y0, y1], [y])
                        if m < 2:
                            p = nps()
                            k.mm(p[:, :n], BO, y[:, :n], True, True, [cst, y], [p])
                            mean = tmp()
                            k.ts(mean[:, :n], p[:, :n], 1.0 / 64, ALU.mult, [p], [mean])
                            yc = tmp()
                            k.tt(yc[:, :n], y[:, :n], mean[:, :n], ALU.subtract, [y, mean], [yc])
                            eps = 64e-5 if m == 0 else 1e-5
                        else:
                            yc = y
                            eps = 1e-6
                        sq = tmp()
                        k.tt(sq[:, :n], yc[:, :n], yc[:, :n], ALU.mult, [yc], [sq])
                        p = nps()
                        k.mm(p[:, :n], BO, sq[:, :n], True, True, [cst, sq], [p])
                        rstd = sq
                        k.ts(rstd[:, :n], p[:, :n], 1.0 / 64, ALU.mult, [p], [rstd], s2=eps, op1=ALU.add)
                        k.act(rstd[:, :n], rstd[:, :n], AF.Ln, [rstd], [rstd])
                        k.act(rstd[:, :n], rstd[:, :n], AF.Exp, [rstd], [rstd], scale=-0.5)
                        o = tmp()
                        k.tt(o[:, :n], yc[:, :n], rstd[:, :n], ALU.mult, [yc, rstd], [o])
                        gn = ('lng', 'retg', 'hng')[m]
                        gcol = vecs[:, l, VC[gn] + hp:VC[gn] + hp + 1]
                        if m < 2:
                            bn = ('lnb', 'retb')[m]
                            bcol = vecs[:, l, VC[bn] + hp:VC[bn] + hp + 1]
                            k.act(o[:, :n], o[:, :n], AF.Identity, [o, vecs], [o], bias=bcol, scale=gcol)
                        else:
                            k.ts(o[:, :n], o[:, :n], gcol, ALU.mult, [o, vecs], [o])
                        if m == 0:
                            bon = ld(AUX, AUX[0][b][hp][:, t0:t0 + n], n)
                            gg = ld(AUX, AUX[1][b][hp][:, t0:t0 + n], n)
                            k.tt(o[:, :n], o[:, :n], bon[:, :n], ALU.add, [o, bon], [o])
                            k.tt(o[:, :n], o[:, :n], gg[:, :n], ALU.mult, [o, gg], [o])
                        else:
                            zc = 15 + hp if m == 1 else 25 + hp
                            g_ = ld(zT, zT[b][zc][:, t0:t0 + n], n)
                            k.act(g_[:, :n], g_[:, :n], AF.Silu, [g_], [g_])
                            k.tt(o[:, :n], o[:, :n], g_[:, :n], ALU.mult, [o, g_], [o])
                        k.dma(BR[m][b][hp][:, t0:t0 + n], o[:, :n], [o], [BR])
        k.barrier()


def attn_phase(k, l, zT, CST, nps):
    cst, cstb, vecs, rope = CST['cst'], CST['cstb'], CST['vecs'], CST['rope']
    BR, KA = CST['BR'], CST['KA']
    BO = cst[:, C_BO:C_BO + 128]
    IDf = cst[:, C_ID:C_ID + 128]
    IDb = cstb[:, C_ID:C_ID + 128]
    ph = ExitStack()
    with ph:
        k.es = ph
        NT = 8
        tm = [k.sb([128, 512], F32, name='at%d_%d' % (l, i)) for i in range(NT)]
        ti = [0]

        def tmp():
            ti[0] = (ti[0] + 1) % NT
            return tm[ti[0]]
        qb = k.sb([128, 2, T], BF16, name='qb%d' % l)
        kd32 = k.sb([128, 2, T], F32, name='kd32_%d' % l)
        kdb = k.sb([128, 2, T], BF16, name='kdb%d' % l)
        vpad = k.sb([128, 18, 2, 2, 128], BF16, name='vpad%d' % l)
        Sf = k.sb([128, T], F32, name='Sf%d' % l)
        Pb = k.sb([128, T], BF16, name='Pb%d' % l)
        PT = k.sb([128, 18, 128], BF16, name='PT%d' % l)
        st_ = k.sb([128, 4], F32, name='ast%d' % l)
        ob = [k.sb([128, 128], F32, name='aob%d_%d' % (l, i)) for i in range(2)]
        k.op('pool', lambda e: e.memset(vpad[:], 0.0), [], [vpad])
        for b in range(NB):
            for (t0, n, isctx) in tok_blocks():
                for which in range(3):
                    zc = 27 + which
                    zt_ = tmp()
                    k.dma(zt_[:, :n], zT[b][zc][:, t0:t0 + n], [zT], [zt_])
                    sq = tmp()
                    k.tt(sq[:, :n], zt_[:, :n], zt_[:, :n], ALU.mult, [zt_], [sq])
                    p = nps()
                    k.mm(p[:, :n], BO, sq[:, :n], True, True, [cst, sq], [p])
                    k.ts(sq[:, :n], p[:, :n], 1.0 / 64, ALU.mult, [p], [sq], s2=1e-6, op1=ALU.add)
                    k.act(sq[:, :n], sq[:, :n], AF.Ln, [sq], [sq])
                    k.act(sq[:, :n], sq[:, :n], AF.Exp, [sq], [sq], scale=-0.5)
                    gname = 'aqg' if which < 2 else 'akg'
                    qn = tmp()
                    k.op('dve', lambda e, qn=qn, zt_=zt_, sq=sq, gname=gname: e.scalar_tensor_tensor(
                        out=qn[:, :n], in0=zt_[:, :n], scalar=vecs[:, l, VC[gname]:VC[gname] + 1], in1=sq[:, :n],
                        op0=ALU.mult, op1=ALU.mult), [zt_, sq, vecs], [qn])
                    if not isctx:
                        tl = t0 - NCTX
                        p = nps()
                        k.mm(p[:, :n], cst[:, C_ROT:C_ROT + 128], qn[:, :n], True, True, [cst, qn], [p])
                        o2 = tmp()
                        k.tt(o2[:, :n], p[:, :n], rope[:, 1, tl:tl + n], ALU.mult, [p, rope], [o2])
                        k.tt(qn[:, :n], qn[:, :n], rope[:, 0, tl:tl + n], ALU.mult, [qn, rope], [qn])
                        k.tt(qn[:, :n], qn[:, :n], o2[:, :n], ALU.add, [qn, o2], [qn])
                    if which < 2:
                        k.ts(qb[:, which, t0:t0 + n], qn[:, :n], 0.125, ALU.mult, [qn], [qb])
                    else:
                        k.dma(KA[b][:, t0:t0 + n], qn[:, :n], [qn], [KA])
            for kv in range(2):
                for half in range(2):
                    k.dma(kd32[64 * half:64 * half + 64, kv, :], KA[b][64 * kv:64 * kv + 64, :], [KA], [kd32])
            k.cp(kdb[:], kd32[:], [kd32], [kdb], eng='pool')
            for kt in range(18):
                vt = tmp()
                k.dma(vt[:, :128], zT[b][30][:, kt * 128:(kt + 1) * 128], [zT], [vt])
                p = nps()
                k.op('pe', lambda e, p=p, vt=vt: e.transpose(out=p[:, :128], in_=vt[:, :128], identity=IDf),
                     [vt, cst], [p])
                for kv in range(2):
                    for g in range(2):
                        k.cp(vpad[:, kt, kv, g, 64 * g:64 * g + 64], p[:, 64 * kv:64 * kv + 64], [p], [vpad],
                             eng=('act' if g else 'dve'))
            qtiles = [(NCTX + i * 128, T) for i in range(NLAT // 128)]
            if l < L - 1:
                qtiles = [(0, NCTX), (128, NCTX)] + qtiles
            for hp in range(2):
                for (q0, nk) in qtiles:
                    nkt = nk // 128
                    pO = CST['PS0']
                    for g in range(2):
                        off = 64 * g
                        for kb0 in range(0, nk, 512):
                            kn = min(512, nk - kb0)
                            p = nps()
                            k.mm(p[:, :kn], qb[off:off + 64, hp, q0:q0 + 128], kdb[off:off + 64, hp, kb0:kb0 + kn],
                                 True, True, [qb, kdb], [p])
                            k.cp(Sf[:, kb0:kb0 + kn], p[:, :kn], [p], [Sf], eng='act')
                        k.op('dve', lambda e: e.tensor_reduce(out=st_[:, 0:1], in_=Sf[:, :nk], axis=AX.X, op=ALU.max),
                             [Sf], [st_])
                        k.ts(st_[:, 1:2], st_[:, 0:1], -1.0, ALU.mult, [st_], [st_])
                        k.act(Pb[:, :nk], Sf[:, :nk], AF.Exp, [Sf, st_], [Pb, st_], bias=st_[:, 1:2],
                              accum_out=st_[:, 2:3])
                        k.op('dve', lambda e: e.reciprocal(out=st_[:, 3:4], in_=st_[:, 2:3]), [st_], [st_])
                        k.ts(Pb[:, :nk], Pb[:, :nk], st_[:, 3:4], ALU.mult, [Pb, st_], [Pb])
                        for k4 in range(0, nkt, 4):
                            nn = min(4, nkt - k4)
                            p = nps()
                            pb_ = p.h[:].bitcast(BF16)
                            for j in range(nn):
                                k.op('pe', lambda e, j=j, pb_=pb_, k4=k4: e.transpose(
                                    out=pb_[:, j * 128:(j + 1) * 128], in_=Pb[:, (k4 + j) * 128:(k4 + j + 1) * 128],
                                    identity=IDb), [Pb, cstb], [p])
                            k.cp(PT[:, k4:k4 + nn, :].rearrange("p a b -> p (a b)"), pb_[:, :nn * 128], [p], [PT],
                                 eng=('act' if (k4 // 4) % 2 else 'dve'))
                        for kt in range(nkt):
                            k.mm(pO[:, :128], vpad[:, kt, hp, g, :], PT[:, kt, :], (g == 0 and kt == 0),
                                 (g == 1 and kt == nkt - 1), [vpad, PT], [pO])
                    o = ob[(q0 // 128) % 2]
                    k.cp(o[:], pO[:, :128], [pO], [o], eng='act')
                    k.dma(BR[3][b][hp][:, q0:q0 + 128], o[:], [o], [BR])
        k.barrier()


def merge_phase(k, l, xT_cur, x1T, modT, IN, CST, nps):
    cst, vecs = CST['cst'], CST['vecs']
    BR = CST['BR']
    ONE = cst[:, C_ONE:C_ONE + 128]
    ph = ExitStack()
    with ph:
        k.es = ph
        wg = k.sb([128, 4, 8, 1024], BF16, name='wg%d' % l)
        wb = k.sb([128, 4, 2, 1024], BF16, name='wb%d' % l)
        wo = k.sb([128, 8, 1024], BF16, name='wo%d' % l)
        stg = [k.sb([128, 1024], F32, name='mstg%d_%d' % (l, i)) for i in range(2)]
        si = 0
        for i in range(4):
            for hh in range(8):
                w = stg[si % 2]
                si += 1
                wv_ = w[:].rearrange("p (k n) -> p k n", k=8)
                k.dma(wv_, IN['w_gate'][l][i].rearrange("(k p) n -> p k n", p=128)[:, :, hh * 128:(hh + 1) * 128],
                      [IN['w_gate']], [w])
                k.cp(wg[:, i, :, hh * 128:(hh + 1) * 128], wv_, [w], [wg], eng='pool')
            for hh in range(2):
                w = stg[si % 2]
                si += 1
                wv_ = w[:].rearrange("p (k n) -> p k n", k=2)
                k.dma(wv_, IN['w_branch'][l][i].rearrange("(k p) n -> p k n", p=128)[:, :, hh * 512:(hh + 1) * 512],
                      [IN['w_branch']], [w])
                k.cp(wb[:, i, :, hh * 512:(hh + 1) * 512], wv_, [w], [wb], eng='pool')
        for hh in range(8):
            w = stg[si % 2]
            si += 1
            wv_ = w[:].rearrange("p (k n) -> p k n", k=8)
            k.dma(wv_, IN['w_out'][l].rearrange("(k p) n -> p k n", p=128)[:, :, hh * 128:(hh + 1) * 128],
                  [IN['w_out']], [w])
            k.cp(wo[:, :, hh * 128:(hh + 1) * 128], wv_, [w], [wo], eng='pool')
        xt = k.sb([128, 8, 512], F32, name='mxt%d' % l)
        ut = k.sb([128, 8, 512], BF16, name='mut%d' % l)
        br32 = [k.sb([128, 2, 512], F32, name='mbr32_%d_%d' % (l, i)) for i in range(2)]
        brb = k.sb([128, 4, 2, 512], BF16, name='mbrb_%d' % l)
        mg = k.sb([128, 8, 512], BF16, name='mmg%d' % l)
        acc = k.sb([128, 512], F32, name='macc%d' % l)
        sg = [k.sb([128, 512], F32, name='msg%d_%d' % (l, i)) for i in range(2)]
        y = k.sb([128, 8, 512], F32, name='my%d' % l)
        ysq = [k.sb([128, 512], F32, name='mysq%d_%d' % (l, i)) for i in range(2)]
        mean = k.sb([128, 512], F32, name='mmean%d' % l)
        rstd = k.sb([128, 512], F32, name='mrstd%d' % l)
        xo = [k.sb([128, 512], F32, name='mxo%d_%d' % (l, i)) for i in range(2)]
        blocks = tok_blocks() if l < L - 1 else tok_blocks()[1:]
        for b in range(NB):
            for (t0, n, isctx) in blocks:
                r = 2 if isctx else b
                k.dma(xt[:, :, :n], xT_cur[b][:, :, t0:t0 + n].rearrange("k p t -> p k t"), [xT_cur], [xt])
                for kk in range(8):
                    k.act(ut[:, kk, :n], xt[:, kk, :n], AF.Identity, [xt, modT], [ut],
                          bias=modT[:, kk, r:r + 1], scale=modT[:, 8 + kk, r:r + 1])
                k.ts(xt[:, :, :n], xt[:, :, :n], ALPHA, ALU.mult, [xt], [xt], eng='pool')
                for i in range(4):
                    b32 = br32[i % 2]
                    k.dma(b32[:, :, :n], BR[i][b][:, :, t0:t0 + n].rearrange("h p t -> p h t"), [BR], [b32])
                    k.cp(brb[:, i, :, :n], b32[:, :, :n], [b32], [brb], eng='pool')
                for mo in range(8):
                    ms = slice(mo * 128, (mo + 1) * 128)
                    for i in range(4):
                        pG = nps()
                        for kk in range(8):
                            k.mm(pG[:, :n], wg[:, i, kk, ms], ut[:, kk, :n], kk == 0, kk == 7, [wg, ut], [pG])
                        pB = nps()
                        for hh in range(2):
                            k.mm(pB[:, :n], wb[:, i, hh, ms], brb[:, i, hh, :n], hh == 0, hh == 1, [wb, brb], [pB])
                        s_ = sg[i % 2]
                        k.act(s_[:, :n], pG[:, :n], AF.Sigmoid, [pG], [s_])
                        if i == 0:
                            k.tt(acc[:, :n], s_[:, :n], pB[:, :n], ALU.mult, [s_, pB], [acc])
                        else:
                            k.tt(s_[:, :n], s_[:, :n], pB[:, :n], ALU.mult, [s_, pB], [s_])
                            if i < 3:
                                k.tt(acc[:, :n], acc[:, :n], s_[:, :n], ALU.add, [acc, s_], [acc])
                            else:
                                k.tt(mg[:, mo, :n], acc[:, :n], s_[:, :n], ALU.add, [acc, s_], [mg])
                for mo in range(8):
                    ms = slice(mo * 128, (mo + 1) * 128)
                    p = nps()
                    for kk in range(8):
                        k.mm(p[:, :n], wo[:, kk, ms], mg[:, kk, :n], kk == 0, kk == 7, [wo, mg], [p])
                    k.op('dve', lambda e, p=p, mo=mo: e.scalar_tensor_tensor(
                        out=y[:, mo, :n], in0=p[:, :n], scalar=modT[:, 16 + mo, r:r + 1], in1=xt[:, mo, :n],
                        op0=ALU.mult, op1=ALU.add), [p, modT, xt], [y])
                layer_norm_cm(k, y, ysq, mean, rstd, n, vecs[:, l, VC['ln1g']:VC['ln1g'] + 8],
                              vecs[:, l, VC['ln1b']:VC['ln1b'] + 8], vecs, ONE, cst, nps, xo,
                              lambda mo, b=b, t0=t0, n=n: x1T[b][mo][:, t0:t0 + n], x1T)
        k.barrier()


def layer_norm_cm(k, y, ysq, mean, rstd, n, gcols, bcols, vecs, ONE, cst, nps, xo, dst_fn, dstbuf):
    p1 = nps()
    for mo in range(8):
        k.mm(p1[:, :n], ONE, y[:, mo, :n], mo == 0, mo == 7, [cst, y], [p1])
    p2 = nps()
    for mo in range(8):
        q_ = ysq[mo % 2]
        k.tt(q_[:, :n], y[:, mo, :n], y[:, mo, :n], ALU.mult, [y], [q_], eng='pool')
        k.mm(p2[:, :n], ONE, q_[:, :n], mo == 0, mo == 7, [cst, q_], [p2])
    k.ts(mean[:, :n], p1[:, :n], 1.0 / D, ALU.mult, [p1], [mean])
    k.tt(rstd[:, :n], mean[:, :n], mean[:, :n], ALU.mult, [mean], [rstd])
    k.op('dve', lambda e: e.scalar_tensor_tensor(out=rstd[:, :n], in0=p2[:, :n], scalar=1.0 / D, in1=rstd[:, :n],
                                                 op0=ALU.mult, op1=ALU.subtract), [p2, rstd], [rstd])
    k.ts(rstd[:, :n], rstd[:, :n], 1e-5, ALU.add, [rstd], [rstd])
    k.act(rstd[:, :n], rstd[:, :n], AF.Ln, [rstd], [rstd])
    k.act(rstd[:, :n], rstd[:, :n], AF.Exp, [rstd], [rstd], scale=-0.5)
    for mo in range(8):
        o = xo[mo % 2]
        k.tt(o[:, :n], y[:, mo, :n], mean[:, :n], ALU.subtract, [y, mean], [o])
        k.tt(o[:, :n], o[:, :n], rstd[:, :n], ALU.mult, [o, rstd], [o])
        k.act(o[:, :n], o[:, :n], AF.Identity, [o, vecs], [o], bias=bcols[:, mo:mo + 1], scale=gcols[:, mo:mo + 1])
        k.dma(dst_fn(mo), o[:, :n], [o], [dstbuf])


def moe_phase(k, l, x1T, dstT, dst_off, modT, IN, CST, nps):
    cst, vecs = CST['cst'], CST['vecs']
    ONE = cst[:, C_ONE:C_ONE + 128]
    IDf = cst[:, C_ID:C_ID + 128]
    ph = ExitStack()
    with ph:
        k.es = ph
        selm = k.sb([16, 2048], F32, name='selm%d' % l)
        k.dma(selm[:], IN['selm'][:], [IN['selm']], [selm])
        wr = k.sb([128, 8, 16], F32, name='wr%d' % l)
        k.dma(wr[:], IN['w_router'][l].rearrange("(k p) e -> p k e", p=128), [IN['w_router']], [wr])
        W = [k.sb([128, 8, 1024], BF16, name='wexp%d_%d' % (l, i)) for i in range(3)]
        stg = [k.sb([128, 1024], F32, name='estg%d_%d' % (l, i)) for i in range(2)]
        acc = k.sb([128, 8, 1024], F32, name='eacc%d' % l)
        u2b = k.sb([128, 8, 1024], BF16, name='eu2b%d' % l)
        hT = k.sb([128, 8, 512], BF16, name='ehT%d' % l)
        affT = k.sb([16, 2048], F32, name='eaff%d' % l)
        work = k.sb([16, 2048], F32, name='ework%d' % l)
        m8 = k.sb([16, 8], F32, name='em8%d' % l)
        x1t = k.sb([128, 8, 512], F32, name='ex1t%d' % l)
        u2f = k.sb([128, 8, 128], F32, name='eu2f%d' % l)
        lg = k.sb([128, 16], F32, name='elg%d' % l)
        st_ = k.sb([128, 4], F32, name='est%d' % l)
        gb = k.sb([128, 512], F32, name='egb%d' % l)
        sl = [k.sb([128, 512], F32, name='esl%d_%d' % (l, i)) for i in range(2)]
        y = x1t
        ysq = [k.sb([128, 512], F32, name='eysq%d_%d' % (l, i)) for i in range(2)]
        mean = k.sb([128, 512], F32, name='emean%d' % l)
        rstd = k.sb([128, 512], F32, name='erstd%d' % l)
        xo = [k.sb([128, 512], F32, name='exo%d_%d' % (l, i)) for i in range(2)]
        si = [0]
        segs = [(NCTX, NLAT, None)] + ([(0, NCTX, 2)] if l < L - 1 else [])
        for b in range(NB):
            for (s0, ntok, rr) in segs:
                r = b if rr is None else rr
                cap = 2 * ntok // 16
                for tt_ in range(ntok // 128):
                    t0 = s0 + tt_ * 128
                    k.dma(x1t[:, :, :128], x1T[b][:, :, t0:t0 + 128].rearrange("k p t -> p k t"), [x1T], [x1t])
                    for kk in range(8):
                        k.act(u2f[:, kk, :], x1t[:, kk, :128], AF.Identity, [x1t, modT], [u2f],
                              bias=modT[:, 24 + kk, r:r + 1], scale=modT[:, 32 + kk, r:r + 1])
                    p = nps()
                    for kk in range(8):
                        k.mm(p[:, :16], u2f[:, kk, :], wr[:, kk, :], kk == 0, kk == 7, [u2f, wr], [p])
                    k.op('dve', lambda e, p=p: e.tensor_reduce(out=st_[:, 0:1], in_=p[:, :16], axis=AX.X, op=ALU.max),
                         [p], [st_])
                    k.ts(st_[:, 1:2], st_[:, 0:1], -1.0, ALU.mult, [st_], [st_])
                    k.act(lg[:], p[:, :16], AF.Exp, [p, st_], [lg, st_], bias=st_[:, 1:2], accum_out=st_[:, 2:3])
                    k.op('dve', lambda e: e.reciprocal(out=st_[:, 3:4], in_=st_[:, 2:3]), [st_], [st_])
                    k.ts(lg[:], lg[:], st_[:, 3:4], ALU.mult, [lg, st_], [lg])
                    p = nps()
                    k.op('pe', lambda e, p=p: e.transpose(out=p[0:16, :128], in_=lg[:], identity=IDf), [lg, cst], [p])
                    k.cp(affT[:, tt_ * 128:(tt_ + 1) * 128], p[0:16, :128], [p], [affT], eng='act')
                k.cp(work[:, :ntok], affT[:, :ntok], [affT], [work])
                nit = cap // 8
                for it in range(nit):
                    k.op('dve', lambda e: e.max(out=m8[:], in_=work[:, :ntok]), [work], [m8])
                    if it < nit - 1:
                        k.op('dve', lambda e: e.match_replace(out=work[:, :ntok], in_to_replace=m8[:],
                                                              in_values=work[:, :ntok], imm_value=-1.0),
                             [work, m8], [work])
                k.ts(work[:, :ntok], affT[:, :ntok], m8[:, 7:8], ALU.is_ge, [affT, m8], [work])
                k.tt(work[:, :ntok], work[:, :ntok], affT[:, :ntok], ALU.mult, [work, affT], [work])
                for c0 in range(0, ntok, 1024):
                    cn = min(1024, ntok - c0)
                    for bb0 in range(0, cn, 512):
                        bn = min(512, cn - bb0)
                        t0 = s0 + c0 + bb0
                        k.dma(x1t[:, :, :bn], x1T[b][:, :, t0:t0 + bn].rearrange("k p t -> p k t"), [x1T], [x1t])
                        for kk in range(8):
                            k.act(u2b[:, kk, bb0:bb0 + bn], x1t[:, kk, :bn], AF.Identity, [x1t, modT], [u2b],
                                  bias=modT[:, 24 + kk, r:r + 1], scale=modT[:, 32 + kk, r:r + 1])
                    for e_ in range(16):
                        for wi, wname in enumerate(('w_e1', 'w_e3', 'w_e2')):
                            wv = IN[wname][l][e_].rearrange("(k p) n -> p k n", p=128)
                            for hh in range(8):
                                w = stg[si[0] % 2]
                                si[0] += 1
                                wv_ = w[:].rearrange("p (k n) -> p k n", k=8)
                                k.dma(wv_, wv[:, :, hh * 128:(hh + 1) * 128], [IN[wname]], [w])
                                k.cp(W[wi][:, :, hh * 128:(hh + 1) * 128], wv_, [w], [W[wi]],
                                     eng=('pool' if hh % 2 else 'act'))
                        for bb0 in range(0, cn, 512):
                            bn = min(512, cn - bb0)
                            p = nps()
                            k.mm(p[:, :bn], selm[0:16, e_ * 128:(e_ + 1) * 128], work[0:16, c0 + bb0:c0 + bb0 + bn],
                                 True, True, [selm, work], [p])
                            k.cp(gb[:, :bn], p[:, :bn], [p], [gb], eng='act')
                            for f in range(8):
                                fs = slice(f * 128, (f + 1) * 128)
                                p1 = nps()
                                for kk in range(8):
                                    k.mm(p1[:, :bn], W[0][:, kk, fs], u2b[:, kk, bb0:bb0 + bn], kk == 0, kk == 7,
                                         [W[0], u2b], [p1])
                                p3 = nps()
                                for kk in range(8):
                                    k.mm(p3[:, :bn], W[1][:, kk, fs], u2b[:, kk, bb0:bb0 + bn], kk == 0, kk == 7,
                                         [W[1], u2b], [p3])
                                s_ = sl[f % 2]
                                k.act(s_[:, :bn], p1[:, :bn], AF.Silu, [p1], [s_])
                                k.tt(s_[:, :bn], s_[:, :bn], p3[:, :bn], ALU.mult, [s_, p3], [s_])
                                k.tt(hT[:, f, :bn], s_[:, :bn], gb[:, :bn], ALU.mult, [s_, gb], [hT], eng='pool')
                            for dch in range(8):
                                py = nps()
                                for f in range(8):
                                    k.mm(py[:, :bn], W[2][:, f, dch * 128:(dch + 1) * 128], hT[:, f, :bn], f == 0, f == 7,
                                         [W[2], hT], [py])
                                if e_ == 0:
                                    k.cp(acc[:, dch, bb0:bb0 + bn], py[:, :bn], [py], [acc], eng='act')
                                else:
                                    k.tt(acc[:, dch, bb0:bb0 + bn], acc[:, dch, bb0:bb0 + bn], py[:, :bn], ALU.add,
                                         [acc, py], [acc])
                    for bb0 in range(0, cn, 512):
                        bn = min(512, cn - bb0)
                        t0 = s0 + c0 + bb0
                        k.dma(x1t[:, :, :bn], x1T[b][:, :, t0:t0 + bn].rearrange("k p t -> p k t"), [x1T], [x1t])
                        k.ts(x1t[:, :, :bn], x1t[:, :, :bn], ALPHA, ALU.mult, [x1t], [x1t], eng='pool')
                        for mo in range(8):
                            k.op('dve', lambda e, mo=mo: e.scalar_tensor_tensor(
                                out=y[:, mo, :bn], in0=acc[:, mo, bb0:bb0 + bn], scalar=modT[:, 40 + mo, r:r + 1],
                                in1=x1t[:, mo, :bn], op0=ALU.mult, op1=ALU.add), [acc, modT, x1t], [x1t])
                        layer_norm_cm(k, y, ysq, mean, rstd, bn, vecs[:, l, VC['ln2g']:VC['ln2g'] + 8],
                                      vecs[:, l, VC['ln2b']:VC['ln2b'] + 8], vecs, ONE, cst, nps, xo,
                                      lambda mo, b=b, t0=t0, bn=bn: dstT[b][mo][:, t0 - dst_off:t0 - dst_off + bn], dstT)
        k.barrier()


def tok_blocks():
    bl = [(0, NCTX, True)]
    for i in range(NLAT // 512):
        bl.append((NCTX + i * 512, 512, False))
    return bl


def build(stop_after=None, dbg=None):
    nc = bass.Bass("TRN2", target_bir_lowering=False)
    es = ExitStack()
    with es:
        k = KB(nc, es)
        IN = {}

        def inp(name, shape, dt=F32):
            IN[name] = k.dram(name, shape, dt, kind='ExternalInput')
            return IN[name]
        xT0 = inp('xT0', [NB, 8, 128, T])
        cT = inp('cT', [128, 8, 3])
        w_mod = inp('w_mod', [L, D, 6 * D])
        b_modT = inp('b_modT', [L, 128, 48])
        w_in = inp('w_in', [L, D, DIN])
        out = k.dram('out', [NB, 8, 128, NLAT], F32, kind='ExternalOutput')
        dbg_out = None
        if dbg is not None:
            dbg_out = k.dram('dbg', dbg, F32, kind='ExternalOutput')

        zT = k.dram('zT', [NB, NZC, 128, T])
        PS = [k.ps([128, 512], F32, name='psb%d' % i) for i in range(8)]
        psi = [0]

        def nps():
            psi[0] = psi[0] % 7 + 1
            return PS[psi[0]]

        CST = {'PS0': PS[0]}
        for nm_, shp in (('vecs', [128, L, NVC]), ('cst', [128, NCST]), ('rope', [128, 2, NLAT])):
            inp(nm_, shp)
            CST[nm_] = k.sb(shp, F32, name=nm_ + '_sb')
            k.dma(CST[nm_][:], IN[nm_][:], [IN[nm_]], [CST[nm_]])
        CST['cstb'] = k.sb([128, NCST], BF16, name='cstb')
        k.cp(CST['cstb'][:], CST['cst'][:], [CST['cst']], [CST['cstb']])
        inp('rwkv_w2', [L, 2, 64, 256])
        inp('rwkv_a2', [L, 2, 64, 256])
        inp('rwkv_g2', [L, 128, 256])
        CST['SI'] = {nm_: k.dram('SI_' + nm_, [3, 2, NB, 2, 128, T]) for nm_ in ('q', 'k', 'v', 'w', 'kk', 'b')}
        CST['YC'] = k.dram('YC', [3, 2, NB, 2, 128, T])
        CST['BR'] = k.dram('BR', [4, NB, 2, 128, T])
        CST['AUX'] = k.dram('AUX', [2, NB, 2, 128, T])
        CST['KA'] = k.dram('KA', [NB, 128, T])
        inp('w_gate', [L, 4, D, D])
        inp('w_branch', [L, 4, 256, D])
        inp('w_out', [L, D, D])
        x1T = k.dram('x1T', [NB, 8, 128, T])
        x2T = k.dram('x2T', [NB, 8, 128, T])
        inp('selm', [16, 2048])
        inp('w_router', [L, D, 16])
        inp('w_e1', [L, 16, D, D])
        inp('w_e3', [L, 16, D, D])
        inp('w_e2', [L, 16, D, D])
        modT = k.sb([128, 48, 3], F32, name='modT')
        silu_c = k.sb([128, 8, 3], F32, name='silu_c')
        bmod = k.sb([128, 48], F32, name='bmod')
        ctile = k.sb([128, 8, 3], F32, name='ctile')
        k.dma(ctile[:], cT[:], reads=[cT], writes=[ctile])
        k.act(silu_c[:], ctile[:], AF.Silu, [ctile], [silu_c])

        xT_cur = xT0
        for l in range(L):
            with ExitStack() as ph:
                k.es = ph
                k.dma(bmod[:], b_modT[l], reads=[b_modT], writes=[bmod])
                wst = [k.sb([128, 8, 768], F32, name='wmod_st%d_%d' % (l, i)) for i in range(2)]
                wv = w_mod.h.ap()[l].rearrange("(k p) n -> p k n", p=128)
                for jb in range(8):
                    w = wst[jb % 2]
                    k.dma(w[:], wv[:, :, jb * 768:(jb + 1) * 768], reads=[w_mod], writes=[w])
                    for jj in range(6):
                        j = jb * 6 + jj
                        p = nps()
                        for kk in range(8):
                            k.mm(p[:, 0:3], w[:, kk, jj * 128:(jj + 1) * 128], silu_c[:, kk, :], kk == 0, kk == 7,
                                 [w, silu_c], [p])
                        k.ts(modT[:, j, :], p[:, 0:3], bmod[:, j:j + 1], ALU.add, [p, bmod], [modT])
                for lo_ in (8, 32):
                    k.ts(modT[:, lo_:lo_ + 8, :], modT[:, lo_:lo_ + 8, :], 1.0, ALU.add, [modT], [modT])
                k.barrier()
            if stop_after == 'A':
                k.es = es
                t = k.sb([128, 144], F32, name='dbgt')
                k.cp(t[:], modT[:].rearrange("p a b -> p (a b)"), [modT], [t])
                k.dma(dbg_out[:], t[:], reads=[t], writes=[dbg_out])
                k.final_wait([dbg_out])
                return nc
            with ExitStack() as ph:
                k.es = ph
                wbf = k.sb([128, 8, DIN], BF16, name='win_bf%d' % l)
                wst = [k.sb([128, 8, 496], F32, name='win_st%d_%d' % (l, i)) for i in range(2)]
                wv = w_in.h.ap()[l].rearrange("(k p) n -> p k n", p=128)
                for cb in range(8):
                    w = wst[cb % 2]
                    k.dma(w[:], wv[:, :, cb * 496:(cb + 1) * 496], reads=[w_in], writes=[w])
                    k.cp(wbf[:, :, cb * 496:(cb + 1) * 496], w[:], [w], [wbf], eng='pool')
                xts = [k.sb([128, 8, 512], F32, name='xt%d_%d' % (l, i)) for i in range(2)]
                uts = [k.sb([128, 8, 512], BF16, name='ut%d_%d' % (l, i)) for i in range(2)]
                zsb = [k.sb([128, 512], F32, name='zsb%d_%d' % (l, i)) for i in range(4)]
                it = 0
                zi = 0
                for b in range(NB):
                    for (t0, n, isctx) in tok_blocks():
                        r = 2 if isctx else b
                        xt = xts[it % 2]
                        ut = uts[it % 2]
                        it += 1
                        k.dma(xt[:, :, :n], xT_cur[b][:, :, t0:t0 + n].rearrange("k p t -> p k t"),
                              reads=[xT_cur], writes=[xt])
                        for kk in range(8):
                            k.act(ut[:, kk, :n], xt[:, kk, :n], AF.Identity, [xt, modT], [ut],
                                  bias=modT[:, kk, r:r + 1], scale=modT[:, 8 + kk, r:r + 1])
                        for mc in range(NZC):
                            p = nps()
                            for kk in range(8):
                                k.mm(p[:, :n], wbf[:, kk, mc * 128:(mc + 1) * 128], ut[:, kk, :n], kk == 0, kk == 7,
                                     [wbf, ut], [p])
                            zs = zsb[zi % 4]
                            zi += 1
                            if zi % 2 == 0:
                                k.cp(zs[:, :n], p[:, :n], [p], [zs], eng='act')
                            else:
                                k.cp(zs[:, :n], p[:, :n], [p], [zs], eng='dve')
                            k.dma(zT[b][mc][:, t0:t0 + n], zs[:, :n], reads=[zs], writes=[zT])
                k.barrier()
            if stop_after == 'B':
                k.es = es
                t = k.sb([128, NZC, 512], F32, name='dbgt')
                k.dma(t[:], zT[0][:, :, 0:512].rearrange("c p t -> p c t"), reads=[zT], writes=[t])
                k.dma(dbg_out[:].rearrange("c p t -> p c t"), t[:], reads=[t], writes=[dbg_out])
                k.final_wait([dbg_out])
                return nc
            mixers_phase(k, l, zT, IN, CST, nps, stop_after, dbg_out)
            if stop_after != 'C':
                attn_phase(k, l, zT, CST, nps)
                merge_phase(k, l, xT_cur, x1T, modT, IN, CST, nps)
            if stop_after == 'D':
                k.es = es
                for b_ in range(NB):
                    t_ = k.sb([128, 8, T], F32, name='dbgD%d' % b_)
                    k.dma(t_[:], x1T[b_].rearrange("c p t -> p c t"), [x1T], [t_])
                    k.dma(dbg_out[b_].rearrange("c p t -> p c t"), t_[:], [t_], [dbg_out])
                k.final_wait([dbg_out])
                return nc
            if stop_after != 'C':
                if l < L - 1:
                    moe_phase(k, l, x1T, x2T, 0, modT, IN, CST, nps)
                    xT_cur = x2T
                    if stop_after == 'E':
                        k.es = es
                        for b_ in range(NB):
                            t_ = k.sb([128, 8, T], F32, name='dbgE%d' % b_)
                            k.dma(t_[:], x2T[b_].rearrange("c p t -> p c t"), [x2T], [t_])
                            k.dma(dbg_out[b_].rearrange("c p t -> p c t"), t_[:], [t_], [dbg_out])
                        k.final_wait([dbg_out])
                        return nc
                else:
                    moe_phase(k, l, x1T, out, NCTX, modT, IN, CST, nps)
            if stop_after == 'C':
                k.es = es
                BR = CST['BR']
                for m_ in range(3):
                    t_ = k.sb([128, 2, T], F32, name='dbgC%d' % m_)
                    k.dma(t_[:], BR[m_][0].rearrange("h p t -> p h t"), [BR], [t_])
                    k.dma(dbg_out[m_].rearrange("h p t -> p h t"), t_[:], [t_], [dbg_out])
                k.final_wait([dbg_out])
                return nc
        k.es = es
        k.final_wait([out])
    return nc


def make_inputs(inputs, core):
    b0 = core * NB
    x = inputs['x'][b0:b0 + NB]
    ctx = inputs['ctx'][b0:b0 + NB]
    xa = np.concatenate([ctx, x], axis=1)
    xT0 = np.ascontiguousarray(xa.transpose(0, 2, 1)).reshape(NB, 8, 128, T)
    c3 = np.stack([inputs['c'][b0], inputs['c'][b0 + 1], inputs['c_ctx']], axis=1)
    cT = np.ascontiguousarray(c3.reshape(8, 128, 3).transpose(1, 0, 2))
    b_modT = np.ascontiguousarray(inputs['b_mod'].reshape(L, 48, 128).transpose(0, 2, 1))
    d_ = {'xT0': xT0, 'cT': cT, 'w_mod': inputs['w_mod'], 'b_modT': b_modT, 'w_in': inputs['w_in']}
    d_.update(host_consts(inputs))
    selm = np.zeros((16, 2048), np.float32)
    for e_ in range(16):
        selm[e_, e_ * 128:(e_ + 1) * 128] = 1.0
    d_['selm'] = selm
    for nm_ in ('rwkv_w2', 'rwkv_a2', 'rwkv_g2', 'w_gate', 'w_branch', 'w_out', 'w_router', 'w_e1', 'w_e3', 'w_e2'):
        d_[nm_] = inputs[nm_]
    return d_


_HC = {}


def host_consts(inputs):
    if 'v' in _HC:
        return _HC['v']
    vecs = np.zeros((128, L, NVC), np.float32)

    def put(name, arr):
        a = np.asarray(arr, np.float32).reshape(L, -1, 128)
        for j in range(a.shape[1]):
            vecs[:, :, VC[name] + j] = a[:, j, :].T
    put('mu', inputs['rwkv_mu'])
    put('w0', inputs['rwkv_w0'].reshape(L, 512))
    put('a0', inputs['rwkv_a0'].reshape(L, 512))
    put('kk', inputs['rwkv_kk'])
    put('ka', inputs['rwkv_ka'])
    put('rk', inputs['rwkv_rk'])
    put('lng', inputs['rwkv_ln_g'])
    put('lnb', inputs['rwkv_ln_b'])
    put('retdec', np.repeat(inputs['ret_decay'].reshape(L, 8), 64, axis=1))
    put('retg', inputs['ret_norm_g'])
    put('retb', inputs['ret_norm_b'])
    hl = inputs['hgrn_lb']
    put('hlb', np.broadcast_to(hl.reshape(1, 2 * L * 256), (L, 2 * L * 256)))
    put('hng', inputs['hgrn_norm_g'])
    put('aqg', np.tile(inputs['attn_q_g'], (1, 2)))
    put('akg', np.tile(inputs['attn_k_g'], (1, 2)))
    put('ln1g', inputs['ln1_g'])
    put('ln1b', inputs['ln1_b'])
    put('ln2g', inputs['ln2_g'])
    put('ln2b', inputs['ln2_b'])
    cst = np.zeros((128, NCST), np.float32)
    p = np.arange(128)
    cst[:, C_BO:C_BO + 128] = (p[:, None] // 64 == p[None, :] // 64)
    cst[:, C_DM:C_DM + 64] = (p[:, None] % 64 == np.arange(64)[None, :])
    rot = np.zeros((128, 128), np.float32)
    for m_ in range(128):
        if m_ % 64 < 32:
            rot[m_ + 32, m_] = -1.0
        else:
            rot[m_ - 32, m_] = 1.0
    cst[:, C_ROT:C_ROT + 128] = rot
    cst[:, C_ID:C_ID + 128] = np.eye(128)
    cst[:, C_ONE:C_ONE + 128] = 1.0
    rows = NLAT // 64
    row = np.repeat(np.arange(rows), 64)
    col = np.tile(np.arange(64), rows)
    inv = (10000.0 ** (-np.arange(16, dtype=np.float32) / 16)).astype(np.float32)
    ang = np.concatenate([row[:, None] * inv, col[:, None] * inv], axis=-1).astype(np.float32)
    cosT = np.cos(ang).astype(np.float32).T
    sinT = np.sin(ang).astype(np.float32).T
    rope = np.stack([np.tile(cosT, (4, 1)), np.tile(sinT, (4, 1))], axis=1)
    _HC['v'] = {'vecs': vecs, 'cst': cst, 'rope': np.ascontiguousarray(rope, dtype=np.float32)}
    return _HC['v']


def kernel(**inputs):
    inputs = {k_: np.asarray(v) for k_, v in inputs.items()}
    nc = build()
    in_maps = [make_inputs(inputs, c) for c in range(8)]
    res = run_bass_kernel_spmd(nc, in_maps, core_ids=list(range(8)))
    outs = []
    for c in range(8):
        o = res.results[c]['out']
        outs.append(o.reshape(NB, D, NLAT).transpose(0, 2, 1))
    return np.ascontiguousarray(np.concatenate(outs, axis=0)).astype(np.float32)
```

```python
import numpy as np
from contextlib import ExitStack
import concourse.bass as bass
import concourse.mybir as mybir
from concourse.bass_utils import run_bass_kernel_spmd

F32, BF16 = mybir.dt.float32, mybir.dt.bfloat16
AF = mybir.ActivationFunctionType
ALU = mybir.AluOpType
AX = mybir.AxisListType

NB = 2
NCTX = 256
NLAT = 2048
T = NCTX + NLAT
D = 1024
L = 2
DIN = 3968
NZC = DIN // 128
ALPHA = float((2 * L) ** 0.25)


class Buf:
    def __init__(self, h, dram=False):
        self.h = h
        self.dram = dram
        self.w = {}
        self.r = {}
        self._ap = h.ap() if dram else None

    def __getitem__(self, k):
        return self._ap[k] if self.dram else self.h[k]


class KB:
    def __init__(self, nc, es):
        self.nc, self.es = nc, es
        self.eng = {'pe': nc.tensor, 'act': nc.scalar, 'dve': nc.vector, 'pool': nc.gpsimd, 'sp': nc.sync}
        self.sem = {e: es.enter_context(nc.semaphore('s_' + e)) for e in ('pe', 'act', 'dve', 'pool')}
        self.cnt = {e: 0 for e in self.sem}
        self.NDS = 16
        for i in range(self.NDS):
            self.sem[i] = es.enter_context(nc.semaphore('d%d' % i))
        self.dcount = 0
        self.known = {e: {} for e in self.eng}
        self.nuniq = 0

    def sb(self, shape, dt=F32, name=None):
        self.nuniq += 1
        return Buf(self.es.enter_context(self.nc.sbuf_tensor(name or 'sb%d' % self.nuniq, list(shape), dt)))

    def ps(self, shape, dt=F32, name=None):
        self.nuniq += 1
        return Buf(self.es.enter_context(self.nc.psum_tensor(name or 'ps%d' % self.nuniq, list(shape), dt)))

    def dram(self, name, shape, dt=F32, kind='Internal'):
        return Buf(self.nc.dram_tensor(name, list(shape), dt, kind=kind), dram=True)

    def _need(self, reads, writes):
        need = {}
        for b in reads:
            for s, v in b.w.items():
                if need.get(s, 0) < v:
                    need[s] = v
        for b in writes:
            for s, v in list(b.w.items()) + list(b.r.items()):
                if need.get(s, 0) < v:
                    need[s] = v
        return need

    def _emit_waits(self, e, need):
        kn = self.known[e]
        eng = self.eng[e]
        for s, v in need.items():
            if kn.get(s, 0) >= v:
                continue
            if e == 'pe' and s == 'pe':
                continue
            eng.wait_ge(self.sem[s], v)
            kn[s] = v

    def _done(self, s, v, reads, writes):
        for b in reads:
            if b.r.get(s, 0) < v:
                b.r[s] = v
        for b in writes:
            b.w = {s: v}
            b.r = {}

    def op(self, e, fn, reads=(), writes=()):
        self._emit_waits(e, self._need(reads, writes))
        inst = fn(self.eng[e])
        self.cnt[e] += 1
        inst.then_inc(self.sem[e], 1)
        self._done(e, self.cnt[e], reads, writes)

    def dma(self, out_ap, in_ap, reads=(), writes=(), q='sp', **kw):
        i = self.dcount
        self.dcount += 1
        s = i % self.NDS
        v = 16 * (i // self.NDS + 1)
        need = self._need(reads, writes)
        if v > 16 and need.get(s, 0) < v - 16:
            need[s] = v - 16
        self._emit_waits(q, need)
        self.eng[q].dma_start(out=out_ap, in_=in_ap, **kw).then_inc(self.sem[s], 16)
        self._done(s, v, reads, writes)

    def barrier(self):
        allev = {e: self.cnt[e] for e in ('pe', 'act', 'dve', 'pool') if self.cnt[e] > 0}
        for i in range(min(self.dcount, self.NDS)):
            last = self.dcount - 1 - ((self.dcount - 1 - i) % self.NDS)
            allev[i] = 16 * (last // self.NDS + 1)
        for e in self.eng:
            need = {s: v for s, v in allev.items() if s != e}
            self._emit_waits(e, need)

    def final_wait(self, bufs):
        need = self._need(bufs, ())
        self._emit_waits('sp', need)

    def mm(self, out, lhsT, rhs, start, stop, reads, writes):
        self.op('pe', lambda e: e.matmul(out, lhsT=lhsT, rhs=rhs, start=start, stop=stop), reads, writes)

    def act(self, out, in_, func, reads, writes, bias=None, scale=None, accum_out=None, eng='act'):
        kw = {}
        if bias is not None:
            kw['bias'] = bias
        if scale is not None:
            kw['scale'] = scale
        if accum_out is not None:
            kw['accum_out'] = accum_out
        self.op('act', lambda e: e.activation(out=out, in_=in_, func=func, **kw), reads, writes)

    def tt(self, out, in0, in1, op, reads, writes, eng='dve'):
        self.op(eng, lambda e: e.tensor_tensor(out=out, in0=in0, in1=in1, op=op), reads, writes)

    def ts(self, out, in0, s1, op0, reads, writes, s2=None, op1=None, eng='dve'):
        if op1 is None:
            self.op(eng, lambda e: e.tensor_scalar(out=out, in0=in0, scalar1=s1, scalar2=None, op0=op0), reads, writes)
        else:
            self.op(eng, lambda e: e.tensor_scalar(out=out, in0=in0, scalar1=s1, scalar2=s2, op0=op0, op1=op1),
                    reads, writes)

    def cp(self, out, in_, reads, writes, eng='dve'):
        if eng == 'act':
            self.op('act', lambda e: e.copy(out=out, in_=in_), reads, writes)
        else:
            self.op(eng, lambda e: e.tensor_copy(out=out, in_=in_), reads, writes)


VC = {}
_o = 0
for _n, _c in [('mu', 9), ('w0', 4), ('a0', 4), ('kk', 2), ('ka', 2), ('rk', 2), ('lng', 2), ('lnb', 2), ('retdec', 4),
               ('retg', 2), ('retb', 2), ('hlb', 8), ('hng', 2), ('aqg', 1), ('akg', 1), ('ln1g', 8), ('ln1b', 8),
               ('ln2g', 8), ('ln2b', 8)]:
    VC[_n] = _o
    _o += _c
NVC = _o
C_BO, C_DM, C_ROT, C_ID, C_ONE, C_MF, C_MR, C_MFS, C_MRS, NCST = 0, 128, 192, 320, 448, 576, 704, 832, 960, 1088
TB = 256
NSTEP_BLK = T // TB


def sap(buf, off, dims):
    h = buf.h
    full = h[:]
    pstride = full.ap[0][0]
    return bass.AP(h, off, [[pstride, 128]] + [list(d) for d in dims])


def mixers_phase(k, l, zT, IN, CST, nps, stop_after, dbg_out):
    vecs = CST['vecs']
    cst = CST['cst']
    cstb = CST['cstb']
    rope = CST['rope']
    SI = CST['SI']
    YC = CST['YC']
    BR = CST['BR']
    AUX = CST['AUX']

    def V(name, j=0):
        return vecs[:, l, VC[name] + j:VC[name] + j + 1]
    BO = cst[:, C_BO:C_BO + 128]
    BOb = cstb[:, C_BO:C_BO + 128]
    ph = ExitStack()
    with ph:
        k.es = ph
        der = k.sb([128, 40], F32, name='der%d' % l)
        k.ts(der[:, 0:9], vecs[:, l, VC['mu']:VC['mu'] + 9], 0.5, ALU.mult, [vecs], [der])
        k.ts(der[:, 9:18], vecs[:, l, VC['mu']:VC['mu'] + 9], -1.0, ALU.mult, [vecs], [der], s2=1.0, op1=ALU.add)
        k.ts(der[:, 18:20], vecs[:, l, VC['ka']:VC['ka'] + 2], -1.0, ALU.mult, [vecs], [der], s2=1.0, op1=ALU.add)
        k.act(der[:, 20:24], vecs[:, l, VC['retdec']:VC['retdec'] + 4], AF.Sigmoid, [vecs], [der])
        if l == 0:
            k.op('dve', lambda e: e.memset(der[:, 24:28], 0.0), [], [der])
        else:
            hl = vecs[:, l, VC['hlb']:VC['hlb'] + 8].rearrange("p (d l h) -> p d l h", d=2, l=2)
            k.tt(der[:, 24:28].rearrange("p (d h) -> p d h", d=2), hl[:, :, 1, :], hl[:, :, 0, :], ALU.subtract,
                 [vecs], [der])
            k.act(der[:, 24:28], der[:, 24:28], AF.Sigmoid, [der], [der])
        k.ts(der[:, 28:32], der[:, 24:28], -1.0, ALU.mult, [der], [der], s2=1.0, op1=ALU.add)
        W2 = k.sb([128, 256], F32, name='W2_%d' % l)
        A2 = k.sb([128, 256], F32, name='A2_%d' % l)
        G2 = k.sb([128, 256], F32, name='G2_%d' % l)
        k.dma(W2[:], IN['rwkv_w2'][l].rearrange("z r c -> (z r) c"), [IN['rwkv_w2']], [W2])
        k.dma(A2[:], IN['rwkv_a2'][l].rearrange("z r c -> (z r) c"), [IN['rwkv_a2']], [A2])
        k.dma(G2[:], IN['rwkv_g2'][l], [IN['rwkv_g2']], [G2])

        NT = 6
        tmps = [k.sb([128, 512], F32, name='ptmp%d_%d' % (l, i)) for i in range(NT)]
        ti = [0]

        def tmp():
            ti[0] = (ti[0] + 1) % NT
            return tmps[ti[0]]
        outs = [k.sb([128, 512], F32, name='pout%d_%d' % (l, i)) for i in range(6)]
        oi = [0]

        def store(dst_ap, dstbuf, fn):
            oi[0] = (oi[0] + 1) % 6
            o = outs[oi[0]]
            fn(o)
            return o

        zs = k.sb([128, 9, 514], F32, name='zs%d' % l)
        twt = k.sb([128, 512], F32, name='twt%d' % l)
        sgt = k.sb([128, 512], F32, name='sgt%d' % l)
        zz = k.sb([128, 9, 512], F32, name='zz%d' % l)
        zt2 = k.sb([128, 9, 512], F32, name='zt2%d' % l)
        zl = [k.sb([128, 512], F32, name='zl%d_%d' % (l, i)) for i in range(4)]
        zli = [0]

        def loadz(b, c, t0, n):
            zli[0] = (zli[0] + 1) % 4
            t_ = zl[zli[0]]
            k.dma(t_[:, :n], zT[b][c][:, t0:t0 + n], [zT], [t_])
            return t_

        def put(dst, name, m, d_, b, hp, t0, n, src):
            k.dma(dst[m][d_][b][hp][:, t0:t0 + n], src[:, :n], [src], [dst])

        for b in range(NB):
            for (t0, n, isctx) in tok_blocks():
                seg0, seg1 = (0, NCTX) if isctx else (NCTX, T)
                k.op('pool', lambda e: e.memset(zs[:, :, 0:1], 0.0), [], [zs])
                k.op('pool', lambda e: e.memset(zs[:, :, n + 1:n + 2], 0.0), [], [zs])
                a0_ = max(t0 - 1, seg0)
                a1_ = min(t0 + n + 1, seg1)
                k.dma(zs[:, :, 1 + (a0_ - t0):1 + (a1_ - t0)],
                      zT[b][0:9][:, :, a0_:a1_].rearrange("c p t -> p c t"), [zT], [zs])
                k.tt(zt2[:, :, :n], zs[:, :, 0:n], zs[:, :, 2:n + 2], ALU.add, [zs], [zt2])
                for c in range(9):
                    k.ts(zt2[:, c, :n], zt2[:, c, :n], der[:, c:c + 1], ALU.mult, [zt2, der], [zt2])
                    k.op('dve', lambda e, c=c: e.scalar_tensor_tensor(out=zz[:, c, :n], in0=zs[:, c, 1:n + 1],
                                                                      scalar=der[:, 9 + c:10 + c], in1=zt2[:, c, :n],
                                                                      op0=ALU.mult, op1=ALU.add), [zs, zt2, der], [zz])
                for hp in range(2):
                    k.dma(SI['q'][0][0][b][hp][:, t0:t0 + n], zz[:, hp, :n], [zz], [SI['q']])
                    k.dma(SI['v'][0][0][b][hp][:, t0:t0 + n], zz[:, 4 + hp, :n], [zz], [SI['v']])
                tw = twt
                k.act(tw[:, :n], zz[:, 6, :n], AF.Tanh, [zz], [tw])
                sg = sgt
                k.act(sg[:, :n], zz[:, 8, :n], AF.Sigmoid, [zz], [sg])
                akeep = {}
                for hp in range(2):
                    kkf = tmp()
                    k.ts(kkf[:, :n], zz[:, 2 + hp, :n], V('kk', hp), ALU.mult, [zz, vecs], [kkf])
                    sq = tmp()
                    k.tt(sq[:, :n], kkf[:, :n], kkf[:, :n], ALU.mult, [kkf], [sq])
                    p = nps()
                    k.mm(p[:, :n], BO, sq[:, :n], True, True, [cst, sq], [p])
                    rs = sq
                    k.ts(rs[:, :n], p[:, :n], 1e-12, ALU.add, [p], [rs])
                    k.act(rs[:, :n], rs[:, :n], AF.Ln, [rs], [rs])
                    k.act(rs[:, :n], rs[:, :n], AF.Exp, [rs], [rs], scale=-0.5)
                    kkn = k.sb([128, 512], F32, name='kkn%d_%d_%d_%d' % (l, b, t0, hp))
                    k.tt(kkn[:, :n], kkf[:, :n], rs[:, :n], ALU.mult, [kkf, rs], [kkn])
                    k.dma(SI['kk'][0][0][b][hp][:, t0:t0 + n], kkn[:, :n], [kkn], [SI['kk']])
                    rk_ = tmp()
                    k.op('dve', lambda e, hp=hp, rk_=rk_: e.scalar_tensor_tensor(
                        out=rk_[:, :n], in0=zz[:, hp, :n], scalar=V('rk', hp), in1=zz[:, 2 + hp, :n],
                        op0=ALU.mult, op1=ALU.mult), [zz, vecs], [rk_])
                    p = nps()
                    k.mm(p[:, :n], BO, rk_[:, :n], True, True, [cst, rk_], [p])
                    bon = tmp()
                    k.tt(bon[:, :n], p[:, :n], zz[:, 4 + hp, :n], ALU.mult, [p, zz], [bon])
                    k.dma(AUX[0][b][hp][:, t0:t0 + n], bon[:, :n], [bon], [AUX])
                    p = nps()
                    k.mm(p[:, :n], G2[:, hp * 128:(hp + 1) * 128], sg[:, :n], True, True, [G2, sg], [p])
                    gg = tmp()
                    k.cp(gg[:, :n], p[:, :n], [p], [gg], eng='act')
                    k.dma(AUX[1][b][hp][:, t0:t0 + n], gg[:, :n], [gg], [AUX])
                    for d_ in range(2):
                        ps_ = slice(64 * d_, 64 * d_ + 64)
                        p = nps()
                        k.mm(p[:, :n], W2[ps_, hp * 128:(hp + 1) * 128], tw[ps_, :n], True, True, [W2, tw], [p])
                        dec = tmp()
                        k.act(dec[:, :n], p[:, :n], AF.Sigmoid, [p, vecs], [dec], bias=V('w0', d_ * 2 + hp))
                        k.act(dec[:, :n], dec[:, :n], AF.Exp, [dec], [dec], scale=-float(np.exp(-0.5)))
                        k.dma(SI['w'][0][d_][b][hp][:, t0:t0 + n], dec[:, :n], [dec], [SI['w']])
                        p = nps()
                        k.mm(p[:, :n], A2[ps_, hp * 128:(hp + 1) * 128], zz[ps_, 7, :n], True, True, [A2, zz], [p])
                        a_ = tmp()
                        k.act(a_[:, :n], p[:, :n], AF.Sigmoid, [p, vecs], [a_], bias=V('a0', d_ * 2 + hp))
                        bb = tmp()
                        k.tt(bb[:, :n], kkn[:, :n], a_[:, :n], ALU.mult, [kkn, a_], [bb])
                        k.dma(SI['b'][0][d_][b][hp][:, t0:t0 + n], bb[:, :n], [bb], [SI['b']])
                        kd = tmp()
                        k.ts(kd[:, :n], a_[:, :n], V('ka', hp), ALU.mult, [a_, vecs, der], [kd],
                             s2=der[:, 18 + hp:19 + hp], op1=ALU.add)
                        k.tt(kd[:, :n], kd[:, :n], zz[:, 2 + hp, :n], ALU.mult, [kd, zz], [kd])
                        k.dma(SI['k'][0][d_][b][hp][:, t0:t0 + n], kd[:, :n], [kd], [SI['k']])
                for hp in range(2):
                    for which, cbase, scl in (('q', 9, 1.0), ('k', 11, 0.125)):
                        zt_ = loadz(b, cbase + hp, t0, n)
                        o = tmp()
                        if isctx:
                            k.ts(o[:, :n], zt_[:, :n], scl, ALU.mult, [zt_], [o])
                        else:
                            tl = t0 - NCTX
                            p = nps()
                            k.mm(p[:, :n], cst[:, C_ROT:C_ROT + 128], zt_[:, :n], True, True, [cst, zt_], [p])
                            o2 = tmp()
                            k.tt(o2[:, :n], p[:, :n], rope[:, 1, tl:tl + n], ALU.mult, [p, rope], [o2])
                            k.tt(o[:, :n], zt_[:, :n], rope[:, 0, tl:tl + n], ALU.mult, [zt_, rope], [o])
                            k.tt(o[:, :n], o[:, :n], o2[:, :n], ALU.add, [o, o2], [o])
                            if scl != 1.0:
                                k.ts(o[:, :n], o[:, :n], scl, ALU.mult, [o], [o])
                        k.dma(SI[which][1][0][b][hp][:, t0:t0 + n], o[:, :n], [o], [SI[which]])
                for hp in range(2):
                    zt_ = loadz(b, 17 + hp, t0, n)
                    o = tmp()
                    k.act(o[:, :n], zt_[:, :n], AF.Silu, [zt_], [o])
                    k.dma(SI['q'][2][0][b][hp][:, t0:t0 + n], o[:, :n], [o], [SI['q']])
                    for d_ in range(2):
                        zt_ = loadz(b, 19 + 2 * d_ + hp, t0, n)
                        sn = tmp()
                        k.act(sn[:, :n], zt_[:, :n], AF.Sigmoid, [zt_], [sn], scale=-1.0)
                        kk_ = tmp()
                        k.ts(kk_[:, :n], sn[:, :n], der[:, 28 + d_ * 2 + hp:29 + d_ * 2 + hp], ALU.mult, [sn, der], [kk_])
                        k.dma(SI['k'][2][d_][b][hp][:, t0:t0 + n], kk_[:, :n], [kk_], [SI['k']])
                        w_ = tmp()
                        k.ts(w_[:, :n], kk_[:, :n], -1.0, ALU.mult, [kk_], [w_], s2=1.0, op1=ALU.add)
                        k.dma(SI['w'][2][d_][b][hp][:, t0:t0 + n], w_[:, :n], [w_], [SI['w']])
        k.barrier()
    import os
    if os.environ.get('USE_CHUNK', '0') == '1':
        scan_all(k, l, zT, CST, nps, mixers=(0,))
        for m in (1, 2):
            chunk_mixer(k, l, m, zT, CST, nps)
    else:
        scan_all(k, l, zT, CST, nps)
    post_mixers(k, l, zT, CST, nps)
    if stop_after == 'C':
        k.es = ExitStack()
        t_ = None


def scan_mixer(k, l, m, zT, CST, nps):
    cst, cstb, vecs = CST['cst'], CST['cstb'], CST['vecs']
    SI, YC = CST['SI'], CST['YC']
    delta = (m == 0)
    BOb = cstb[:, C_BO:C_BO + 128]
    ph = ExitStack()
    with ph:
        k.es = ph
        names = ['q', 'k', 'v'] + (['w'] if m != 1 else []) + (['kk', 'b'] if delta else [])
        A = {nm: k.sb([128, 2, 4, TB], F32, name='A%s_%d_%d' % (nm, l, m)) for nm in names}
        Wc = k.sb([128, 2, 4], F32, name='Wc_%d_%d' % (l, m))
        if m == 1:
            for b in range(NB):
                k.act(Wc[:, :, 2 * b:2 * b + 2],
                      vecs[:, l, VC['retdec']:VC['retdec'] + 4].rearrange("p (d h) -> p d h", d=2),
                      AF.Sigmoid, [vecs], [Wc])
        S = k.sb([128, 512], F32, name='S_%d_%d' % (l, m))
        k.op('pool', lambda e: e.memset(S[:], 0.0), [], [S])
        Ych = k.sb([128, 2, 4, TB], F32, name='Ych_%d_%d' % (l, m))
        NR = 3
        R = [dict(vd=k.sb([128, 512], BF16), vbs=k.sb([128, 512], F32), kv=k.sb([128, 512], F32),
                  t1=k.sb([128, 512], BF16), t2=k.sb([128, 512], F32), t3=k.sb([128, 512], BF16),
                  t5=k.sb([128, 512], F32)) for _ in range(NR)]

        def v4(ap):
            return ap.rearrange("p (d c v) -> p d c v", d=2, c=4)

        def src(nm, d_, b, st):
            if m == 0:
                dd = d_ if nm in ('k', 'w', 'b') else 0
                return SI[nm], SI[nm][0][dd][b][:, :, st:st + TB]
            if m == 1:
                if nm == 'v':
                    return zT, zT[b][13:15][:, :, st:st + TB]
                return SI[nm], SI[nm][1][0][b][:, :, st:st + TB]
            if nm == 'v':
                return zT, zT[b][23:25][:, :, st:st + TB]
            dd = d_ if nm in ('k', 'w') else 0
            return SI[nm], SI[nm][2][dd][b][:, :, st:st + TB]
        S4 = v4(S[:])
        Dm4 = sap(cst, C_DM, [[0, 2], [0, 4], [1, 64]])
        Dm3 = sap(cst, C_DM, [[0, 8], [1, 64]])
        for j in range(NSTEP_BLK):
            st = [TB * j, 0 if j == 0 else T - TB * j]
            for nm in names:
                for d_ in range(2):
                    for b in range(NB):
                        sb_, sa_ = src(nm, d_, b, st[d_])
                        k.dma(A[nm][:, d_, 2 * b:2 * b + 2, :], sa_.rearrange("h p t -> p h t"), [sb_], [A[nm]])
            for s_ in range(TB):
                c0, c1 = s_, TB - 1 - s_
                r = R[s_ % NR]

                def step(nm):
                    if nm == 'w' and m == 1:
                        return sap(Wc, 0, [[4, 2], [1, 4], [0, 64]]), Wc
                    return sap(A[nm], c0, [[4 * TB + c1 - c0, 2], [TB, 4], [0, 64]]), A[nm]
                a_, ab = step('v')
                k.tt(v4(r['vd'][:]), a_, Dm4, ALU.mult, [ab, cst], [r['vd']], eng='pool')
                pA = nps()
                k.mm(pA[:], BOb, r['vd'][:], True, True, [cstb, r['vd']], [pA])
                k.cp(r['vbs'][:], pA[:], [pA], [r['vbs']], eng='act')
                a_, ab = step('k')
                k.tt(v4(r['kv'][:]), v4(r['vbs'][:]), a_, ALU.mult, [r['vbs'], ab], [r['kv']], eng='pool')
                if delta:
                    a_, ab = step('kk')
                    k.tt(v4(r['t1'][:]), S4, a_, ALU.mult, [S, ab], [r['t1']])
                    pB = nps()
                    k.mm(pB[:], BOb, r['t1'][:], True, True, [cstb, r['t1']], [pB])
                a_, ab = step('w')
                k.tt(S4, S4, a_, ALU.mult, [S, ab], [S])
                if delta:
                    a_, ab = step('b')
                    k.tt(v4(r['t2'][:]), v4(pB[:]), a_, ALU.mult, [pB, ab], [r['t2']])
                    k.tt(S[:], S[:], r['t2'][:], ALU.subtract, [S, r['t2']], [S])
                k.tt(S[:], S[:], r['kv'][:], ALU.add, [S, r['kv']], [S])
                a_, ab = step('q')
                k.tt(v4(r['t3'][:]), S4, a_, ALU.mult, [S, ab], [r['t3']])
                pC = nps()
                k.mm(pC[:], BOb, r['t3'][:], True, True, [cstb, r['t3']], [pC])
                t5v = r['t5'][:].rearrange("p (c v) -> p c v", c=8)
                k.tt(t5v, pC[:].rearrange("p (c v) -> p c v", c=8), Dm3, ALU.mult, [pC, cst], [r['t5']])
                yo = sap(Ych, c0, [[4 * TB + c1 - c0, 2], [TB, 4]])
                k.op('dve', lambda e, yo=yo, t5v=t5v: e.tensor_reduce(out=yo, in_=t5v, axis=AX.X, op=ALU.add),
                     [r['t5']], [Ych])
            for d_ in range(2):
                for b in range(NB):
                    k.dma(YC[m][d_][b][:, :, st[d_]:st[d_] + TB].rearrange("h p t -> p h t"),
                          Ych[:, d_, 2 * b:2 * b + 2, :], [Ych], [YC])
        k.barrier()


def chunk_order(d_):
    C = 64
    if d_ == 0:
        return [i * C for i in range(T // C)]
    return [i * C for i in reversed(range(NCTX // C))] + [i * C for i in reversed(range(NCTX // C, T // C))]


def chunk_mixer(k, l, m, zT, CST, nps):
    cst, cstb, vecs = CST['cst'], CST['cstb'], CST['vecs']
    SI, YC = CST['SI'], CST['YC']
    C = 64
    IDf = cst[:, C_ID:C_ID + 128]
    IDb = cstb[:, C_ID:C_ID + 128]
    BO = cst[:, C_BO:C_BO + 128]
    ph = ExitStack()
    with ph:
        k.es = ph
        zeros = k.sb([128, T], F32, name='cz_%d_%d' % (l, m))
        k.op('pool', lambda e: e.memset(zeros[:], 0.0), [], [zeros])
        q = k.sb([128, T], F32, name='cq_%d_%d' % (l, m))
        v = k.sb([128, T], F32, name='cv_%d_%d' % (l, m))
        kk_ = [k.sb([128, T], F32, name='ck_%d_%d_%d' % (l, m, d_)) for d_ in range(2)]
        cum = [k.sb([128, T], F32, name='ccum_%d_%d_%d' % (l, m, d_)) for d_ in range(2)]
        cpv = [k.sb([128, T], F32, name='ccpv_%d_%d_%d' % (l, m, d_)) for d_ in range(2)]
        Ych = [k.sb([128, T], F32, name='cY_%d_%d_%d' % (l, m, d_)) for d_ in range(2)]
        S = [k.sb([128, 128], F32, name='cS_%d_%d_%d' % (l, m, d_)) for d_ in range(2)]
        NR = 3
        R = [[dict(G=k.sb([128, C], F32), Gi=k.sb([128, C], F32), qt=k.sb([128, C], BF16), kt=k.sb([128, 128], BF16), kt32=k.sb([128, C], F32),
                   AT=k.sb([64, 2, C], BF16), Vtok=k.sb([64, 128], BF16), Vpad=k.sb([64, 2, 128], BF16),
                   ktok=k.sb([64, 128], BF16), Sb=k.sb([128, 128], BF16), t=k.sb([128, 128], F32),
                   sc=k.sb([128, 4], F32)) for _ in range(NR)] for d_ in range(2)]
        for d_ in range(2):
            for r in R[d_]:
                k.op('pool', lambda e, r=r: e.memset(r['Vpad'][:], 0.0), [], [r['Vpad']])
                k.op('pool', lambda e, r=r: e.memset(r['kt'][:], 0.0), [], [r['kt']])
        for b in range(NB):
            for hp in range(2):
                k.dma(q[:], SI['q'][m][0][b][hp], [SI['q']], [q])
                k.dma(v[:], zT[b][(13 if m == 1 else 23) + hp], [zT], [v])
                for d_ in range(2):
                    kd = 0 if m == 1 else d_
                    k.dma(kk_[d_][:], SI['k'][m][kd][b][hp], [SI['k']], [kk_[d_]])
                    if m == 1:
                        col = VC['retdec'] + d_ * 2 + hp
                        k.act(R[d_][0]['sc'][:, 0:1], vecs[:, l, col:col + 1], AF.Sigmoid, [vecs], [R[d_][0]['sc']])
                        k.act(R[d_][0]['sc'][:, 0:1], R[d_][0]['sc'][:, 0:1], AF.Ln, [R[d_][0]['sc']], [R[d_][0]['sc']])
                        k.act(cpv[d_][:], zeros[:], AF.Identity, [zeros, R[d_][0]['sc']], [cpv[d_]],
                              bias=R[d_][0]['sc'][:, 0:1])
                    else:
                        k.dma(cpv[d_][:], SI['w'][m][d_][b][hp], [SI['w']], [cpv[d_]])
                        k.act(cpv[d_][:], cpv[d_][:], AF.Ln, [cpv[d_]], [cpv[d_]])
                    k.op('dve', lambda e, d_=d_: e.tensor_tensor_scan(out=cum[d_][:], data0=cpv[d_][:], data1=zeros[:],
                                                                        initial=0.0, op0=ALU.add, op1=ALU.add),
                         [cpv[d_], zeros], [cum[d_]])
                    k.tt(cpv[d_][:], cum[d_][:], cpv[d_][:], ALU.subtract, [cum[d_], cpv[d_]], [cpv[d_]])
                    k.op('pool', lambda e, d_=d_: e.memset(S[d_][:], 0.0), [], [S[d_]])
                orders = [chunk_order(0), chunk_order(1)]
                import os
                _lim = int(os.environ.get('CM_LIMIT', T // C))
                _stg = float(os.environ.get('CM_STAGE', 99))
                for ci in range(min(_lim, T // C)):
                    for d_ in range(2):
                        a = orders[d_][ci]
                        e_ = a + C - 1
                        r = R[d_][ci % NR]
                        sc = r['sc']
                        sl = slice(a, a + C)
                        mid = a + 31 if d_ == 0 else a + 32
                        if d_ == 0:
                            k.ts(sc[:, 0:1], cum[d_][:, mid:mid + 1], -1.0, ALU.mult, [cum[d_]], [sc])
                            k.ts(sc[:, 1:2], cpv[d_][:, a:a + 1], -1.0, ALU.mult, [cpv[d_]], [sc])
                            k.act(r['G'][:], cum[d_][:, sl], AF.Exp, [cum[d_], sc], [r['G']], bias=sc[:, 0:1], scale=1.0)
                            k.act(r['Gi'][:], cum[d_][:, sl], AF.Exp, [cum[d_]], [r['Gi']],
                                  bias=cum[d_][:, mid:mid + 1], scale=-1.0)
                            k.act(sc[:, 2:3], cum[d_][:, mid:mid + 1], AF.Exp, [cum[d_], sc], [sc], bias=sc[:, 1:2],
                                  scale=1.0)
                            gC = r['G'][:, C - 1:C]
                            mk = cstb[0:64, C_MF:C_MF + 128]
                        else:
                            k.ts(sc[:, 0:1], cpv[d_][:, mid:mid + 1], -1.0, ALU.mult, [cpv[d_]], [sc])
                            k.act(r['G'][:], cpv[d_][:, sl], AF.Exp, [cpv[d_]], [r['G']],
                                  bias=cpv[d_][:, mid:mid + 1], scale=-1.0)
                            k.act(r['Gi'][:], cpv[d_][:, sl], AF.Exp, [cpv[d_], sc], [r['Gi']], bias=sc[:, 0:1], scale=1.0)
                            k.act(sc[:, 2:3], cpv[d_][:, mid:mid + 1], AF.Exp, [cpv[d_], cum[d_]], [sc],
                                  bias=cum[d_][:, e_:e_ + 1], scale=-1.0)
                            gC = r['G'][:, 0:1]
                            mk = cstb[0:64, C_MR:C_MR + 128]
                        if _stg < 1:
                            continue
                        k.tt(r['qt'][:], q[:, sl], r['G'][:], ALU.mult, [q, r['G']], [r['qt']])
                        k.tt(r['kt32'][:], kk_[d_][:, sl], r['Gi'][:], ALU.mult, [kk_[d_], r['Gi']], [r['kt32']], eng='pool')
                        k.cp(r['kt'][:, 0:C], r['kt32'][:], [r['kt32']], [r['kt']], eng='pool')
                        if _stg < 2:
                            continue
                        for h in range(2):
                            hs = slice(64 * h, 64 * h + 64)
                            pA = nps()
                            k.mm(pA[0:64, 0:C], r['kt'][hs, 0:C], r['qt'][hs, :], True, True, [r['kt'], r['qt']], [pA])
                            k.tt(r['AT'][:, h, :], pA[0:64, 0:C], mk[:, 0:C], ALU.mult, [pA, cstb], [r['AT']],
                                 eng=('dve'))
                        if _stg < 3:
                            continue
                        pV = nps()
                        k.op('pe', lambda e, pV=pV, sl=sl: e.transpose(out=pV[0:64, 0:128], in_=v[:, sl], identity=IDf),
                             [v, cst], [pV])
                        k.cp(r['Vtok'][:], pV[0:64, 0:128], [pV], [r['Vtok']], eng='act')
                        vp_out = bass.AP(r['Vpad'].h, 0, [[256, 64], [192, 2], [1, 64]])
                        k.cp(vp_out, pV[0:64, 0:128].rearrange("p (g c) -> p g c", g=2), [pV], [r['Vpad']])
                        if _stg < 4:
                            continue
                        pK = nps()
                        k.op('pe', lambda e, pK=pK, r=r: e.transpose(out=pK[0:64, 0:128], in_=r['kt32'][:], identity=IDf),
                             [r['kt32'], cst], [pK])
                        k.cp(r['ktok'][:], pK[0:64, 0:128], [pK], [r['ktok']], eng='act')
                        if _stg < 5:
                            continue
                        k.ts(r['Sb'][:], S[d_][:], sc[:, 2:3], ALU.mult, [S[d_], sc], [r['Sb']])
                        pY = nps()
                        k.mm(pY[:, 0:C], r['Sb'][:], r['qt'][:], True, True, [r['Sb'], r['qt']], [pY])
                        pY2 = nps()
                        for g in range(2):
                            k.mm(pY2[:, 0:C], r['Vpad'][:, g, :], r['AT'][:, g, :], g == 0, g == 1, [r['Vpad'], r['AT']], [pY2])
                        k.cp(Ych[d_][:, sl], pY[:, 0:C], [pY], [Ych[d_]], eng='act')
                        k.tt(Ych[d_][:, sl], Ych[d_][:, sl], pY2[:, 0:C], ALU.add, [Ych[d_], pY2], [Ych[d_]])
                        pS = nps()
                        k.mm(pS[:, 0:128], r['ktok'][:], r['Vtok'][:], True, True, [r['ktok'], r['Vtok']], [pS])
                        k.tt(r['t'][:], pS[:, 0:128], BO, ALU.mult, [pS, cst], [r['t']])
                        k.op('dve', lambda e, d_=d_, r=r, sc=sc: e.scalar_tensor_tensor(
                            out=S[d_][:], in0=S[d_][:], scalar=sc[:, 2:3], in1=r['t'][:], op0=ALU.mult, op1=ALU.add),
                            [S[d_], sc, r['t']], [S[d_]])
                        k.ts(S[d_][:], S[d_][:], gC, ALU.mult, [S[d_], r['G']], [S[d_]])
                for d_ in range(2):
                    k.dma(YC[m][d_][b][hp], Ych[d_][:], [Ych[d_]], [YC])
        k.barrier()


class ScanM:
    def __init__(self, k, l, m, zT, CST, nps, TBs):
        self.k, self.l, self.m, self.zT, self.nps, self.TB = k, l, m, zT, nps, TBs
        self.cst, self.cstb, self.vecs = CST['cst'], CST['cstb'], CST['vecs']
        self.SI, self.YC = CST['SI'], CST['YC']
        self.delta = (m == 0)
        TB_ = TBs
        self.names = ['q', 'k', 'v'] + (['w'] if m != 1 else []) + (['kk', 'b'] if self.delta else [])
        self.A = {nm: k.sb([128, 2, 4, TB_], F32, name='A%s_%d_%d' % (nm, l, m)) for nm in self.names}
        self.Wc = k.sb([128, 2, 4], F32, name='Wc_%d_%d' % (l, m))
        if m == 1:
            for b in range(NB):
                k.act(self.Wc[:, :, 2 * b:2 * b + 2],
                      self.vecs[:, l, VC['retdec']:VC['retdec'] + 4].rearrange("p (d h) -> p d h", d=2),
                      AF.Sigmoid, [self.vecs], [self.Wc])
        self.S = k.sb([128, 512], F32, name='S_%d_%d' % (l, m))
        k.op('pool', lambda e: e.memset(self.S[:], 0.0), [], [self.S])
        self.Ych = k.sb([64, 2, 8, TB_], F32, name='Ych_%d_%d' % (l, m))
        NR = 2
        self.R = []
        for _ in range(NR):
            d = dict(vd=k.sb([128, 512], BF16), vbs=k.sb([128, 512], F32), kv=k.sb([128, 512], F32),
                     t3=k.sb([128, 512], BF16))
            if self.delta:
                d['t1'] = k.sb([128, 512], BF16)
                d['t2'] = k.sb([128, 512], F32)
            self.R.append(d)

    def st(self, j):
        TB_ = self.TB
        s1 = (NCTX - TB_ * (j + 1)) if TB_ * (j + 1) <= NCTX else (T + NCTX - TB_ * (j + 1))
        return [TB_ * j, s1]

    def src(self, nm, d_, b, st):
        m, SI, zT, TB_ = self.m, self.SI, self.zT, self.TB
        if m == 0:
            dd = d_ if nm in ('k', 'w', 'b') else 0
            return SI[nm], SI[nm][0][dd][b][:, :, st:st + TB_]
        if m == 1:
            if nm == 'v':
                return zT, zT[b][13:15][:, :, st:st + TB_]
            return SI[nm], SI[nm][1][0][b][:, :, st:st + TB_]
        if nm == 'v':
            return zT, zT[b][23:25][:, :, st:st + TB_]
        dd = d_ if nm in ('k', 'w') else 0
        return SI[nm], SI[nm][2][dd][b][:, :, st:st + TB_]

    def load(self, j):
        k = self.k
        st = self.st(j)
        for nm in self.names:
            for d_ in range(2):
                for b in range(NB):
                    sb_, sa_ = self.src(nm, d_, b, st[d_])
                    k.dma(self.A[nm][:, d_, 2 * b:2 * b + 2, :], sa_.rearrange("h p t -> p h t"), [sb_], [self.A[nm]])

    def store(self, j):
        k = self.k
        st = self.st(j)
        for d_ in range(2):
            for b in range(NB):
                for hp in range(2):
                    for h in range(2):
                        k.dma(self.YC[self.m][d_][b][hp][64 * h:64 * h + 64, st[d_]:st[d_] + self.TB],
                              self.Ych[:, d_, (b * 2 + hp) * 2 + h, :], [self.Ych], [self.YC])

    def step(self, s_):
        k, m, TB_, A, S, cst, cstb, nps = self.k, self.m, self.TB, self.A, self.S, self.cst, self.cstb, self.nps
        BOb = cstb[:, C_BO:C_BO + 128]
        c0, c1 = s_, TB_ - 1 - s_
        r = self.R[s_ % len(self.R)]

        def v4(ap):
            return ap.rearrange("p (d c v) -> p d c v", d=2, c=4)

        def step(nm):
            if nm == 'w' and m == 1:
                return sap(self.Wc, 0, [[4, 2], [1, 4], [0, 64]]), self.Wc
            return sap(A[nm], c0, [[4 * TB_ + c1 - c0, 2], [TB_, 4], [0, 64]]), A[nm]
        S4 = v4(S[:])
        Dm4 = sap(cst, C_DM, [[0, 2], [0, 4], [1, 64]])
        Dm3 = sap(cst, C_DM, [[0, 8], [1, 64]])
        a_, ab = step('v')
        k.tt(v4(r['vd'][:]), a_, Dm4, ALU.mult, [ab, cst], [r['vd']], eng='pool')
        pA = nps()
        k.mm(pA[:], BOb, r['vd'][:], True, True, [cstb, r['vd']], [pA])
        k.cp(r['vbs'][:], pA[:], [pA], [r['vbs']], eng='act')
        a_, ab = step('k')
        k.tt(v4(r['kv'][:]), v4(r['vbs'][:]), a_, ALU.mult, [r['vbs'], ab], [r['kv']], eng='pool')
        if self.delta:
            a_, ab = step('kk')
            k.tt(v4(r['t1'][:]), S4, a_, ALU.mult, [S, ab], [r['t1']])
            pB = nps()
            k.mm(pB[:], BOb, r['t1'][:], True, True, [cstb, r['t1']], [pB])
        a_, ab = step('w')
        k.tt(S4, S4, a_, ALU.mult, [S, ab], [S])
        if self.delta:
            a_, ab = step('b')
            k.tt(v4(r['t2'][:]), v4(pB[:]), a_, ALU.mult, [pB, ab], [r['t2']])
            k.tt(S[:], S[:], r['t2'][:], ALU.subtract, [S, r['t2']], [S])
        k.tt(S[:], S[:], r['kv'][:], ALU.add, [S, r['kv']], [S])
        a_, ab = step('q')
        k.tt(v4(r['t3'][:]), S4, a_, ALU.mult, [S, ab], [r['t3']])
        pC = nps()
        hm = sap(cstb, C_BO, [[64, 2]])
        for c in range(8):
            k.mm(pC[0:64, 2 * c:2 * c + 2], r['t3'][:, c * 64:(c + 1) * 64], hm, True, True, [r['t3'], cstb], [pC])
        yo = bass.AP(self.Ych.h, c0, [[2 * 8 * TB_, 64], [8 * TB_ + c1 - c0, 2], [TB_, 8]])
        k.cp(yo, pC[0:64, 0:16].rearrange("p (d c) -> p d c", d=2), [pC], [self.Ych], eng='act')


def scan_all(k, l, zT, CST, nps, mixers=(0, 1, 2)):
    TBs = 128
    ph = ExitStack()
    with ph:
        k.es = ph
        Ms = [ScanM(k, l, m, zT, CST, nps, TBs) for m in mixers]
        for j in range(T // TBs):
            for M_ in Ms:
                M_.load(j)
            for s_ in range(TBs):
                for M_ in Ms:
                    M_.step(s_)
            for M_ in Ms:
                M_.store(j)
        k.barrier()


def post_mixers(k, l, zT, CST, nps):
    cst, vecs = CST['cst'], CST['vecs']
    YC, BR, AUX = CST['YC'], CST['BR'], CST['AUX']
    BO = cst[:, C_BO:C_BO + 128]
    ph = ExitStack()
    with ph:
        k.es = ph
        NT = 10
        tm = [k.sb([128, 512], F32, name='pm%d_%d' % (l, i)) for i in range(NT)]
        ti = [0]

        def tmp():
            ti[0] = (ti[0] + 1) % NT
            return tm[ti[0]]

        def ld(buf, ap, n):
            t_ = tmp()
            k.dma(t_[:, :n], ap, [buf], [t_])
            return t_
        blocks = tok_blocks() if l < L - 1 else tok_blocks()[1:]
        for m in range(3):
            for b in range(NB):
                for hp in range(2):
                    for (t0, n, isctx) in blocks:
                        y0 = ld(YC, YC[m][0][b][hp][:, t0:t0 + n], n)
                        y1 = ld(YC, YC[m][1][b][hp][:, t0:t0 + n], n)
                        y = tmp()
                        k.tt(y[:, :n], y0[:, :n], y1[:, :n], ALU.add, [y0, y1], [y])
                        if m < 2:
                            p = nps()
                            k.mm(p[:, :n], BO, y[:, :n], True, True, [cst, y], [p])
                            mean = tmp()
                            k.ts(mean[:, :n], p[:, :n], 1.0 / 64, ALU.mult, [p], [mean])
                            yc = tmp()
                            k.tt(yc[:, :n], y[:, :n], mean[:, :n], ALU.subtract, [y, mean], [yc])
                            eps = 64e-5 if m == 0 else 1e-5
                        else:
                            yc = y
                            eps = 1e-6
                        sq = tmp()
                        k.tt(sq[:, :n], yc[:, :n], yc[:, :n], ALU.mult, [yc], [sq])
                        p = nps()
                        k.mm(p[:, :n], BO, sq[:, :n], True, True, [cst, sq], [p])
                        rstd = sq
                        k.ts(rstd[:, :n], p[:, :n], 1.0 / 64, ALU.mult, [p], [rstd], s2=eps, op1=ALU.add)
                        k.act(rstd[:, :n], rstd[:, :n], AF.Ln, [rstd], [rstd])
                        k.act(rstd[:, :n], rstd[:, :n], AF.Exp, [rstd], [rstd], scale=-0.5)
                        o = tmp()
                        k.tt(o[:, :n], yc[:, :n], rstd[:, :n], ALU.mult, [yc, rstd], [o])
                        gn = ('lng', 'retg', 'hng')[m]
                        gcol = vecs[:, l, VC[gn] + hp:VC[gn] + hp + 1]
                        if m < 2:
                            bn = ('lnb', 'retb')[m]
                            bcol = vecs[:, l, VC[bn] + hp:VC[bn] + hp + 1]
                            k.act(o[:, :n], o[:, :n], AF.Identity, [o, vecs], [o], bias=bcol, scale=gcol)
                        else:
                            k.ts(o[:, :n], o[:, :n], gcol, ALU.mult, [o, vecs], [o])
                        if m == 0:
                            bon = ld(AUX, AUX[0][b][hp][:, t0:t0 + n], n)
                            gg = ld(AUX, AUX[1][b][hp][:, t0:t0 + n], n)
                            k.tt(o[:, :n], o[:, :n], bon[:, :n], ALU.add, [o, bon], [o])
                            k.tt(o[:, :n], o[:, :n], gg[:, :n], ALU.mult, [o, gg], [o])
                        else:
                            zc = 15 + hp if m == 1 else 25 + hp
                            g_ = ld(zT, zT[b][zc][:, t0:t0 + n], n)
                            k.act(g_[:, :n], g_[:, :n], AF.Silu, [g_], [g_])
                            k.tt(o[:, :n], o[:, :n], g_[:, :n], ALU.mult, [o, g_], [o])
                        k.dma(BR[m][b][hp][:, t0:t0 + n], o[:, :n], [o], [BR])
        k.barrier()


def attn_phase(k, l, zT, CST, nps):
    cst, cstb, vecs, rope = CST['cst'], CST['cstb'], CST['vecs'], CST['rope']
    BR, KA = CST['BR'], CST['KA']
    BO = cst[:, C_BO:C_BO + 128]
    IDf = cst[:, C_ID:C_ID + 128]
    IDb = cstb[:, C_ID:C_ID + 128]
    ph = ExitStack()
    with ph:
        k.es = ph
        NT = 8
        tm = [k.sb([128, 512], F32, name='at%d_%d' % (l, i)) for i in range(NT)]
        ti = [0]

        def tmp():
            ti[0] = (ti[0] + 1) % NT
            return tm[ti[0]]
        qb = k.sb([128, 2, T], BF16, name='qb%d' % l)
        kd32 = k.sb([128, 2, T], F32, name='kd32_%d' % l)
        kdb = k.sb([128, 2, T], BF16, name='kdb%d' % l)
        vpad = k.sb([128, 18, 2, 2, 128], BF16, name='vpad%d' % l)
        Sf = k.sb([128, T], F32, name='Sf%d' % l)
        Pb = k.sb([128, T], BF16, name='Pb%d' % l)
        PT = k.sb([128, 18, 128], BF16, name='PT%d' % l)
        st_ = k.sb([128, 4], F32, name='ast%d' % l)
        ob = [k.sb([128, 128], F32, name='aob%d_%d' % (l, i)) for i in range(2)]
        k.op('pool', lambda e: e.memset(vpad[:], 0.0), [], [vpad])
        for b in range(NB):
            for (t0, n, isctx) in tok_blocks():
                for which in range(3):
                    zc = 27 + which
                    zt_ = tmp()
                    k.dma(zt_[:, :n], zT[b][zc][:, t0:t0 + n], [zT], [zt_])
                    sq = tmp()
                    k.tt(sq[:, :n], zt_[:, :n], zt_[:, :n], ALU.mult, [zt_], [sq])
                    p = nps()
                    k.mm(p[:, :n], BO, sq[:, :n], True, True, [cst, sq], [p])
                    k.ts(sq[:, :n], p[:, :n], 1.0 / 64, ALU.mult, [p], [sq], s2=1e-6, op1=ALU.add)
                    k.act(sq[:, :n], sq[:, :n], AF.Ln, [sq], [sq])
                    k.act(sq[:, :n], sq[:, :n], AF.Exp, [sq], [sq], scale=-0.5)
                    gname = 'aqg' if which < 2 else 'akg'
                    qn = tmp()
                    k.op('dve', lambda e, qn=qn, zt_=zt_, sq=sq, gname=gname: e.scalar_tensor_tensor(
                        out=qn[:, :n], in0=zt_[:, :n], scalar=vecs[:, l, VC[gname]:VC[gname] + 1], in1=sq[:, :n],
                        op0=ALU.mult, op1=ALU.mult), [zt_, sq, vecs], [qn])
                    if not isctx:
                        tl = t0 - NCTX
                        p = nps()
                        k.mm(p[:, :n], cst[:, C_ROT:C_ROT + 128], qn[:, :n], True, True, [cst, qn], [p])
                        o2 = tmp()
                        k.tt(o2[:, :n], p[:, :n], rope[:, 1, tl:tl + n], ALU.mult, [p, rope], [o2])
                        k.tt(qn[:, :n], qn[:, :n], rope[:, 0, tl:tl + n], ALU.mult, [qn, rope], [qn])
                        k.tt(qn[:, :n], qn[:, :n], o2[:, :n], ALU.add, [qn, o2], [qn])
                    if which < 2:
                        k.ts(qb[:, which, t0:t0 + n], qn[:, :n], 0.125, ALU.mult, [qn], [qb])
                    else:
                        k.dma(KA[b][:, t0:t0 + n], qn[:, :n], [qn], [KA])
            for kv in range(2):
                for half in range(2):
                    k.dma(kd32[64 * half:64 * half + 64, kv, :], KA[b][64 * kv:64 * kv + 64, :], [KA], [kd32])
            k.cp(kdb[:], kd32[:], [kd32], [kdb], eng='pool')
            for kt in range(18):
                vt = tmp()
                k.dma(vt[:, :128], zT[b][30][:, kt * 128:(kt + 1) * 128], [zT], [vt])
                p = nps()
                k.op('pe', lambda e, p=p, vt=vt: e.transpose(out=p[:, :128], in_=vt[:, :128], identity=IDf),
                     [vt, cst], [p])
                for kv in range(2):
                    for g in range(2):
                        k.cp(vpad[:, kt, kv, g, 64 * g:64 * g + 64], p[:, 64 * kv:64 * kv + 64], [p], [vpad],
                             eng=('act' if g else 'dve'))
            qtiles = [(NCTX + i * 128, T) for i in range(NLAT // 128)]
            if l < L - 1:
                qtiles = [(0, NCTX), (128, NCTX)] + qtiles
            for hp in range(2):
                for (q0, nk) in qtiles:
                    nkt = nk // 128
                    pO = CST['PS0']
                    for g in range(2):
                        off = 64 * g
                        for kb0 in range(0, nk, 512):
                            kn = min(512, nk - kb0)
                            p = nps()
                            k.mm(p[:, :kn], qb[off:off + 64, hp, q0:q0 + 128], kdb[off:off + 64, hp, kb0:kb0 + kn],
                                 True, True, [qb, kdb], [p])
                            k.cp(Sf[:, kb0:kb0 + kn], p[:, :kn], [p], [Sf], eng='act')
                        k.op('dve', lambda e: e.tensor_reduce(out=st_[:, 0:1], in_=Sf[:, :nk], axis=AX.X, op=ALU.max),
                             [Sf], [st_])
                        k.ts(st_[:, 1:2], st_[:, 0:1], -1.0, ALU.mult, [st_], [st_])
                        k.act(Pb[:, :nk], Sf[:, :nk], AF.Exp, [Sf, st_], [Pb, st_], bias=st_[:, 1:2],
                              accum_out=st_[:, 2:3])
                        k.op('dve', lambda e: e.reciprocal(out=st_[:, 3:4], in_=st_[:, 2:3]), [st_], [st_])
                        k.ts(Pb[:, :nk], Pb[:, :nk], st_[:, 3:4], ALU.mult, [Pb, st_], [Pb])
                        for k4 in range(0, nkt, 4):
                            nn = min(4, nkt - k4)
                            p = nps()
                            pb_ = p.h[:].bitcast(BF16)
                            for j in range(nn):
                                k.op('pe', lambda e, j=j, pb_=pb_, k4=k4: e.transpose(
                                    out=pb_[:, j * 128:(j + 1) * 128], in_=Pb[:, (k4 + j) * 128:(k4 + j + 1) * 128],
                                    identity=IDb), [Pb, cstb], [p])
                            k.cp(PT[:, k4:k4 + nn, :].rearrange("p a b -> p (a b)"), pb_[:, :nn * 128], [p], [PT],
                                 eng=('act' if (k4 // 4) % 2 else 'dve'))
                        for kt in range(nkt):
                            k.mm(pO[:, :128], vpad[:, kt, hp, g, :], PT[:, kt, :], (g == 0 and kt == 0),
                                 (g == 1 and kt == nkt - 1), [vpad, PT], [pO])
                    o = ob[(q0 // 128) % 2]
                    k.cp(o[:], pO[:, :128], [pO], [o], eng='act')
                    k.dma(BR[3][b][hp][:, q0:q0 + 128], o[:], [o], [BR])
        k.barrier()


def merge_phase(k, l, xT_cur, x1T, modT, IN, CST, nps):
    cst, vecs = CST['cst'], CST['vecs']
    BR = CST['BR']
    ONE = cst[:, C_ONE:C_ONE + 128]
    ph = ExitStack()
    with ph:
        k.es = ph
        wg = k.sb([128, 4, 8, 1024], BF16, name='wg%d' % l)
        wb = k.sb([128, 4, 2, 1024], BF16, name='wb%d' % l)
        wo = k.sb([128, 8, 1024], BF16, name='wo%d' % l)
        stg = [k.sb([128, 1024], F32, name='mstg%d_%d' % (l, i)) for i in range(2)]
        si = 0
        for i in range(4):
            for hh in range(8):
                w = stg[si % 2]
                si += 1
                wv_ = w[:].rearrange("p (k n) -> p k n", k=8)
                k.dma(wv_, IN['w_gate'][l][i].rearrange("(k p) n -> p k n", p=128)[:, :, hh * 128:(hh + 1) * 128],
                      [IN['w_gate']], [w])
                k.cp(wg[:, i, :, hh * 128:(hh + 1) * 128], wv_, [w], [wg], eng='pool')
            for hh in range(2):
                w = stg[si % 2]
                si += 1
                wv_ = w[:].rearrange("p (k n) -> p k n", k=2)
                k.dma(wv_, IN['w_branch'][l][i].rearrange("(k p) n -> p k n", p=128)[:, :, hh * 512:(hh + 1) * 512],
                      [IN['w_branch']], [w])
                k.cp(wb[:, i, :, hh * 512:(hh + 1) * 512], wv_, [w], [wb], eng='pool')
        for hh in range(8):
            w = stg[si % 2]
            si += 1
            wv_ = w[:].rearrange("p (k n) -> p k n", k=8)
            k.dma(wv_, IN['w_out'][l].rearrange("(k p) n -> p k n", p=128)[:, :, hh * 128:(hh + 1) * 128],
                  [IN['w_out']], [w])
            k.cp(wo[:, :, hh * 128:(hh + 1) * 128], wv_, [w], [wo], eng='pool')
        xt = k.sb([128, 8, 512], F32, name='mxt%d' % l)
        ut = k.sb([128, 8, 512], BF16, name='mut%d' % l)
        br32 = [k.sb([128, 2, 512], F32, name='mbr32_%d_%d' % (l, i)) for i in range(1)]
        brb = k.sb([128, 4, 2, 512], BF16, name='mbrb_%d' % l)
        mg = k.sb([128, 8, 512], BF16, name='mmg%d' % l)
        acc = k.sb([128, 512], F32, name='macc%d' % l)
        sg = [k.sb([128, 512], F32, name='msg%d_%d' % (l, i)) for i in range(2)]
        y = k.sb([128, 8, 512], F32, name='my%d' % l)
        ysq = [k.sb([128, 512], F32, name='mysq%d_%d' % (l, i)) for i in range(2)]
        mean = k.sb([128, 512], F32, name='mmean%d' % l)
        rstd = k.sb([128, 512], F32, name='mrstd%d' % l)
        xo = [k.sb([128, 512], F32, name='mxo%d_%d' % (l, i)) for i in range(2)]
        blocks = tok_blocks() if l < L - 1 else tok_blocks()[1:]
        for b in range(NB):
            for (t0, n, isctx) in blocks:
                r = 2 if isctx else b
                k.dma(xt[:, :, :n], xT_cur[b][:, :, t0:t0 + n].rearrange("k p t -> p k t"), [xT_cur], [xt])
                for kk in range(8):
                    k.act(ut[:, kk, :n], xt[:, kk, :n], AF.Identity, [xt, modT], [ut],
                          bias=modT[:, kk, r:r + 1], scale=modT[:, 8 + kk, r:r + 1])
                k.ts(xt[:, :, :n], xt[:, :, :n], ALPHA, ALU.mult, [xt], [xt], eng='pool')
                for i in range(4):
                    b32 = br32[0]
                    k.dma(b32[:, :, :n], BR[i][b][:, :, t0:t0 + n].rearrange("h p t -> p h t"), [BR], [b32])
                    k.cp(brb[:, i, :, :n], b32[:, :, :n], [b32], [brb], eng='pool')
                for mo in range(8):
                    ms = slice(mo * 128, (mo + 1) * 128)
                    for i in range(4):
                        pG = nps()
                        for kk in range(8):
                            k.mm(pG[:, :n], wg[:, i, kk, ms], ut[:, kk, :n], kk == 0, kk == 7, [wg, ut], [pG])
                        pB = nps()
                        for hh in range(2):
                            k.mm(pB[:, :n], wb[:, i, hh, ms], brb[:, i, hh, :n], hh == 0, hh == 1, [wb, brb], [pB])
                        s_ = sg[i % 2]
                        k.act(s_[:, :n], pG[:, :n], AF.Sigmoid, [pG], [s_])
                        if i == 0:
                            k.tt(acc[:, :n], s_[:, :n], pB[:, :n], ALU.mult, [s_, pB], [acc])
                        else:
                            k.tt(s_[:, :n], s_[:, :n], pB[:, :n], ALU.mult, [s_, pB], [s_])
                            if i < 3:
                                k.tt(acc[:, :n], acc[:, :n], s_[:, :n], ALU.add, [acc, s_], [acc])
                            else:
                                k.tt(mg[:, mo, :n], acc[:, :n], s_[:, :n], ALU.add, [acc, s_], [mg])
                for mo in range(8):
                    ms = slice(mo * 128, (mo + 1) * 128)
                    p = nps()
                    for kk in range(8):
                        k.mm(p[:, :n], wo[:, kk, ms], mg[:, kk, :n], kk == 0, kk == 7, [wo, mg], [p])
                    k.op('dve', lambda e, p=p, mo=mo: e.scalar_tensor_tensor(
                        out=y[:, mo, :n], in0=p[:, :n], scalar=modT[:, 16 + mo, r:r + 1], in1=xt[:, mo, :n],
                        op0=ALU.mult, op1=ALU.add), [p, modT, xt], [y])
                layer_norm_cm(k, y, ysq, mean, rstd, n, vecs[:, l, VC['ln1g']:VC['ln1g'] + 8],
                              vecs[:, l, VC['ln1b']:VC['ln1b'] + 8], vecs, ONE, cst, nps, xo,
                              lambda mo, b=b, t0=t0, n=n: x1T[b][mo][:, t0:t0 + n], x1T)
        k.barrier()


def layer_norm_cm(k, y, ysq, mean, rstd, n, gcols, bcols, vecs, ONE, cst, nps, xo, dst_fn, dstbuf):
    p1 = nps()
    for mo in range(8):
        k.mm(p1[:, :n], ONE, y[:, mo, :n], mo == 0, mo == 7, [cst, y], [p1])
    p2 = nps()
    for mo in range(8):
        q_ = ysq[mo % 2]
        k.tt(q_[:, :n], y[:, mo, :n], y[:, mo, :n], ALU.mult, [y], [q_], eng='pool')
        k.mm(p2[:, :n], ONE, q_[:, :n], mo == 0, mo == 7, [cst, q_], [p2])
    k.ts(mean[:, :n], p1[:, :n], 1.0 / D, ALU.mult, [p1], [mean])
    k.tt(rstd[:, :n], mean[:, :n], mean[:, :n], ALU.mult, [mean], [rstd])
    k.op('dve', lambda e: e.scalar_tensor_tensor(out=rstd[:, :n], in0=p2[:, :n], scalar=1.0 / D, in1=rstd[:, :n],
                                                 op0=ALU.mult, op1=ALU.subtract), [p2, rstd], [rstd])
    k.ts(rstd[:, :n], rstd[:, :n], 1e-5, ALU.add, [rstd], [rstd])
    k.act(rstd[:, :n], rstd[:, :n], AF.Ln, [rstd], [rstd])
    k.act(rstd[:, :n], rstd[:, :n], AF.Exp, [rstd], [rstd], scale=-0.5)
    for mo in range(8):
        o = xo[mo % 2]
        k.tt(o[:, :n], y[:, mo, :n], mean[:, :n], ALU.subtract, [y, mean], [o])
        k.tt(o[:, :n], o[:, :n], rstd[:, :n], ALU.mult, [o, rstd], [o])
        k.act(o[:, :n], o[:, :n], AF.Identity, [o, vecs], [o], bias=bcols[:, mo:mo + 1], scale=gcols[:, mo:mo + 1])
        k.dma(dst_fn(mo), o[:, :n], [o], [dstbuf])


def moe_phase(k, l, x1T, dstT, dst_off, modT, IN, CST, nps):
    cst, vecs = CST['cst'], CST['vecs']
    ONE = cst[:, C_ONE:C_ONE + 128]
    IDf = cst[:, C_ID:C_ID + 128]
    ph = ExitStack()
    with ph:
        k.es = ph
        selm = k.sb([16, 2048], F32, name='selm%d' % l)
        k.dma(selm[:], IN['selm'][:], [IN['selm']], [selm])
        wr = k.sb([128, 8, 16], F32, name='wr%d' % l)
        k.dma(wr[:], IN['w_router'][l].rearrange("(k p) e -> p k e", p=128), [IN['w_router']], [wr])
        W = [k.sb([128, 8, 1024], BF16, name='wexp%d_%d' % (l, i)) for i in range(3)]
        stg = [k.sb([128, 1024], F32, name='estg%d_%d' % (l, i)) for i in range(2)]
        acc = k.sb([128, 8, 1024], F32, name='eacc%d' % l)
        u2b = k.sb([128, 8, 1024], BF16, name='eu2b%d' % l)
        hT = k.sb([128, 8, 512], BF16, name='ehT%d' % l)
        affT = k.sb([16, 2048], F32, name='eaff%d' % l)
        work = k.sb([16, 2048], F32, name='ework%d' % l)
        m8 = k.sb([16, 8], F32, name='em8%d' % l)
        x1t = k.sb([128, 8, 512], F32, name='ex1t%d' % l)
        u2f = k.sb([128, 8, 128], F32, name='eu2f%d' % l)
        lg = k.sb([128, 16], F32, name='elg%d' % l)
        st_ = k.sb([128, 4], F32, name='est%d' % l)
        gb = k.sb([128, 512], F32, name='egb%d' % l)
        sl = [k.sb([128, 512], F32, name='esl%d_%d' % (l, i)) for i in range(2)]
        y = x1t
        ysq = [k.sb([128, 512], F32, name='eysq%d_%d' % (l, i)) for i in range(2)]
        mean = k.sb([128, 512], F32, name='emean%d' % l)
        rstd = k.sb([128, 512], F32, name='erstd%d' % l)
        xo = [k.sb([128, 512], F32, name='exo%d_%d' % (l, i)) for i in range(2)]
        si = [0]
        segs = [(NCTX, NLAT, None)] + ([(0, NCTX, 2)] if l < L - 1 else [])
        for b in range(NB):
            for (s0, ntok, rr) in segs:
                r = b if rr is None else rr
                cap = 2 * ntok // 16
                for tt_ in range(ntok // 128):
                    t0 = s0 + tt_ * 128
                    k.dma(x1t[:, :, :128], x1T[b][:, :, t0:t0 + 128].rearrange("k p t -> p k t"), [x1T], [x1t])
                    for kk in range(8):
                        k.act(u2f[:, kk, :], x1t[:, kk, :128], AF.Identity, [x1t, modT], [u2f],
                              bias=modT[:, 24 + kk, r:r + 1], scale=modT[:, 32 + kk, r:r + 1])
                    p = nps()
                    for kk in range(8):
                        k.mm(p[:, :16], u2f[:, kk, :], wr[:, kk, :], kk == 0, kk == 7, [u2f, wr], [p])
                    k.op('dve', lambda e, p=p: e.tensor_reduce(out=st_[:, 0:1], in_=p[:, :16], axis=AX.X, op=ALU.max),
                         [p], [st_])
                    k.ts(st_[:, 1:2], st_[:, 0:1], -1.0, ALU.mult, [st_], [st_])
                    k.act(lg[:], p[:, :16], AF.Exp, [p, st_], [lg, st_], bias=st_[:, 1:2], accum_out=st_[:, 2:3])
                    k.op('dve', lambda e: e.reciprocal(out=st_[:, 3:4], in_=st_[:, 2:3]), [st_], [st_])
                    k.ts(lg[:], lg[:], st_[:, 3:4], ALU.mult, [lg, st_], [lg])
                    p = nps()
                    k.op('pe', lambda e, p=p: e.transpose(out=p[0:16, :128], in_=lg[:], identity=IDf), [lg, cst], [p])
                    k.cp(affT[:, tt_ * 128:(tt_ + 1) * 128], p[0:16, :128], [p], [affT], eng='act')
                k.cp(work[:, :ntok], affT[:, :ntok], [affT], [work])
                nit = cap // 8
                for it in range(nit):
                    k.op('dve', lambda e: e.max(out=m8[:], in_=work[:, :ntok]), [work], [m8])
                    if it < nit - 1:
                        k.op('dve', lambda e: e.match_replace(out=work[:, :ntok], in_to_replace=m8[:],
                                                              in_values=work[:, :ntok], imm_value=-1.0),
                             [work, m8], [work])
                k.ts(work[:, :ntok], affT[:, :ntok], m8[:, 7:8], ALU.is_ge, [affT, m8], [work])
                k.tt(work[:, :ntok], work[:, :ntok], affT[:, :ntok], ALU.mult, [work, affT], [work])
                for c0 in range(0, ntok, 1024):
                    cn = min(1024, ntok - c0)
                    for bb0 in range(0, cn, 512):
                        bn = min(512, cn - bb0)
                        t0 = s0 + c0 + bb0
                        k.dma(x1t[:, :, :bn], x1T[b][:, :, t0:t0 + bn].rearrange("k p t -> p k t"), [x1T], [x1t])
                        for kk in range(8):
                            k.act(u2b[:, kk, bb0:bb0 + bn], x1t[:, kk, :bn], AF.Identity, [x1t, modT], [u2b],
                                  bias=modT[:, 24 + kk, r:r + 1], scale=modT[:, 32 + kk, r:r + 1])
                    for e_ in range(16):
                        for wi, wname in enumerate(('w_e1', 'w_e3', 'w_e2')):
                            wv = IN[wname][l][e_].rearrange("(k p) n -> p k n", p=128)
                            for hh in range(8):
                                w = stg[si[0] % 2]
                                si[0] += 1
                                wv_ = w[:].rearrange("p (k n) -> p k n", k=8)
                                k.dma(wv_, wv[:, :, hh * 128:(hh + 1) * 128], [IN[wname]], [w])
                                k.cp(W[wi][:, :, hh * 128:(hh + 1) * 128], wv_, [w], [W[wi]],
                                     eng=('pool' if hh % 2 else 'act'))
                        for bb0 in range(0, cn, 512):
                            bn = min(512, cn - bb0)
                            p = nps()
                            k.mm(p[:, :bn], selm[0:16, e_ * 128:(e_ + 1) * 128], work[0:16, c0 + bb0:c0 + bb0 + bn],
                                 True, True, [selm, work], [p])
                            k.cp(gb[:, :bn], p[:, :bn], [p], [gb], eng='act')
                            for f in range(8):
                                fs = slice(f * 128, (f + 1) * 128)
                                p1 = nps()
                                for kk in range(8):
                                    k.mm(p1[:, :bn], W[0][:, kk, fs], u2b[:, kk, bb0:bb0 + bn], kk == 0, kk == 7,
                                         [W[0], u2b], [p1])
                                p3 = nps()
                                for kk in range(8):
                                    k.mm(p3[:, :bn], W[1][:, kk, fs], u2b[:, kk, bb0:bb0 + bn], kk == 0, kk == 7,
                                         [W[1], u2b], [p3])
                                s_ = sl[f % 2]
                                k.act(s_[:, :bn], p1[:, :bn], AF.Silu, [p1], [s_])
                                k.tt(s_[:, :bn], s_[:, :bn], p3[:, :bn], ALU.mult, [s_, p3], [s_])
                                k.tt(hT[:, f, :bn], s_[:, :bn], gb[:, :bn], ALU.mult, [s_, gb], [hT], eng='pool')
                            for dch in range(8):
                                py = nps()
                                for f in range(8):
                                    k.mm(py[:, :bn], W[2][:, f, dch * 128:(dch + 1) * 128], hT[:, f, :bn], f == 0, f == 7,
                                         [W[2], hT], [py])
                                if e_ == 0:
                                    k.cp(acc[:, dch, bb0:bb0 + bn], py[:, :bn], [py], [acc], eng='act')
                                else:
                                    k.tt(acc[:, dch, bb0:bb0 + bn], acc[:, dch, bb0:bb0 + bn], py[:, :bn], ALU.add,
                                         [acc, py], [acc])
                    for bb0 in range(0, cn, 512):
                        bn = min(512, cn - bb0)
                        t0 = s0 + c0 + bb0
                        k.dma(x1t[:, :, :bn], x1T[b][:, :, t0:t0 + bn].rearrange("k p t -> p k t"), [x1T], [x1t])
                        k.ts(x1t[:, :, :bn], x1t[:, :, :bn], ALPHA, ALU.mult, [x1t], [x1t], eng='pool')
                        for mo in range(8):
                            k.op('dve', lambda e, mo=mo: e.scalar_tensor_tensor(
                                out=y[:, mo, :bn], in0=acc[:, mo, bb0:bb0 + bn], scalar=modT[:, 40 + mo, r:r + 1],
                                in1=x1t[:, mo, :bn], op0=ALU.mult, op1=ALU.add), [acc, modT, x1t], [x1t])
                        layer_norm_cm(k, y, ysq, mean, rstd, bn, vecs[:, l, VC['ln2g']:VC['ln2g'] + 8],
                                      vecs[:, l, VC['ln2b']:VC['ln2b'] + 8], vecs, ONE, cst, nps, xo,
                                      lambda mo, b=b, t0=t0, bn=bn: dstT[b][mo][:, t0 - dst_off:t0 - dst_off + bn], dstT)
        k.barrier()


def tok_blocks():
    bl = [(0, NCTX, True)]
    for i in range(NLAT // 512):
        bl.append((NCTX + i * 512, 512, False))
    return bl


def build(stop_after=None, dbg=None):
    nc = bass.Bass("TRN2", target_bir_lowering=False)
    es = ExitStack()
    with es:
        k = KB(nc, es)
        IN = {}

        def inp(name, shape, dt=F32):
            IN[name] = k.dram(name, shape, dt, kind='ExternalInput')
            return IN[name]
        xT0 = inp('xT0', [NB, 8, 128, T])
        cT = inp('cT', [128, 8, 3])
        w_mod = inp('w_mod', [L, D, 6 * D])
        b_modT = inp('b_modT', [L, 128, 48])
        w_in = inp('w_in', [L, D, DIN])
        out = k.dram('out', [NB, 8, 128, NLAT], F32, kind='ExternalOutput')
        dbg_out = None
        if dbg is not None:
            dbg_out = k.dram('dbg', dbg, F32, kind='ExternalOutput')

        zT = k.dram('zT', [NB, NZC, 128, T])
        PS = [k.ps([128, 512], F32, name='psb%d' % i) for i in range(8)]
        psi = [0]

        def nps():
            psi[0] = psi[0] % 7 + 1
            return PS[psi[0]]

        CST = {'PS0': PS[0]}
        for nm_, shp in (('vecs', [128, L, NVC]), ('cst', [128, NCST]), ('rope', [128, 2, NLAT])):
            inp(nm_, shp)
            CST[nm_] = k.sb(shp, F32, name=nm_ + '_sb')
            k.dma(CST[nm_][:], IN[nm_][:], [IN[nm_]], [CST[nm_]])
        CST['cstb'] = k.sb([128, NCST], BF16, name='cstb')
        k.cp(CST['cstb'][:], CST['cst'][:], [CST['cst']], [CST['cstb']])
        inp('rwkv_w2', [L, 2, 64, 256])
        inp('rwkv_a2', [L, 2, 64, 256])
        inp('rwkv_g2', [L, 128, 256])
        CST['SI'] = {nm_: k.dram('SI_' + nm_, [3, 2, NB, 2, 128, T]) for nm_ in ('q', 'k', 'v', 'w', 'kk', 'b')}
        CST['YC'] = k.dram('YC', [3, 2, NB, 2, 128, T])
        CST['BR'] = k.dram('BR', [4, NB, 2, 128, T])
        CST['AUX'] = k.dram('AUX', [2, NB, 2, 128, T])
        CST['KA'] = k.dram('KA', [NB, 128, T])
        inp('w_gate', [L, 4, D, D])
        inp('w_branch', [L, 4, 256, D])
        inp('w_out', [L, D, D])
        x1T = k.dram('x1T', [NB, 8, 128, T])
        x2T = k.dram('x2T', [NB, 8, 128, T])
        inp('selm', [16, 2048])
        inp('w_router', [L, D, 16])
        inp('w_e1', [L, 16, D, D])
        inp('w_e3', [L, 16, D, D])
        inp('w_e2', [L, 16, D, D])
        modT = k.sb([128, 48, 3], F32, name='modT')
        silu_c = k.sb([128, 8, 3], F32, name='silu_c')
        bmod = k.sb([128, 48], F32, name='bmod')
        ctile = k.sb([128, 8, 3], F32, name='ctile')
        k.dma(ctile[:], cT[:], reads=[cT], writes=[ctile])
        k.act(silu_c[:], ctile[:], AF.Silu, [ctile], [silu_c])

        xT_cur = xT0
        for l in range(L):
            with ExitStack() as ph:
                k.es = ph
                k.dma(bmod[:], b_modT[l], reads=[b_modT], writes=[bmod])
                wst = [k.sb([128, 8, 768], F32, name='wmod_st%d_%d' % (l, i)) for i in range(2)]
                wv = w_mod.h.ap()[l].rearrange("(k p) n -> p k n", p=128)
                for jb in range(8):
                    w = wst[jb % 2]
                    k.dma(w[:], wv[:, :, jb * 768:(jb + 1) * 768], reads=[w_mod], writes=[w])
                    for jj in range(6):
                        j = jb * 6 + jj
                        p = nps()
                        for kk in range(8):
                            k.mm(p[:, 0:3], w[:, kk, jj * 128:(jj + 1) * 128], silu_c[:, kk, :], kk == 0, kk == 7,
                                 [w, silu_c], [p])
                        k.ts(modT[:, j, :], p[:, 0:3], bmod[:, j:j + 1], ALU.add, [p, bmod], [modT])
                for lo_ in (8, 32):
                    k.ts(modT[:, lo_:lo_ + 8, :], modT[:, lo_:lo_ + 8, :], 1.0, ALU.add, [modT], [modT])
                k.barrier()
            if stop_after == 'A':
                k.es = es
                t = k.sb([128, 144], F32, name='dbgt')
                k.cp(t[:], modT[:].rearrange("p a b -> p (a b)"), [modT], [t])
                k.dma(dbg_out[:], t[:], reads=[t], writes=[dbg_out])
                k.final_wait([dbg_out])
                return nc
            with ExitStack() as ph:
                k.es = ph
                wbf = k.sb([128, 8, DIN], BF16, name='win_bf%d' % l)
                wst = [k.sb([128, 8, 496], F32, name='win_st%d_%d' % (l, i)) for i in range(2)]
                wv = w_in.h.ap()[l].rearrange("(k p) n -> p k n", p=128)
                for cb in range(8):
                    w = wst[cb % 2]
                    k.dma(w[:], wv[:, :, cb * 496:(cb + 1) * 496], reads=[w_in], writes=[w])
                    k.cp(wbf[:, :, cb * 496:(cb + 1) * 496], w[:], [w], [wbf], eng='pool')
                xts = [k.sb([128, 8, 512], F32, name='xt%d_%d' % (l, i)) for i in range(2)]
                uts = [k.sb([128, 8, 512], BF16, name='ut%d_%d' % (l, i)) for i in range(2)]
                zsb = [k.sb([128, 512], F32, name='zsb%d_%d' % (l, i)) for i in range(4)]
                it = 0
                zi = 0
                for b in range(NB):
                    for (t0, n, isctx) in tok_blocks():
                        r = 2 if isctx else b
                        xt = xts[it % 2]
                        ut = uts[it % 2]
                        it += 1
                        k.dma(xt[:, :, :n], xT_cur[b][:, :, t0:t0 + n].rearrange("k p t -> p k t"),
                              reads=[xT_cur], writes=[xt])
                        for kk in range(8):
                            k.act(ut[:, kk, :n], xt[:, kk, :n], AF.Identity, [xt, modT], [ut],
                                  bias=modT[:, kk, r:r + 1], scale=modT[:, 8 + kk, r:r + 1])
                        for mc in range(NZC):
                            p = nps()
                            for kk in range(8):
                                k.mm(p[:, :n], wbf[:, kk, mc * 128:(mc + 1) * 128], ut[:, kk, :n], kk == 0, kk == 7,
                                     [wbf, ut], [p])
                            zs = zsb[zi % 4]
                            zi += 1
                            if zi % 2 == 0:
                                k.cp(zs[:, :n], p[:, :n], [p], [zs], eng='act')
                            else:
                                k.cp(zs[:, :n], p[:, :n], [p], [zs], eng='dve')
                            k.dma(zT[b][mc][:, t0:t0 + n], zs[:, :n], reads=[zs], writes=[zT])
                k.barrier()
            if stop_after == 'B':
                k.es = es
                t = k.sb([128, NZC, 512], F32, name='dbgt')
                k.dma(t[:], zT[0][:, :, 0:512].rearrange("c p t -> p c t"), reads=[zT], writes=[t])
                k.dma(dbg_out[:].rearrange("c p t -> p c t"), t[:], reads=[t], writes=[dbg_out])
                k.final_wait([dbg_out])
                return nc
            mixers_phase(k, l, zT, IN, CST, nps, stop_after, dbg_out)
            if stop_after != 'C':
                attn_phase(k, l, zT, CST, nps)
                merge_phase(k, l, xT_cur, x1T, modT, IN, CST, nps)
            if stop_after == 'D':
                k.es = es
                for b_ in range(NB):
                    t_ = k.sb([128, 8, T], F32, name='dbgD%d' % b_)
                    k.dma(t_[:], x1T[b_].rearrange("c p t -> p c t"), [x1T], [t_])
                    k.dma(dbg_out[b_].rearrange("c p t -> p c t"), t_[:], [t_], [dbg_out])
                k.final_wait([dbg_out])
                return nc
            if stop_after != 'C':
                if l < L - 1:
                    moe_phase(k, l, x1T, x2T, 0, modT, IN, CST, nps)
                    xT_cur = x2T
                    if stop_after == 'E':
                        k.es = es
                        for b_ in range(NB):
                            t_ = k.sb([128, 8, T], F32, name='dbgE%d' % b_)
                            k.dma(t_[:], x2T[b_].rearrange("c p t -> p c t"), [x2T], [t_])
                            k.dma(dbg_out[b_].rearrange("c p t -> p c t"), t_[:], [t_], [dbg_out])
                        k.final_wait([dbg_out])
                        return nc
                else:
                    moe_phase(k, l, x1T, out, NCTX, modT, IN, CST, nps)
            if stop_after == 'C':
                k.es = es
                BR = CST['BR']
                for m_ in range(3):
                    t_ = k.sb([128, 2, T], F32, name='dbgC%d' % m_)
                    k.dma(t_[:], BR[m_][0].rearrange("h p t -> p h t"), [BR], [t_])
                    k.dma(dbg_out[m_].rearrange("h p t -> p h t"), t_[:], [t_], [dbg_out])
                k.final_wait([dbg_out])
                return nc
        k.es = es
        k.final_wait([out])
    return nc


def make_inputs(inputs, core):
    b0 = core * NB
    x = inputs['x'][b0:b0 + NB]
    ctx = inputs['ctx'][b0:b0 + NB]
    xa = np.concatenate([ctx, x], axis=1)
    xT0 = np.ascontiguousarray(xa.transpose(0, 2, 1)).reshape(NB, 8, 128, T)
    c3 = np.stack([inputs['c'][b0], inputs['c'][b0 + 1], inputs['c_ctx']], axis=1)
    cT = np.ascontiguousarray(c3.reshape(8, 128, 3).transpose(1, 0, 2))
    b_modT = np.ascontiguousarray(inputs['b_mod'].reshape(L, 48, 128).transpose(0, 2, 1))
    d_ = {'xT0': xT0, 'cT': cT, 'w_mod': inputs['w_mod'], 'b_modT': b_modT, 'w_in': inputs['w_in']}
    d_.update(host_consts(inputs))
    selm = np.zeros((16, 2048), np.float32)
    for e_ in range(16):
        selm[e_, e_ * 128:(e_ + 1) * 128] = 1.0
    d_['selm'] = selm
    for nm_ in ('rwkv_w2', 'rwkv_a2', 'rwkv_g2', 'w_gate', 'w_branch', 'w_out', 'w_router', 'w_e1', 'w_e3', 'w_e2'):
        d_[nm_] = inputs[nm_]
    return d_


_HC = {}


def host_consts(inputs):
    if 'v' in _HC:
        return _HC['v']
    vecs = np.zeros((128, L, NVC), np.float32)

    def put(name, arr):
        a = np.asarray(arr, np.float32).reshape(L, -1, 128)
        for j in range(a.shape[1]):
            vecs[:, :, VC[name] + j] = a[:, j, :].T
    put('mu', inputs['rwkv_mu'])
    put('w0', inputs['rwkv_w0'].reshape(L, 512))
    put('a0', inputs['rwkv_a0'].reshape(L, 512))
    put('kk', inputs['rwkv_kk'])
    put('ka', inputs['rwkv_ka'])
    put('rk', inputs['rwkv_rk'])
    put('lng', inputs['rwkv_ln_g'])
    put('lnb', inputs['rwkv_ln_b'])
    put('retdec', np.repeat(inputs['ret_decay'].reshape(L, 8), 64, axis=1))
    put('retg', inputs['ret_norm_g'])
    put('retb', inputs['ret_norm_b'])
    hl = inputs['hgrn_lb']
    put('hlb', np.broadcast_to(hl.reshape(1, 2 * L * 256), (L, 2 * L * 256)))
    put('hng', inputs['hgrn_norm_g'])
    put('aqg', np.tile(inputs['attn_q_g'], (1, 2)))
    put('akg', np.tile(inputs['attn_k_g'], (1, 2)))
    put('ln1g', inputs['ln1_g'])
    put('ln1b', inputs['ln1_b'])
    put('ln2g', inputs['ln2_g'])
    put('ln2b', inputs['ln2_b'])
    cst = np.zeros((128, NCST), np.float32)
    p = np.arange(128)
    cst[:, C_BO:C_BO + 128] = (p[:, None] // 64 == p[None, :] // 64)
    cst[:, C_DM:C_DM + 64] = (p[:, None] % 64 == np.arange(64)[None, :])
    rot = np.zeros((128, 128), np.float32)
    for m_ in range(128):
        if m_ % 64 < 32:
            rot[m_ + 32, m_] = -1.0
        else:
            rot[m_ - 32, m_] = 1.0
    cst[:, C_ROT:C_ROT + 128] = rot
    cst[:, C_ID:C_ID + 128] = np.eye(128)
    cst[:, C_ONE:C_ONE + 128] = 1.0
    si_ = np.arange(128)[:, None]
    ti_ = np.arange(64)[None, :]
    for h_ in range(2):
        cst[:, C_MF + 64 * h_:C_MF + 64 * h_ + 64] = (ti_ >= si_)
        cst[:, C_MR + 64 * h_:C_MR + 64 * h_ + 64] = (ti_ <= si_)
        cst[:, C_MFS + 64 * h_:C_MFS + 64 * h_ + 64] = (ti_ > si_)
        cst[:, C_MRS + 64 * h_:C_MRS + 64 * h_ + 64] = (ti_ < si_)
    rows = NLAT // 64
    row = np.repeat(np.arange(rows), 64)
    col = np.tile(np.arange(64), rows)
    inv = (10000.0 ** (-np.arange(16, dtype=np.float32) / 16)).astype(np.float32)
    ang = np.concatenate([row[:, None] * inv, col[:, None] * inv], axis=-1).astype(np.float32)
    cosT = np.cos(ang).astype(np.float32).T
    sinT = np.sin(ang).astype(np.float32).T
    rope = np.stack([np.tile(cosT, (4, 1)), np.tile(sinT, (4, 1))], axis=1)
    _HC['v'] = {'vecs': vecs, 'cst': cst, 'rope': np.ascontiguousarray(rope, dtype=np.float32)}
    return _HC['v']


def kernel(**inputs):
    inputs = {k_: np.asarray(v) for k_, v in inputs.items()}
    nc = build()
    in_maps = [make_inputs(inputs, c) for c in range(8)]
    res = run_bass_kernel_spmd(nc, in_maps, core_ids=list(range(8)))
    outs = []
    for c in range(8):
        o = res.results[c]['out']
        outs.append(o.reshape(NB, D, NLAT).transpose(0, 2, 1))
    return np.ascontiguousarray(np.concatenate(outs, axis=0)).astype(np.float32)
```

```python
import numpy as np
from contextlib import ExitStack
import concourse.bass as bass
import concourse.mybir as mybir
from concourse.bass_utils import run_bass_kernel_spmd

F32, BF16 = mybir.dt.float32, mybir.dt.bfloat16
AF = mybir.ActivationFunctionType
ALU = mybir.AluOpType
AX = mybir.AxisListType

NB = 2
NCTX = 256
NLAT = 2048
T = NCTX + NLAT
D = 1024
L = 2
DIN = 3968
NZC = DIN // 128
ALPHA = float((2 * L) ** 0.25)


class Buf:
    def __init__(self, h, dram=False):
        self.h = h
        self.dram = dram
        self.w = {}
        self.r = {}
        self._ap = h.ap() if dram else None

    def __getitem__(self, k):
        return self._ap[k] if self.dram else self.h[k]


class KB:
    def __init__(self, nc, es):
        self.nc, self.es = nc, es
        self.eng = {'pe': nc.tensor, 'act': nc.scalar, 'dve': nc.vector, 'pool': nc.gpsimd, 'sp': nc.sync}
        self.sem = {e: es.enter_context(nc.semaphore('s_' + e)) for e in ('pe', 'act', 'dve', 'pool')}
        self.cnt = {e: 0 for e in self.sem}
        self.NDS = 16
        for i in range(self.NDS):
            self.sem[i] = es.enter_context(nc.semaphore('d%d' % i))
        self.dcount = 0
        self.known = {e: {} for e in self.eng}
        self.nuniq = 0

    def sb(self, shape, dt=F32, name=None):
        self.nuniq += 1
        return Buf(self.es.enter_context(self.nc.sbuf_tensor(name or 'sb%d' % self.nuniq, list(shape), dt)))

    def ps(self, shape, dt=F32, name=None):
        self.nuniq += 1
        return Buf(self.es.enter_context(self.nc.psum_tensor(name or 'ps%d' % self.nuniq, list(shape), dt)))

    def dram(self, name, shape, dt=F32, kind='Internal'):
        return Buf(self.nc.dram_tensor(name, list(shape), dt, kind=kind), dram=True)

    def _need(self, reads, writes):
        need = {}
        for b in reads:
            for s, v in b.w.items():
                if need.get(s, 0) < v:
                    need[s] = v
        for b in writes:
            for s, v in list(b.w.items()) + list(b.r.items()):
                if need.get(s, 0) < v:
                    need[s] = v
        return need

    def _emit_waits(self, e, need):
        kn = self.known[e]
        eng = self.eng[e]
        for s, v in need.items():
            if kn.get(s, 0) >= v:
                continue
            if e == 'pe' and s == 'pe':
                continue
            eng.wait_ge(self.sem[s], v)
            kn[s] = v

    def _done(self, s, v, reads, writes):
        for b in reads:
            if b.r.get(s, 0) < v:
                b.r[s] = v
        for b in writes:
            b.w = {s: v}
            b.r = {}

    def op(self, e, fn, reads=(), writes=()):
        self._emit_waits(e, self._need(reads, writes))
        inst = fn(self.eng[e])
        self.cnt[e] += 1
        inst.then_inc(self.sem[e], 1)
        self._done(e, self.cnt[e], reads, writes)

    def dma(self, out_ap, in_ap, reads=(), writes=(), q='sp', **kw):
        i = self.dcount
        self.dcount += 1
        s = i % self.NDS
        v = 16 * (i // self.NDS + 1)
        need = self._need(reads, writes)
        if v > 16 and need.get(s, 0) < v - 16:
            need[s] = v - 16
        self._emit_waits(q, need)
        self.eng[q].dma_start(out=out_ap, in_=in_ap, **kw).then_inc(self.sem[s], 16)
        self._done(s, v, reads, writes)

    def barrier(self):
        allev = {e: self.cnt[e] for e in ('pe', 'act', 'dve', 'pool') if self.cnt[e] > 0}
        for i in range(min(self.dcount, self.NDS)):
            last = self.dcount - 1 - ((self.dcount - 1 - i) % self.NDS)
            allev[i] = 16 * (last // self.NDS + 1)
        for e in self.eng:
            need = {s: v for s, v in allev.items() if s != e}
            self._emit_waits(e, need)

    def final_wait(self, bufs):
        need = self._need(bufs, ())
        self._emit_waits('sp', need)

    def mm(self, out, lhsT, rhs, start, stop, reads, writes):
        self.op('pe', lambda e: e.matmul(out, lhsT=lhsT, rhs=rhs, start=start, stop=stop), reads, writes)

    def act(self, out, in_, func, reads, writes, bias=None, scale=None, accum_out=None, eng='act'):
        kw = {}
        if bias is not None:
            kw['bias'] = bias
        if scale is not None:
            kw['scale'] = scale
        if accum_out is not None:
            kw['accum_out'] = accum_out
        self.op('act', lambda e: e.activation(out=out, in_=in_, func=func, **kw), reads, writes)

    def tt(self, out, in0, in1, op, reads, writes, eng='dve'):
        self.op(eng, lambda e: e.tensor_tensor(out=out, in0=in0, in1=in1, op=op), reads, writes)

    def ts(self, out, in0, s1, op0, reads, writes, s2=None, op1=None, eng='dve'):
        if op1 is None:
            self.op(eng, lambda e: e.tensor_scalar(out=out, in0=in0, scalar1=s1, scalar2=None, op0=op0), reads, writes)
        else:
            self.op(eng, lambda e: e.tensor_scalar(out=out, in0=in0, scalar1=s1, scalar2=s2, op0=op0, op1=op1),
                    reads, writes)

    def cp(self, out, in_, reads, writes, eng='dve'):
        if eng == 'act':
            self.op('act', lambda e: e.copy(out=out, in_=in_), reads, writes)
        else:
            self.op(eng, lambda e: e.tensor_copy(out=out, in_=in_), reads, writes)


VC = {}
_o = 0
for _n, _c in [('mu', 9), ('w0', 4), ('a0', 4), ('kk', 2), ('ka', 2), ('rk', 2), ('lng', 2), ('lnb', 2), ('retdec', 4),
               ('retg', 2), ('retb', 2), ('hlb', 8), ('hng', 2), ('aqg', 1), ('akg', 1), ('ln1g', 8), ('ln1b', 8),
               ('ln2g', 8), ('ln2b', 8)]:
    VC[_n] = _o
    _o += _c
NVC = _o
C_BO, C_DM, C_ROT, C_ID, C_ONE, C_MF, C_MR, C_MFS, C_MRS, NCST = 0, 128, 192, 320, 448, 576, 704, 832, 960, 1088
TB = 256
NSTEP_BLK = T // TB


def sap(buf, off, dims):
    h = buf.h
    full = h[:]
    pstride = full.ap[0][0]
    return bass.AP(h, off, [[pstride, 128]] + [list(d) for d in dims])


def mixers_phase(k, l, zT, IN, CST, nps, stop_after, dbg_out):
    vecs = CST['vecs']
    cst = CST['cst']
    cstb = CST['cstb']
    rope = CST['rope']
    SI = CST['SI']
    YC = CST['YC']
    BR = CST['BR']
    AUX = CST['AUX']

    def V(name, j=0):
        return vecs[:, l, VC[name] + j:VC[name] + j + 1]
    BO = cst[:, C_BO:C_BO + 128]
    BOb = cstb[:, C_BO:C_BO + 128]
    ph = ExitStack()
    with ph:
        k.es = ph
        der = k.sb([128, 40], F32, name='der%d' % l)
        k.ts(der[:, 0:9], vecs[:, l, VC['mu']:VC['mu'] + 9], 0.5, ALU.mult, [vecs], [der])
        k.ts(der[:, 9:18], vecs[:, l, VC['mu']:VC['mu'] + 9], -1.0, ALU.mult, [vecs], [der], s2=1.0, op1=ALU.add)
        k.ts(der[:, 18:20], vecs[:, l, VC['ka']:VC['ka'] + 2], -1.0, ALU.mult, [vecs], [der], s2=1.0, op1=ALU.add)
        k.act(der[:, 20:24], vecs[:, l, VC['retdec']:VC['retdec'] + 4], AF.Sigmoid, [vecs], [der])
        if l == 0:
            k.op('dve', lambda e: e.memset(der[:, 24:28], 0.0), [], [der])
        else:
            hl = vecs[:, l, VC['hlb']:VC['hlb'] + 8].rearrange("p (d l h) -> p d l h", d=2, l=2)
            k.tt(der[:, 24:28].rearrange("p (d h) -> p d h", d=2), hl[:, :, 1, :], hl[:, :, 0, :], ALU.subtract,
                 [vecs], [der])
            k.act(der[:, 24:28], der[:, 24:28], AF.Sigmoid, [der], [der])
        k.ts(der[:, 28:32], der[:, 24:28], -1.0, ALU.mult, [der], [der], s2=1.0, op1=ALU.add)
        W2 = k.sb([128, 256], F32, name='W2_%d' % l)
        A2 = k.sb([128, 256], F32, name='A2_%d' % l)
        G2 = k.sb([128, 256], F32, name='G2_%d' % l)
        k.dma(W2[:], IN['rwkv_w2'][l].rearrange("z r c -> (z r) c"), [IN['rwkv_w2']], [W2])
        k.dma(A2[:], IN['rwkv_a2'][l].rearrange("z r c -> (z r) c"), [IN['rwkv_a2']], [A2])
        k.dma(G2[:], IN['rwkv_g2'][l], [IN['rwkv_g2']], [G2])

        NT = 6
        tmps = [k.sb([128, 512], F32, name='ptmp%d_%d' % (l, i)) for i in range(NT)]
        ti = [0]

        def tmp():
            ti[0] = (ti[0] + 1) % NT
            return tmps[ti[0]]
        outs = [k.sb([128, 512], F32, name='pout%d_%d' % (l, i)) for i in range(6)]
        oi = [0]

        def store(dst_ap, dstbuf, fn):
            oi[0] = (oi[0] + 1) % 6
            o = outs[oi[0]]
            fn(o)
            return o

        zs = k.sb([128, 9, 514], F32, name='zs%d' % l)
        twt = k.sb([128, 512], F32, name='twt%d' % l)
        sgt = k.sb([128, 512], F32, name='sgt%d' % l)
        zz = k.sb([128, 9, 512], F32, name='zz%d' % l)
        zt2 = k.sb([128, 9, 512], F32, name='zt2%d' % l)
        zl = [k.sb([128, 512], F32, name='zl%d_%d' % (l, i)) for i in range(4)]
        zli = [0]

        def loadz(b, c, t0, n):
            zli[0] = (zli[0] + 1) % 4
            t_ = zl[zli[0]]
            k.dma(t_[:, :n], zT[b][c][:, t0:t0 + n], [zT], [t_])
            return t_

        def put(dst, name, m, d_, b, hp, t0, n, src):
            k.dma(dst[m][d_][b][hp][:, t0:t0 + n], src[:, :n], [src], [dst])

        for b in range(NB):
            for (t0, n, isctx) in tok_blocks():
                seg0, seg1 = (0, NCTX) if isctx else (NCTX, T)
                k.op('pool', lambda e: e.memset(zs[:, :, 0:1], 0.0), [], [zs])
                k.op('pool', lambda e: e.memset(zs[:, :, n + 1:n + 2], 0.0), [], [zs])
                a0_ = max(t0 - 1, seg0)
                a1_ = min(t0 + n + 1, seg1)
                k.dma(zs[:, :, 1 + (a0_ - t0):1 + (a1_ - t0)],
                      zT[b][0:9][:, :, a0_:a1_].rearrange("c p t -> p c t"), [zT], [zs])
                k.tt(zt2[:, :, :n], zs[:, :, 0:n], zs[:, :, 2:n + 2], ALU.add, [zs], [zt2])
                for c in range(9):
                    k.ts(zt2[:, c, :n], zt2[:, c, :n], der[:, c:c + 1], ALU.mult, [zt2, der], [zt2])
                    k.op('dve', lambda e, c=c: e.scalar_tensor_tensor(out=zz[:, c, :n], in0=zs[:, c, 1:n + 1],
                                                                      scalar=der[:, 9 + c:10 + c], in1=zt2[:, c, :n],
                                                                      op0=ALU.mult, op1=ALU.add), [zs, zt2, der], [zz])
                for hp in range(2):
                    k.dma(SI['q'][0][0][b][hp][:, t0:t0 + n], zz[:, hp, :n], [zz], [SI['q']])
                    k.dma(SI['v'][0][0][b][hp][:, t0:t0 + n], zz[:, 4 + hp, :n], [zz], [SI['v']])
                tw = twt
                k.act(tw[:, :n], zz[:, 6, :n], AF.Tanh, [zz], [tw])
                sg = sgt
                k.act(sg[:, :n], zz[:, 8, :n], AF.Sigmoid, [zz], [sg])
                akeep = {}
                for hp in range(2):
                    kkf = tmp()
                    k.ts(kkf[:, :n], zz[:, 2 + hp, :n], V('kk', hp), ALU.mult, [zz, vecs], [kkf])
                    sq = tmp()
                    k.tt(sq[:, :n], kkf[:, :n], kkf[:, :n], ALU.mult, [kkf], [sq])
                    p = nps()
                    k.mm(p[:, :n], BO, sq[:, :n], True, True, [cst, sq], [p])
                    rs = sq
                    k.ts(rs[:, :n], p[:, :n], 1e-12, ALU.add, [p], [rs])
                    k.act(rs[:, :n], rs[:, :n], AF.Ln, [rs], [rs])
                    k.act(rs[:, :n], rs[:, :n], AF.Exp, [rs], [rs], scale=-0.5)
                    kkn = k.sb([128, 512], F32, name='kkn%d_%d_%d_%d' % (l, b, t0, hp))
                    k.tt(kkn[:, :n], kkf[:, :n], rs[:, :n], ALU.mult, [kkf, rs], [kkn])
                    k.dma(SI['kk'][0][0][b][hp][:, t0:t0 + n], kkn[:, :n], [kkn], [SI['kk']])
                    rk_ = tmp()
                    k.op('dve', lambda e, hp=hp, rk_=rk_: e.scalar_tensor_tensor(
                        out=rk_[:, :n], in0=zz[:, hp, :n], scalar=V('rk', hp), in1=zz[:, 2 + hp, :n],
                        op0=ALU.mult, op1=ALU.mult), [zz, vecs], [rk_])
                    p = nps()
                    k.mm(p[:, :n], BO, rk_[:, :n], True, True, [cst, rk_], [p])
                    bon = tmp()
                    k.tt(bon[:, :n], p[:, :n], zz[:, 4 + hp, :n], ALU.mult, [p, zz], [bon])
                    k.dma(AUX[0][b][hp][:, t0:t0 + n], bon[:, :n], [bon], [AUX])
                    p = nps()
                    k.mm(p[:, :n], G2[:, hp * 128:(hp + 1) * 128], sg[:, :n], True, True, [G2, sg], [p])
                    gg = tmp()
                    k.cp(gg[:, :n], p[:, :n], [p], [gg], eng='act')
                    k.dma(AUX[1][b][hp][:, t0:t0 + n], gg[:, :n], [gg], [AUX])
                    for d_ in range(2):
                        ps_ = slice(64 * d_, 64 * d_ + 64)
                        p = nps()
                        k.mm(p[:, :n], W2[ps_, hp * 128:(hp + 1) * 128], tw[ps_, :n], True, True, [W2, tw], [p])
                        dec = tmp()
                        k.act(dec[:, :n], p[:, :n], AF.Sigmoid, [p, vecs], [dec], bias=V('w0', d_ * 2 + hp))
                        k.act(dec[:, :n], dec[:, :n], AF.Exp, [dec], [dec], scale=-float(np.exp(-0.5)))
                        k.dma(SI['w'][0][d_][b][hp][:, t0:t0 + n], dec[:, :n], [dec], [SI['w']])
                        p = nps()
                        k.mm(p[:, :n], A2[ps_, hp * 128:(hp + 1) * 128], zz[ps_, 7, :n], True, True, [A2, zz], [p])
                        a_ = tmp()
                        k.act(a_[:, :n], p[:, :n], AF.Sigmoid, [p, vecs], [a_], bias=V('a0', d_ * 2 + hp))
                        bb = tmp()
                        k.tt(bb[:, :n], kkn[:, :n], a_[:, :n], ALU.mult, [kkn, a_], [bb])
                        k.dma(SI['b'][0][d_][b][hp][:, t0:t0 + n], bb[:, :n], [bb], [SI['b']])
                        kd = tmp()
                        k.ts(kd[:, :n], a_[:, :n], V('ka', hp), ALU.mult, [a_, vecs, der], [kd],
                             s2=der[:, 18 + hp:19 + hp], op1=ALU.add)
                        k.tt(kd[:, :n], kd[:, :n], zz[:, 2 + hp, :n], ALU.mult, [kd, zz], [kd])
                        k.dma(SI['k'][0][d_][b][hp][:, t0:t0 + n], kd[:, :n], [kd], [SI['k']])
                for hp in range(2):
                    for which, cbase, scl in (('q', 9, 1.0), ('k', 11, 0.125)):
                        zt_ = loadz(b, cbase + hp, t0, n)
                        o = tmp()
                        if isctx:
                            k.ts(o[:, :n], zt_[:, :n], scl, ALU.mult, [zt_], [o])
                        else:
                            tl = t0 - NCTX
                            p = nps()
                            k.mm(p[:, :n], cst[:, C_ROT:C_ROT + 128], zt_[:, :n], True, True, [cst, zt_], [p])
                            o2 = tmp()
                            k.tt(o2[:, :n], p[:, :n], rope[:, 1, tl:tl + n], ALU.mult, [p, rope], [o2])
                            k.tt(o[:, :n], zt_[:, :n], rope[:, 0, tl:tl + n], ALU.mult, [zt_, rope], [o])
                            k.tt(o[:, :n], o[:, :n], o2[:, :n], ALU.add, [o, o2], [o])
                            if scl != 1.0:
                                k.ts(o[:, :n], o[:, :n], scl, ALU.mult, [o], [o])
                        k.dma(SI[which][1][0][b][hp][:, t0:t0 + n], o[:, :n], [o], [SI[which]])
                for hp in range(2):
                    zt_ = loadz(b, 17 + hp, t0, n)
                    o = tmp()
                    k.act(o[:, :n], zt_[:, :n], AF.Silu, [zt_], [o])
                    k.dma(SI['q'][2][0][b][hp][:, t0:t0 + n], o[:, :n], [o], [SI['q']])
                    for d_ in range(2):
                        zt_ = loadz(b, 19 + 2 * d_ + hp, t0, n)
                        sn = tmp()
                        k.act(sn[:, :n], zt_[:, :n], AF.Sigmoid, [zt_], [sn], scale=-1.0)
                        kk_ = tmp()
                        k.ts(kk_[:, :n], sn[:, :n], der[:, 28 + d_ * 2 + hp:29 + d_ * 2 + hp], ALU.mult, [sn, der], [kk_])
                        k.dma(SI['k'][2][d_][b][hp][:, t0:t0 + n], kk_[:, :n], [kk_], [SI['k']])
                        w_ = tmp()
                        k.ts(w_[:, :n], kk_[:, :n], -1.0, ALU.mult, [kk_], [w_], s2=1.0, op1=ALU.add)
                        k.dma(SI['w'][2][d_][b][hp][:, t0:t0 + n], w_[:, :n], [w_], [SI['w']])
        k.barrier()
    import os
    if os.environ.get('USE_CHUNK', '0') == '1':
        scan_all(k, l, zT, CST, nps, mixers=(0,))
        for m in (1, 2):
            chunk_mixer(k, l, m, zT, CST, nps)
    else:
        scan_all(k, l, zT, CST, nps)
    post_mixers(k, l, zT, CST, nps)
    if stop_after == 'C':
        k.es = ExitStack()
        t_ = None


def scan_mixer(k, l, m, zT, CST, nps):
    cst, cstb, vecs = CST['cst'], CST['cstb'], CST['vecs']
    SI, YC = CST['SI'], CST['YC']
    delta = (m == 0)
    BOb = cstb[:, C_BO:C_BO + 128]
    ph = ExitStack()
    with ph:
        k.es = ph
        names = ['q', 'k', 'v'] + (['w'] if m != 1 else []) + (['kk', 'b'] if delta else [])
        A = {nm: k.sb([128, 2, 4, TB], F32, name='A%s_%d_%d' % (nm, l, m)) for nm in names}
        Wc = k.sb([128, 2, 4], F32, name='Wc_%d_%d' % (l, m))
        if m == 1:
            for b in range(NB):
                k.act(Wc[:, :, 2 * b:2 * b + 2],
                      vecs[:, l, VC['retdec']:VC['retdec'] + 4].rearrange("p (d h) -> p d h", d=2),
                      AF.Sigmoid, [vecs], [Wc])
        S = k.sb([128, 512], F32, name='S_%d_%d' % (l, m))
        k.op('pool', lambda e: e.memset(S[:], 0.0), [], [S])
        Ych = k.sb([128, 2, 4, TB], F32, name='Ych_%d_%d' % (l, m))
        NR = 3
        R = [dict(vd=k.sb([128, 512], BF16), vbs=k.sb([128, 512], F32), kv=k.sb([128, 512], F32),
                  t1=k.sb([128, 512], BF16), t2=k.sb([128, 512], F32), t3=k.sb([128, 512], BF16),
                  t5=k.sb([128, 512], F32)) for _ in range(NR)]

        def v4(ap):
            return ap.rearrange("p (d c v) -> p d c v", d=2, c=4)

        def src(nm, d_, b, st):
            if m == 0:
                dd = d_ if nm in ('k', 'w', 'b') else 0
                return SI[nm], SI[nm][0][dd][b][:, :, st:st + TB]
            if m == 1:
                if nm == 'v':
                    return zT, zT[b][13:15][:, :, st:st + TB]
                return SI[nm], SI[nm][1][0][b][:, :, st:st + TB]
            if nm == 'v':
                return zT, zT[b][23:25][:, :, st:st + TB]
            dd = d_ if nm in ('k', 'w') else 0
            return SI[nm], SI[nm][2][dd][b][:, :, st:st + TB]
        S4 = v4(S[:])
        Dm4 = sap(cst, C_DM, [[0, 2], [0, 4], [1, 64]])
        Dm3 = sap(cst, C_DM, [[0, 8], [1, 64]])
        for j in range(NSTEP_BLK):
            st = [TB * j, 0 if j == 0 else T - TB * j]
            for nm in names:
                for d_ in range(2):
                    for b in range(NB):
                        sb_, sa_ = src(nm, d_, b, st[d_])
                        k.dma(A[nm][:, d_, 2 * b:2 * b + 2, :], sa_.rearrange("h p t -> p h t"), [sb_], [A[nm]])
            for s_ in range(TB):
                c0, c1 = s_, TB - 1 - s_
                r = R[s_ % NR]

                def step(nm):
                    if nm == 'w' and m == 1:
                        return sap(Wc, 0, [[4, 2], [1, 4], [0, 64]]), Wc
                    return sap(A[nm], c0, [[4 * TB + c1 - c0, 2], [TB, 4], [0, 64]]), A[nm]
                a_, ab = step('v')
                k.tt(v4(r['vd'][:]), a_, Dm4, ALU.mult, [ab, cst], [r['vd']], eng='pool')
                pA = nps()
                k.mm(pA[:], BOb, r['vd'][:], True, True, [cstb, r['vd']], [pA])
                k.cp(r['vbs'][:], pA[:], [pA], [r['vbs']], eng='act')
                a_, ab = step('k')
                k.tt(v4(r['kv'][:]), v4(r['vbs'][:]), a_, ALU.mult, [r['vbs'], ab], [r['kv']], eng='pool')
                if delta:
                    a_, ab = step('kk')
                    k.tt(v4(r['t1'][:]), S4, a_, ALU.mult, [S, ab], [r['t1']])
                    pB = nps()
                    k.mm(pB[:], BOb, r['t1'][:], True, True, [cstb, r['t1']], [pB])
                a_, ab = step('w')
                k.tt(S4, S4, a_, ALU.mult, [S, ab], [S])
                if delta:
                    a_, ab = step('b')
                    k.tt(v4(r['t2'][:]), v4(pB[:]), a_, ALU.mult, [pB, ab], [r['t2']])
                    k.tt(S[:], S[:], r['t2'][:], ALU.subtract, [S, r['t2']], [S])
                k.tt(S[:], S[:], r['kv'][:], ALU.add, [S, r['kv']], [S])
                a_, ab = step('q')
                k.tt(v4(r['t3'][:]), S4, a_, ALU.mult, [S, ab], [r['t3']])
                pC = nps()
                k.mm(pC[:], BOb, r['t3'][:], True, True, [cstb, r['t3']], [pC])
                t5v = r['t5'][:].rearrange("p (c v) -> p c v", c=8)
                k.tt(t5v, pC[:].rearrange("p (c v) -> p c v", c=8), Dm3, ALU.mult, [pC, cst], [r['t5']])
                yo = sap(Ych, c0, [[4 * TB + c1 - c0, 2], [TB, 4]])
                k.op('dve', lambda e, yo=yo, t5v=t5v: e.tensor_reduce(out=yo, in_=t5v, axis=AX.X, op=ALU.add),
                     [r['t5']], [Ych])
            for d_ in range(2):
                for b in range(NB):
                    k.dma(YC[m][d_][b][:, :, st[d_]:st[d_] + TB].rearrange("h p t -> p h t"),
                          Ych[:, d_, 2 * b:2 * b + 2, :], [Ych], [YC])
        k.barrier()


def chunk_order(d_):
    C = 64
    if d_ == 0:
        return [i * C for i in range(T // C)]
    return [i * C for i in reversed(range(NCTX // C))] + [i * C for i in reversed(range(NCTX // C, T // C))]


def chunk_mixer(k, l, m, zT, CST, nps):
    cst, cstb, vecs = CST['cst'], CST['cstb'], CST['vecs']
    SI, YC = CST['SI'], CST['YC']
    C = 64
    IDf = cst[:, C_ID:C_ID + 128]
    IDb = cstb[:, C_ID:C_ID + 128]
    BO = cst[:, C_BO:C_BO + 128]
    ph = ExitStack()
    with ph:
        k.es = ph
        zeros = k.sb([128, T], F32, name='cz_%d_%d' % (l, m))
        k.op('pool', lambda e: e.memset(zeros[:], 0.0), [], [zeros])
        q = k.sb([128, T], F32, name='cq_%d_%d' % (l, m))
        v = k.sb([128, T], F32, name='cv_%d_%d' % (l, m))
        kk_ = [k.sb([128, T], F32, name='ck_%d_%d_%d' % (l, m, d_)) for d_ in range(2)]
        cum = [k.sb([128, T], F32, name='ccum_%d_%d_%d' % (l, m, d_)) for d_ in range(2)]
        cpv = [k.sb([128, T], F32, name='ccpv_%d_%d_%d' % (l, m, d_)) for d_ in range(2)]
        Ych = [k.sb([128, T], F32, name='cY_%d_%d_%d' % (l, m, d_)) for d_ in range(2)]
        S = [k.sb([128, 128], F32, name='cS_%d_%d_%d' % (l, m, d_)) for d_ in range(2)]
        NR = 3
        R = [[dict(G=k.sb([128, C], F32), Gi=k.sb([128, C], F32), qt=k.sb([128, C], BF16), kt=k.sb([128, 128], BF16), kt32=k.sb([128, C], F32),
                   AT=k.sb([64, 2, C], BF16), Vtok=k.sb([64, 128], BF16), Vpad=k.sb([64, 2, 128], BF16),
                   ktok=k.sb([64, 128], BF16), Sb=k.sb([128, 128], BF16), t=k.sb([128, 128], F32),
                   sc=k.sb([128, 4], F32)) for _ in range(NR)] for d_ in range(2)]
        for d_ in range(2):
            for r in R[d_]:
                k.op('pool', lambda e, r=r: e.memset(r['Vpad'][:], 0.0), [], [r['Vpad']])
                k.op('pool', lambda e, r=r: e.memset(r['kt'][:], 0.0), [], [r['kt']])
        for b in range(NB):
            for hp in range(2):
                k.dma(q[:], SI['q'][m][0][b][hp], [SI['q']], [q])
                k.dma(v[:], zT[b][(13 if m == 1 else 23) + hp], [zT], [v])
                for d_ in range(2):
                    kd = 0 if m == 1 else d_
                    k.dma(kk_[d_][:], SI['k'][m][kd][b][hp], [SI['k']], [kk_[d_]])
                    if m == 1:
                        col = VC['retdec'] + d_ * 2 + hp
                        k.act(R[d_][0]['sc'][:, 0:1], vecs[:, l, col:col + 1], AF.Sigmoid, [vecs], [R[d_][0]['sc']])
                        k.act(R[d_][0]['sc'][:, 0:1], R[d_][0]['sc'][:, 0:1], AF.Ln, [R[d_][0]['sc']], [R[d_][0]['sc']])
                        k.act(cpv[d_][:], zeros[:], AF.Identity, [zeros, R[d_][0]['sc']], [cpv[d_]],
                              bias=R[d_][0]['sc'][:, 0:1])
                    else:
                        k.dma(cpv[d_][:], SI['w'][m][d_][b][hp], [SI['w']], [cpv[d_]])
                        k.act(cpv[d_][:], cpv[d_][:], AF.Ln, [cpv[d_]], [cpv[d_]])
                    k.op('dve', lambda e, d_=d_: e.tensor_tensor_scan(out=cum[d_][:], data0=cpv[d_][:], data1=zeros[:],
                                                                        initial=0.0, op0=ALU.add, op1=ALU.add),
                         [cpv[d_], zeros], [cum[d_]])
                    k.tt(cpv[d_][:], cum[d_][:], cpv[d_][:], ALU.subtract, [cum[d_], cpv[d_]], [cpv[d_]])
                    k.op('pool', lambda e, d_=d_: e.memset(S[d_][:], 0.0), [], [S[d_]])
                orders = [chunk_order(0), chunk_order(1)]
                import os
                _lim = int(os.environ.get('CM_LIMIT', T // C))
                _stg = float(os.environ.get('CM_STAGE', 99))
                for ci in range(min(_lim, T // C)):
                    for d_ in range(2):
                        a = orders[d_][ci]
                        e_ = a + C - 1
                        r = R[d_][ci % NR]
                        sc = r['sc']
                        sl = slice(a, a + C)
                        mid = a + 31 if d_ == 0 else a + 32
                        if d_ == 0:
                            k.ts(sc[:, 0:1], cum[d_][:, mid:mid + 1], -1.0, ALU.mult, [cum[d_]], [sc])
                            k.ts(sc[:, 1:2], cpv[d_][:, a:a + 1], -1.0, ALU.mult, [cpv[d_]], [sc])
                            k.act(r['G'][:], cum[d_][:, sl], AF.Exp, [cum[d_], sc], [r['G']], bias=sc[:, 0:1], scale=1.0)
                            k.act(r['Gi'][:], cum[d_][:, sl], AF.Exp, [cum[d_]], [r['Gi']],
                                  bias=cum[d_][:, mid:mid + 1], scale=-1.0)
                            k.act(sc[:, 2:3], cum[d_][:, mid:mid + 1], AF.Exp, [cum[d_], sc], [sc], bias=sc[:, 1:2],
                                  scale=1.0)
                            gC = r['G'][:, C - 1:C]
                            mk = cstb[0:64, C_MF:C_MF + 128]
                        else:
                            k.ts(sc[:, 0:1], cpv[d_][:, mid:mid + 1], -1.0, ALU.mult, [cpv[d_]], [sc])
                            k.act(r['G'][:], cpv[d_][:, sl], AF.Exp, [cpv[d_]], [r['G']],
                                  bias=cpv[d_][:, mid:mid + 1], scale=-1.0)
                            k.act(r['Gi'][:], cpv[d_][:, sl], AF.Exp, [cpv[d_], sc], [r['Gi']], bias=sc[:, 0:1], scale=1.0)
                            k.act(sc[:, 2:3], cpv[d_][:, mid:mid + 1], AF.Exp, [cpv[d_], cum[d_]], [sc],
                                  bias=cum[d_][:, e_:e_ + 1], scale=-1.0)
                            gC = r['G'][:, 0:1]
                            mk = cstb[0:64, C_MR:C_MR + 128]
                        if _stg < 1:
                            continue
                        k.tt(r['qt'][:], q[:, sl], r['G'][:], ALU.mult, [q, r['G']], [r['qt']])
                        k.tt(r['kt32'][:], kk_[d_][:, sl], r['Gi'][:], ALU.mult, [kk_[d_], r['Gi']], [r['kt32']], eng='pool')
                        k.cp(r['kt'][:, 0:C], r['kt32'][:], [r['kt32']], [r['kt']], eng='pool')
                        if _stg < 2:
                            continue
                        for h in range(2):
                            hs = slice(64 * h, 64 * h + 64)
                            pA = nps()
                            k.mm(pA[0:64, 0:C], r['kt'][hs, 0:C], r['qt'][hs, :], True, True, [r['kt'], r['qt']], [pA])
                            k.tt(r['AT'][:, h, :], pA[0:64, 0:C], mk[:, 0:C], ALU.mult, [pA, cstb], [r['AT']],
                                 eng=('dve'))
                        if _stg < 3:
                            continue
                        pV = nps()
                        k.op('pe', lambda e, pV=pV, sl=sl: e.transpose(out=pV[0:64, 0:128], in_=v[:, sl], identity=IDf),
                             [v, cst], [pV])
                        k.cp(r['Vtok'][:], pV[0:64, 0:128], [pV], [r['Vtok']], eng='act')
                        vp_out = bass.AP(r['Vpad'].h, 0, [[256, 64], [192, 2], [1, 64]])
                        k.cp(vp_out, pV[0:64, 0:128].rearrange("p (g c) -> p g c", g=2), [pV], [r['Vpad']])
                        if _stg < 4:
                            continue
                        pK = nps()
                        k.op('pe', lambda e, pK=pK, r=r: e.transpose(out=pK[0:64, 0:128], in_=r['kt32'][:], identity=IDf),
                             [r['kt32'], cst], [pK])
                        k.cp(r['ktok'][:], pK[0:64, 0:128], [pK], [r['ktok']], eng='act')
                        if _stg < 5:
                            continue
                        k.ts(r['Sb'][:], S[d_][:], sc[:, 2:3], ALU.mult, [S[d_], sc], [r['Sb']])
                        pY = nps()
                        k.mm(pY[:, 0:C], r['Sb'][:], r['qt'][:], True, True, [r['Sb'], r['qt']], [pY])
                        pY2 = nps()
                        for g in range(2):
                            k.mm(pY2[:, 0:C], r['Vpad'][:, g, :], r['AT'][:, g, :], g == 0, g == 1, [r['Vpad'], r['AT']], [pY2])
                        k.cp(Ych[d_][:, sl], pY[:, 0:C], [pY], [Ych[d_]], eng='act')
                        k.tt(Ych[d_][:, sl], Ych[d_][:, sl], pY2[:, 0:C], ALU.add, [Ych[d_], pY2], [Ych[d_]])
                        pS = nps()
                        k.mm(pS[:, 0:128], r['ktok'][:], r['Vtok'][:], True, True, [r['ktok'], r['Vtok']], [pS])
                        k.tt(r['t'][:], pS[:, 0:128], BO, ALU.mult, [pS, cst], [r['t']])
                        k.op('dve', lambda e, d_=d_, r=r, sc=sc: e.scalar_tensor_tensor(
                            out=S[d_][:], in0=S[d_][:], scalar=sc[:, 2:3], in1=r['t'][:], op0=ALU.mult, op1=ALU.add),
                            [S[d_], sc, r['t']], [S[d_]])
                        k.ts(S[d_][:], S[d_][:], gC, ALU.mult, [S[d_], r['G']], [S[d_]])
                for d_ in range(2):
                    k.dma(YC[m][d_][b][hp], Ych[d_][:], [Ych[d_]], [YC])
        k.barrier()


class ScanM:
    def __init__(self, k, l, m, zT, CST, nps, TBs):
        self.k, self.l, self.m, self.zT, self.nps, self.TB = k, l, m, zT, nps, TBs
        self.cst, self.cstb, self.vecs = CST['cst'], CST['cstb'], CST['vecs']
        self.SI, self.YC = CST['SI'], CST['YC']
        self.delta = (m == 0)
        TB_ = TBs
        self.names = ['q', 'k', 'v'] + (['w'] if m != 1 else []) + (['kk', 'b'] if self.delta else [])
        self.A = {nm: k.sb([128, 2, 4, TB_], F32, name='A%s_%d_%d' % (nm, l, m)) for nm in self.names}
        self.Wc = k.sb([128, 2, 4], F32, name='Wc_%d_%d' % (l, m))
        if m == 1:
            for b in range(NB):
                k.act(self.Wc[:, :, 2 * b:2 * b + 2],
                      self.vecs[:, l, VC['retdec']:VC['retdec'] + 4].rearrange("p (d h) -> p d h", d=2),
                      AF.Sigmoid, [self.vecs], [self.Wc])
        self.S = k.sb([128, 512], F32, name='S_%d_%d' % (l, m))
        k.op('pool', lambda e: e.memset(self.S[:], 0.0), [], [self.S])
        self.Ych = k.sb([64, 2, 8, TB_], F32, name='Ych_%d_%d' % (l, m))
        NR = 2
        self.R = []
        for _ in range(NR):
            d = dict(vd=k.sb([128, 512], BF16), vbs=k.sb([128, 512], F32), kv=k.sb([128, 512], F32),
                     t3=k.sb([128, 512], BF16))
            if self.delta:
                d['t1'] = k.sb([128, 512], BF16)
                d['t2'] = k.sb([128, 512], F32)
            self.R.append(d)

    def st(self, j):
        TB_ = self.TB
        s1 = (NCTX - TB_ * (j + 1)) if TB_ * (j + 1) <= NCTX else (T + NCTX - TB_ * (j + 1))
        return [TB_ * j, s1]

    def src(self, nm, d_, b, st):
        m, SI, zT, TB_ = self.m, self.SI, self.zT, self.TB
        if m == 0:
            dd = d_ if nm in ('k', 'w', 'b') else 0
            return SI[nm], SI[nm][0][dd][b][:, :, st:st + TB_]
        if m == 1:
            if nm == 'v':
                return zT, zT[b][13:15][:, :, st:st + TB_]
            return SI[nm], SI[nm][1][0][b][:, :, st:st + TB_]
        if nm == 'v':
            return zT, zT[b][23:25][:, :, st:st + TB_]
        dd = d_ if nm in ('k', 'w') else 0
        return SI[nm], SI[nm][2][dd][b][:, :, st:st + TB_]

    def load(self, j):
        k = self.k
        st = self.st(j)
        for nm in self.names:
            for d_ in range(2):
                for b in range(NB):
                    sb_, sa_ = self.src(nm, d_, b, st[d_])
                    k.dma(self.A[nm][:, d_, 2 * b:2 * b + 2, :], sa_.rearrange("h p t -> p h t"), [sb_], [self.A[nm]])

    def store(self, j):
        k = self.k
        st = self.st(j)
        for d_ in range(2):
            for b in range(NB):
                for hp in range(2):
                    for h in range(2):
                        k.dma(self.YC[self.m][d_][b][hp][64 * h:64 * h + 64, st[d_]:st[d_] + self.TB],
                              self.Ych[:, d_, (b * 2 + hp) * 2 + h, :], [self.Ych], [self.YC])

    def step(self, s_):
        k, m, TB_, A, S, cst, cstb, nps = self.k, self.m, self.TB, self.A, self.S, self.cst, self.cstb, self.nps
        BOb = cstb[:, C_BO:C_BO + 128]
        c0, c1 = s_, TB_ - 1 - s_
        r = self.R[s_ % len(self.R)]

        def v4(ap):
            return ap.rearrange("p (d c v) -> p d c v", d=2, c=4)

        def step(nm):
            if nm == 'w' and m == 1:
                return sap(self.Wc, 0, [[4, 2], [1, 4], [0, 64]]), self.Wc
            return sap(A[nm], c0, [[4 * TB_ + c1 - c0, 2], [TB_, 4], [0, 64]]), A[nm]
        S4 = v4(S[:])
        Dm4 = sap(cst, C_DM, [[0, 2], [0, 4], [1, 64]])
        Dm3 = sap(cst, C_DM, [[0, 8], [1, 64]])
        a_, ab = step('v')
        k.tt(v4(r['vd'][:]), a_, Dm4, ALU.mult, [ab, cst], [r['vd']], eng=('dve' if m == 1 else 'pool'))
        pA = nps()
        k.mm(pA[:], BOb, r['vd'][:], True, True, [cstb, r['vd']], [pA])
        k.cp(r['vbs'][:], pA[:], [pA], [r['vbs']], eng='act')
        a_, ab = step('k')
        k.tt(v4(r['kv'][:]), v4(r['vbs'][:]), a_, ALU.mult, [r['vbs'], ab], [r['kv']], eng='pool')
        if self.delta:
            a_, ab = step('kk')
            k.tt(v4(r['t1'][:]), S4, a_, ALU.mult, [S, ab], [r['t1']])
            pB = nps()
            k.mm(pB[:], BOb, r['t1'][:], True, True, [cstb, r['t1']], [pB])
        a_, ab = step('w')
        k.tt(S4, S4, a_, ALU.mult, [S, ab], [S])
        if self.delta:
            a_, ab = step('b')
            k.tt(v4(r['t2'][:]), v4(pB[:]), a_, ALU.mult, [pB, ab], [r['t2']])
            k.tt(S[:], S[:], r['t2'][:], ALU.subtract, [S, r['t2']], [S])
        k.tt(S[:], S[:], r['kv'][:], ALU.add, [S, r['kv']], [S])
        a_, ab = step('q')
        k.tt(v4(r['t3'][:]), S4, a_, ALU.mult, [S, ab], [r['t3']])
        pC = nps()
        hm = sap(cstb, C_BO, [[64, 2]])
        for c in range(8):
            k.mm(pC[0:64, 2 * c:2 * c + 2], r['t3'][:, c * 64:(c + 1) * 64], hm, True, True, [r['t3'], cstb], [pC])
        yo = bass.AP(self.Ych.h, c0, [[2 * 8 * TB_, 64], [8 * TB_ + c1 - c0, 2], [TB_, 8]])
        k.cp(yo, pC[0:64, 0:16].rearrange("p (d c) -> p d c", d=2), [pC], [self.Ych], eng='act')


def scan_all(k, l, zT, CST, nps, mixers=(0, 1, 2)):
    TBs = 128
    ph = ExitStack()
    with ph:
        k.es = ph
        Ms = [ScanM(k, l, m, zT, CST, nps, TBs) for m in mixers]
        for j in range(T // TBs):
            for M_ in Ms:
                M_.load(j)
            for s_ in range(TBs):
                for M_ in Ms:
                    M_.step(s_)
            for M_ in Ms:
                M_.store(j)
        k.barrier()


def post_mixers(k, l, zT, CST, nps):
    cst, vecs = CST['cst'], CST['vecs']
    YC, BR, AUX = CST['YC'], CST['BR'], CST['AUX']
    BO = cst[:, C_BO:C_BO + 128]
    ph = ExitStack()
    with ph:
        k.es = ph
        NT = 10
        tm = [k.sb([128, 512], F32, name='pm%d_%d' % (l, i)) for i in range(NT)]
        ti = [0]

        def tmp():
            ti[0] = (ti[0] + 1) % NT
            return tm[ti[0]]

        def ld(buf, ap, n):
            t_ = tmp()
            k.dma(t_[:, :n], ap, [buf], [t_])
            return t_
        blocks = tok_blocks() if l < L - 1 else tok_blocks()[1:]
        for m in range(3):
            for b in range(NB):
                for hp in range(2):
                    for (t0, n, isctx) in blocks:
                        y0 = ld(YC, YC[m][0][b][hp][:, t0:t0 + n], n)
                        y1 = ld(YC, YC[m][1][b][hp][:, t0:t0 + n], n)
                        y = tmp()
                        k.tt(y[:, :n], y0[:, :n], y1[:, :n], ALU.add, [y0, y1], [y])
                        if m < 2:
                            p = nps()
                            k.mm(p[:, :n], BO, y[:, :n], True, True, [cst, y], [p])
                            mean = tmp()
                            k.ts(mean[:, :n], p[:, :n], 1.0 / 64, ALU.mult, [p], [mean])
                            yc = tmp()
                            k.tt(yc[:, :n], y[:, :n], mean[:, :n], ALU.subtract, [y, mean], [yc])
                            eps = 64e-5 if m == 0 else 1e-5
                        else:
                            yc = y
                            eps = 1e-6
                        sq = tmp()
                        k.tt(sq[:, :n], yc[:, :n], yc[:, :n], ALU.mult, [yc], [sq])
                        p = nps()
                        k.mm(p[:, :n], BO, sq[:, :n], True, True, [cst, sq], [p])
                        rstd = sq
                        k.ts(rstd[:, :n], p[:, :n], 1.0 / 64, ALU.mult, [p], [rstd], s2=eps, op1=ALU.add)
                        k.act(rstd[:, :n], rstd[:, :n], AF.Ln, [rstd], [rstd])
                        k.act(rstd[:, :n], rstd[:, :n], AF.Exp, [rstd], [rstd], scale=-0.5)
                        o = tmp()
                        k.tt(o[:, :n], yc[:, :n], rstd[:, :n], ALU.mult, [yc, rstd], [o])
                        gn = ('lng', 'retg', 'hng')[m]
                        gcol = vecs[:, l, VC[gn] + hp:VC[gn] + hp + 1]
                        if m < 2:
                            bn = ('lnb', 'retb')[m]
                            bcol = vecs[:, l, VC[bn] + hp:VC[bn] + hp + 1]
                            k.act(o[:, :n], o[:, :n], AF.Identity, [o, vecs], [o], bias=bcol, scale=gcol)
                        else:
                            k.ts(o[:, :n], o[:, :n], gcol, ALU.mult, [o, vecs], [o])
                        if m == 0:
                            bon = ld(AUX, AUX[0][b][hp][:, t0:t0 + n], n)
                            gg = ld(AUX, AUX[1][b][hp][:, t0:t0 + n], n)
                            k.tt(o[:, :n], o[:, :n], bon[:, :n], ALU.add, [o, bon], [o])
                            k.tt(o[:, :n], o[:, :n], gg[:, :n], ALU.mult, [o, gg], [o])
                        else:
                            zc = 15 + hp if m == 1 else 25 + hp
                            g_ = ld(zT, zT[b][zc][:, t0:t0 + n], n)
                            k.act(g_[:, :n], g_[:, :n], AF.Silu, [g_], [g_])
                            k.tt(o[:, :n], o[:, :n], g_[:, :n], ALU.mult, [o, g_], [o])
                        k.dma(BR[m][b][hp][:, t0:t0 + n], o[:, :n], [o], [BR])
        k.barrier()


def attn_phase(k, l, zT, CST, nps):
    cst, cstb, vecs, rope = CST['cst'], CST['cstb'], CST['vecs'], CST['rope']
    BR, KA = CST['BR'], CST['KA']
    BO = cst[:, C_BO:C_BO + 128]
    IDf = cst[:, C_ID:C_ID + 128]
    IDb = cstb[:, C_ID:C_ID + 128]
    ph = ExitStack()
    with ph:
        k.es = ph
        NT = 8
        tm = [k.sb([128, 512], F32, name='at%d_%d' % (l, i)) for i in range(NT)]
        ti = [0]

        def tmp():
            ti[0] = (ti[0] + 1) % NT
            return tm[ti[0]]
        qb = k.sb([128, 2, T], BF16, name='qb%d' % l)
        kd32 = k.sb([128, 2, T], F32, name='kd32_%d' % l)
        kdb = k.sb([128, 2, T], BF16, name='kdb%d' % l)
        vpad = k.sb([128, 18, 2, 2, 128], BF16, name='vpad%d' % l)
        Sf = k.sb([128, T], F32, name='Sf%d' % l)
        Pb = k.sb([128, T], BF16, name='Pb%d' % l)
        PT = k.sb([128, 18, 128], BF16, name='PT%d' % l)
        st_ = k.sb([128, 4], F32, name='ast%d' % l)
        ob = [k.sb([128, 128], F32, name='aob%d_%d' % (l, i)) for i in range(2)]
        k.op('pool', lambda e: e.memset(vpad[:], 0.0), [], [vpad])
        for b in range(NB):
            for (t0, n, isctx) in tok_blocks():
                for which in range(3):
                    zc = 27 + which
                    zt_ = tmp()
                    k.dma(zt_[:, :n], zT[b][zc][:, t0:t0 + n], [zT], [zt_])
                    sq = tmp()
                    k.tt(sq[:, :n], zt_[:, :n], zt_[:, :n], ALU.mult, [zt_], [sq])
                    p = nps()
                    k.mm(p[:, :n], BO, sq[:, :n], True, True, [cst, sq], [p])
                    k.ts(sq[:, :n], p[:, :n], 1.0 / 64, ALU.mult, [p], [sq], s2=1e-6, op1=ALU.add)
                    k.act(sq[:, :n], sq[:, :n], AF.Ln, [sq], [sq])
                    k.act(sq[:, :n], sq[:, :n], AF.Exp, [sq], [sq], scale=-0.5)
                    gname = 'aqg' if which < 2 else 'akg'
                    qn = tmp()
                    k.op('dve', lambda e, qn=qn, zt_=zt_, sq=sq, gname=gname: e.scalar_tensor_tensor(
                        out=qn[:, :n], in0=zt_[:, :n], scalar=vecs[:, l, VC[gname]:VC[gname] + 1], in1=sq[:, :n],
                        op0=ALU.mult, op1=ALU.mult), [zt_, sq, vecs], [qn])
                    if not isctx:
                        tl = t0 - NCTX
                        p = nps()
                        k.mm(p[:, :n], cst[:, C_ROT:C_ROT + 128], qn[:, :n], True, True, [cst, qn], [p])
                        o2 = tmp()
                        k.tt(o2[:, :n], p[:, :n], rope[:, 1, tl:tl + n], ALU.mult, [p, rope], [o2])
                        k.tt(qn[:, :n], qn[:, :n], rope[:, 0, tl:tl + n], ALU.mult, [qn, rope], [qn])
                        k.tt(qn[:, :n], qn[:, :n], o2[:, :n], ALU.add, [qn, o2], [qn])
                    if which < 2:
                        k.ts(qb[:, which, t0:t0 + n], qn[:, :n], 0.125, ALU.mult, [qn], [qb])
                    else:
                        k.dma(KA[b][:, t0:t0 + n], qn[:, :n], [qn], [KA])
            for kv in range(2):
                for half in range(2):
                    k.dma(kd32[64 * half:64 * half + 64, kv, :], KA[b][64 * kv:64 * kv + 64, :], [KA], [kd32])
            k.cp(kdb[:], kd32[:], [kd32], [kdb], eng='pool')
            for kt in range(18):
                vt = tmp()
                k.dma(vt[:, :128], zT[b][30][:, kt * 128:(kt + 1) * 128], [zT], [vt])
                p = nps()
                k.op('pe', lambda e, p=p, vt=vt: e.transpose(out=p[:, :128], in_=vt[:, :128], identity=IDf),
                     [vt, cst], [p])
                for kv in range(2):
                    for g in range(2):
                        k.cp(vpad[:, kt, kv, g, 64 * g:64 * g + 64], p[:, 64 * kv:64 * kv + 64], [p], [vpad],
                             eng=('act' if g else 'dve'))
            qtiles = [(NCTX + i * 128, T) for i in range(NLAT // 128)]
            if l < L - 1:
                qtiles = [(0, NCTX), (128, NCTX)] + qtiles
            for hp in range(2):
                for (q0, nk) in qtiles:
                    nkt = nk // 128
                    pO = CST['PS0']
                    for g in range(2):
                        off = 64 * g
                        for kb0 in range(0, nk, 512):
                            kn = min(512, nk - kb0)
                            p = nps()
                            k.mm(p[:, :kn], qb[off:off + 64, hp, q0:q0 + 128], kdb[off:off + 64, hp, kb0:kb0 + kn],
                                 True, True, [qb, kdb], [p])
                            k.cp(Sf[:, kb0:kb0 + kn], p[:, :kn], [p], [Sf], eng='act')
                        k.op('dve', lambda e: e.tensor_reduce(out=st_[:, 0:1], in_=Sf[:, :nk], axis=AX.X, op=ALU.max),
                             [Sf], [st_])
                        k.ts(st_[:, 1:2], st_[:, 0:1], -1.0, ALU.mult, [st_], [st_])
                        k.act(Pb[:, :nk], Sf[:, :nk], AF.Exp, [Sf, st_], [Pb, st_], bias=st_[:, 1:2],
                              accum_out=st_[:, 2:3])
                        k.op('dve', lambda e: e.reciprocal(out=st_[:, 3:4], in_=st_[:, 2:3]), [st_], [st_])
                        k.ts(Pb[:, :nk], Pb[:, :nk], st_[:, 3:4], ALU.mult, [Pb, st_], [Pb])
                        for k4 in range(0, nkt, 4):
                            nn = min(4, nkt - k4)
                            p = nps()
                            pb_ = p.h[:].bitcast(BF16)
                            for j in range(nn):
                                k.op('pe', lambda e, j=j, pb_=pb_, k4=k4: e.transpose(
                                    out=pb_[:, j * 128:(j + 1) * 128], in_=Pb[:, (k4 + j) * 128:(k4 + j + 1) * 128],
                                    identity=IDb), [Pb, cstb], [p])
                            k.cp(PT[:, k4:k4 + nn, :].rearrange("p a b -> p (a b)"), pb_[:, :nn * 128], [p], [PT],
                                 eng=('act' if (k4 // 4) % 2 else 'dve'))
                        for kt in range(nkt):
                            k.mm(pO[:, :128], vpad[:, kt, hp, g, :], PT[:, kt, :], (g == 0 and kt == 0),
                                 (g == 1 and kt == nkt - 1), [vpad, PT], [pO])
                    o = ob[(q0 // 128) % 2]
                    k.cp(o[:], pO[:, :128], [pO], [o], eng='act')
                    k.dma(BR[3][b][hp][:, q0:q0 + 128], o[:], [o], [BR])
        k.barrier()


def merge_phase(k, l, xT_cur, x1T, modT, IN, CST, nps):
    cst, vecs = CST['cst'], CST['vecs']
    BR = CST['BR']
    ONE = cst[:, C_ONE:C_ONE + 128]
    ph = ExitStack()
    with ph:
        k.es = ph
        wg = k.sb([128, 4, 8, 1024], BF16, name='wg%d' % l)
        wb = k.sb([128, 4, 2, 1024], BF16, name='wb%d' % l)
        wo = k.sb([128, 8, 1024], BF16, name='wo%d' % l)
        stg = [k.sb([128, 1024], F32, name='mstg%d_%d' % (l, i)) for i in range(2)]
        si = 0
        for i in range(4):
            for hh in range(8):
                w = stg[si % 2]
                si += 1
                wv_ = w[:].rearrange("p (k n) -> p k n", k=8)
                k.dma(wv_, IN['w_gate'][l][i].rearrange("(k p) n -> p k n", p=128)[:, :, hh * 128:(hh + 1) * 128],
                      [IN['w_gate']], [w])
                k.cp(wg[:, i, :, hh * 128:(hh + 1) * 128], wv_, [w], [wg], eng='pool')
            for hh in range(2):
                w = stg[si % 2]
                si += 1
                wv_ = w[:].rearrange("p (k n) -> p k n", k=2)
                k.dma(wv_, IN['w_branch'][l][i].rearrange("(k p) n -> p k n", p=128)[:, :, hh * 512:(hh + 1) * 512],
                      [IN['w_branch']], [w])
                k.cp(wb[:, i, :, hh * 512:(hh + 1) * 512], wv_, [w], [wb], eng='pool')
        for hh in range(8):
            w = stg[si % 2]
            si += 1
            wv_ = w[:].rearrange("p (k n) -> p k n", k=8)
            k.dma(wv_, IN['w_out'][l].rearrange("(k p) n -> p k n", p=128)[:, :, hh * 128:(hh + 1) * 128],
                  [IN['w_out']], [w])
            k.cp(wo[:, :, hh * 128:(hh + 1) * 128], wv_, [w], [wo], eng='pool')
        xt = k.sb([128, 8, 512], F32, name='mxt%d' % l)
        ut = k.sb([128, 8, 512], BF16, name='mut%d' % l)
        br32 = [k.sb([128, 2, 512], F32, name='mbr32_%d_%d' % (l, i)) for i in range(1)]
        brb = k.sb([128, 4, 2, 512], BF16, name='mbrb_%d' % l)
        mg = k.sb([128, 8, 512], BF16, name='mmg%d' % l)
        acc = k.sb([128, 512], F32, name='macc%d' % l)
        sg = [k.sb([128, 512], F32, name='msg%d_%d' % (l, i)) for i in range(2)]
        y = k.sb([128, 8, 512], F32, name='my%d' % l)
        ysq = [k.sb([128, 512], F32, name='mysq%d_%d' % (l, i)) for i in range(2)]
        mean = k.sb([128, 512], F32, name='mmean%d' % l)
        rstd = k.sb([128, 512], F32, name='mrstd%d' % l)
        xo = [k.sb([128, 512], F32, name='mxo%d_%d' % (l, i)) for i in range(2)]
        blocks = tok_blocks() if l < L - 1 else tok_blocks()[1:]
        for b in range(NB):
            for (t0, n, isctx) in blocks:
                r = 2 if isctx else b
                k.dma(xt[:, :, :n], xT_cur[b][:, :, t0:t0 + n].rearrange("k p t -> p k t"), [xT_cur], [xt])
                for kk in range(8):
                    k.act(ut[:, kk, :n], xt[:, kk, :n], AF.Identity, [xt, modT], [ut],
                          bias=modT[:, kk, r:r + 1], scale=modT[:, 8 + kk, r:r + 1])
                k.ts(xt[:, :, :n], xt[:, :, :n], ALPHA, ALU.mult, [xt], [xt], eng='pool')
                for i in range(4):
                    b32 = br32[0]
                    k.dma(b32[:, :, :n], BR[i][b][:, :, t0:t0 + n].rearrange("h p t -> p h t"), [BR], [b32])
                    k.cp(brb[:, i, :, :n], b32[:, :, :n], [b32], [brb], eng='pool')
                for mo in range(8):
                    ms = slice(mo * 128, (mo + 1) * 128)
                    for i in range(4):
                        pG = nps()
                        for kk in range(8):
                            k.mm(pG[:, :n], wg[:, i, kk, ms], ut[:, kk, :n], kk == 0, kk == 7, [wg, ut], [pG])
                        pB = nps()
                        for hh in range(2):
                            k.mm(pB[:, :n], wb[:, i, hh, ms], brb[:, i, hh, :n], hh == 0, hh == 1, [wb, brb], [pB])
                        s_ = sg[i % 2]
                        k.act(s_[:, :n], pG[:, :n], AF.Sigmoid, [pG], [s_])
                        if i == 0:
                            k.tt(acc[:, :n], s_[:, :n], pB[:, :n], ALU.mult, [s_, pB], [acc])
                        else:
                            k.tt(s_[:, :n], s_[:, :n], pB[:, :n], ALU.mult, [s_, pB], [s_])
                            if i < 3:
                                k.tt(acc[:, :n], acc[:, :n], s_[:, :n], ALU.add, [acc, s_], [acc])
                            else:
                                k.tt(mg[:, mo, :n], acc[:, :n], s_[:, :n], ALU.add, [acc, s_], [mg])
                for mo in range(8):
                    ms = slice(mo * 128, (mo + 1) * 128)
                    p = nps()
                    for kk in range(8):
                        k.mm(p[:, :n], wo[:, kk, ms], mg[:, kk, :n], kk == 0, kk == 7, [wo, mg], [p])
                    k.op('dve', lambda e, p=p, mo=mo: e.scalar_tensor_tensor(
                        out=y[:, mo, :n], in0=p[:, :n], scalar=modT[:, 16 + mo, r:r + 1], in1=xt[:, mo, :n],
                        op0=ALU.mult, op1=ALU.add), [p, modT, xt], [y])
                layer_norm_cm(k, y, ysq, mean, rstd, n, vecs[:, l, VC['ln1g']:VC['ln1g'] + 8],
                              vecs[:, l, VC['ln1b']:VC['ln1b'] + 8], vecs, ONE, cst, nps, xo,
                              lambda mo, b=b, t0=t0, n=n: x1T[b][mo][:, t0:t0 + n], x1T)
        k.barrier()


def layer_norm_cm(k, y, ysq, mean, rstd, n, gcols, bcols, vecs, ONE, cst, nps, xo, dst_fn, dstbuf):
    p1 = nps()
    for mo in range(8):
        k.mm(p1[:, :n], ONE, y[:, mo, :n], mo == 0, mo == 7, [cst, y], [p1])
    p2 = nps()
    for mo in range(8):
        q_ = ysq[mo % 2]
        k.tt(q_[:, :n], y[:, mo, :n], y[:, mo, :n], ALU.mult, [y], [q_], eng='pool')
        k.mm(p2[:, :n], ONE, q_[:, :n], mo == 0, mo == 7, [cst, q_], [p2])
    k.ts(mean[:, :n], p1[:, :n], 1.0 / D, ALU.mult, [p1], [mean])
    k.tt(rstd[:, :n], mean[:, :n], mean[:, :n], ALU.mult, [mean], [rstd])
    k.op('dve', lambda e: e.scalar_tensor_tensor(out=rstd[:, :n], in0=p2[:, :n], scalar=1.0 / D, in1=rstd[:, :n],
                                                 op0=ALU.mult, op1=ALU.subtract), [p2, rstd], [rstd])
    k.ts(rstd[:, :n], rstd[:, :n], 1e-5, ALU.add, [rstd], [rstd])
    k.act(rstd[:, :n], rstd[:, :n], AF.Ln, [rstd], [rstd])
    k.act(rstd[:, :n], rstd[:, :n], AF.Exp, [rstd], [rstd], scale=-0.5)
    for mo in range(8):
        o = xo[mo % 2]
        k.tt(o[:, :n], y[:, mo, :n], mean[:, :n], ALU.subtract, [y, mean], [o])
        k.tt(o[:, :n], o[:, :n], rstd[:, :n], ALU.mult, [o, rstd], [o])
        k.act(o[:, :n], o[:, :n], AF.Identity, [o, vecs], [o], bias=bcols[:, mo:mo + 1], scale=gcols[:, mo:mo + 1])
        k.dma(dst_fn(mo), o[:, :n], [o], [dstbuf])


def moe_phase(k, l, x1T, dstT, dst_off, modT, IN, CST, nps):
    cst, vecs = CST['cst'], CST['vecs']
    ONE = cst[:, C_ONE:C_ONE + 128]
    IDf = cst[:, C_ID:C_ID + 128]
    ph = ExitStack()
    with ph:
        k.es = ph
        selm = k.sb([16, 2048], F32, name='selm%d' % l)
        k.dma(selm[:], IN['selm'][:], [IN['selm']], [selm])
        wr = k.sb([128, 8, 16], F32, name='wr%d' % l)
        k.dma(wr[:], IN['w_router'][l].rearrange("(k p) e -> p k e", p=128), [IN['w_router']], [wr])
        W = [k.sb([128, 8, 1024], BF16, name='wexp%d_%d' % (l, i)) for i in range(3)]
        stg = [k.sb([128, 1024], F32, name='estg%d_%d' % (l, i)) for i in range(2)]
        acc = k.sb([128, 8, 1024], F32, name='eacc%d' % l)
        u2b = k.sb([128, 8, 1024], BF16, name='eu2b%d' % l)
        hT = k.sb([128, 8, 512], BF16, name='ehT%d' % l)
        affT = k.sb([16, 2048], F32, name='eaff%d' % l)
        work = k.sb([16, 2048], F32, name='ework%d' % l)
        m8 = k.sb([16, 8], F32, name='em8%d' % l)
        x1t = k.sb([128, 8, 512], F32, name='ex1t%d' % l)
        u2f = k.sb([128, 8, 128], F32, name='eu2f%d' % l)
        lg = k.sb([128, 16], F32, name='elg%d' % l)
        st_ = k.sb([128, 4], F32, name='est%d' % l)
        gb = k.sb([128, 512], F32, name='egb%d' % l)
        sl = [k.sb([128, 512], F32, name='esl%d_%d' % (l, i)) for i in range(2)]
        y = x1t
        ysq = [k.sb([128, 512], F32, name='eysq%d_%d' % (l, i)) for i in range(2)]
        mean = k.sb([128, 512], F32, name='emean%d' % l)
        rstd = k.sb([128, 512], F32, name='erstd%d' % l)
        xo = [k.sb([128, 512], F32, name='exo%d_%d' % (l, i)) for i in range(2)]
        si = [0]
        segs = [(NCTX, NLAT, None)] + ([(0, NCTX, 2)] if l < L - 1 else [])
        for b in range(NB):
            for (s0, ntok, rr) in segs:
                r = b if rr is None else rr
                cap = 2 * ntok // 16
                for tt_ in range(ntok // 128):
                    t0 = s0 + tt_ * 128
                    k.dma(x1t[:, :, :128], x1T[b][:, :, t0:t0 + 128].rearrange("k p t -> p k t"), [x1T], [x1t])
                    for kk in range(8):
                        k.act(u2f[:, kk, :], x1t[:, kk, :128], AF.Identity, [x1t, modT], [u2f],
                              bias=modT[:, 24 + kk, r:r + 1], scale=modT[:, 32 + kk, r:r + 1])
                    p = nps()
                    for kk in range(8):
                        k.mm(p[:, :16], u2f[:, kk, :], wr[:, kk, :], kk == 0, kk == 7, [u2f, wr], [p])
                    k.op('dve', lambda e, p=p: e.tensor_reduce(out=st_[:, 0:1], in_=p[:, :16], axis=AX.X, op=ALU.max),
                         [p], [st_])
                    k.ts(st_[:, 1:2], st_[:, 0:1], -1.0, ALU.mult, [st_], [st_])
                    k.act(lg[:], p[:, :16], AF.Exp, [p, st_], [lg, st_], bias=st_[:, 1:2], accum_out=st_[:, 2:3])
                    k.op('dve', lambda e: e.reciprocal(out=st_[:, 3:4], in_=st_[:, 2:3]), [st_], [st_])
                    k.ts(lg[:], lg[:], st_[:, 3:4], ALU.mult, [lg, st_], [lg])
                    p = nps()
                    k.op('pe', lambda e, p=p: e.transpose(out=p[0:16, :128], in_=lg[:], identity=IDf), [lg, cst], [p])
                    k.cp(affT[:, tt_ * 128:(tt_ + 1) * 128], p[0:16, :128], [p], [affT], eng='act')
                k.cp(work[:, :ntok], affT[:, :ntok], [affT], [work])
                nit = cap // 8
                for it in range(nit):
                    k.op('dve', lambda e: e.max(out=m8[:], in_=work[:, :ntok]), [work], [m8])
                    if it < nit - 1:
                        k.op('dve', lambda e: e.match_replace(out=work[:, :ntok], in_to_replace=m8[:],
                                                              in_values=work[:, :ntok], imm_value=-1.0),
                             [work, m8], [work])
                k.ts(work[:, :ntok], affT[:, :ntok], m8[:, 7:8], ALU.is_ge, [affT, m8], [work])
                k.tt(work[:, :ntok], work[:, :ntok], affT[:, :ntok], ALU.mult, [work, affT], [work])
                for c0 in range(0, ntok, 1024):
                    cn = min(1024, ntok - c0)
                    for bb0 in range(0, cn, 512):
                        bn = min(512, cn - bb0)
                        t0 = s0 + c0 + bb0
                        k.dma(x1t[:, :, :bn], x1T[b][:, :, t0:t0 + bn].rearrange("k p t -> p k t"), [x1T], [x1t])
                        for kk in range(8):
                            k.act(u2b[:, kk, bb0:bb0 + bn], x1t[:, kk, :bn], AF.Identity, [x1t, modT], [u2b],
                                  bias=modT[:, 24 + kk, r:r + 1], scale=modT[:, 32 + kk, r:r + 1])
                    for e_ in range(16):
                        for wi, wname in enumerate(('w_e1', 'w_e3', 'w_e2')):
                            wv = IN[wname][l][e_].rearrange("(k p) n -> p k n", p=128)
                            for hh in range(8):
                                w = stg[si[0] % 2]
                                si[0] += 1
                                wv_ = w[:].rearrange("p (k n) -> p k n", k=8)
                                k.dma(wv_, wv[:, :, hh * 128:(hh + 1) * 128], [IN[wname]], [w])
                                k.cp(W[wi][:, :, hh * 128:(hh + 1) * 128], wv_, [w], [W[wi]],
                                     eng=('pool' if hh % 2 else 'act'))
                        for bb0 in range(0, cn, 512):
                            bn = min(512, cn - bb0)
                            p = nps()
                            k.mm(p[:, :bn], selm[0:16, e_ * 128:(e_ + 1) * 128], work[0:16, c0 + bb0:c0 + bb0 + bn],
                                 True, True, [selm, work], [p])
                            k.cp(gb[:, :bn], p[:, :bn], [p], [gb], eng='act')
                            for f in range(8):
                                fs = slice(f * 128, (f + 1) * 128)
                                p1 = nps()
                                for kk in range(8):
                                    k.mm(p1[:, :bn], W[0][:, kk, fs], u2b[:, kk, bb0:bb0 + bn], kk == 0, kk == 7,
                                         [W[0], u2b], [p1])
                                p3 = nps()
                                for kk in range(8):
                                    k.mm(p3[:, :bn], W[1][:, kk, fs], u2b[:, kk, bb0:bb0 + bn], kk == 0, kk == 7,
                                         [W[1], u2b], [p3])
                                s_ = sl[f % 2]
                                k.act(s_[:, :bn], p1[:, :bn], AF.Silu, [p1], [s_])
                                k.tt(s_[:, :bn], s_[:, :bn], p3[:, :bn], ALU.mult, [s_, p3], [s_])
                                k.tt(hT[:, f, :bn], s_[:, :bn], gb[:, :bn], ALU.mult, [s_, gb], [hT], eng='pool')
                            for dch in range(8):
                                py = nps()
                                for f in range(8):
                                    k.mm(py[:, :bn], W[2][:, f, dch * 128:(dch + 1) * 128], hT[:, f, :bn], f == 0, f == 7,
                                         [W[2], hT], [py])
                                if e_ == 0:
                                    k.cp(acc[:, dch, bb0:bb0 + bn], py[:, :bn], [py], [acc], eng='act')
                                else:
                                    k.tt(acc[:, dch, bb0:bb0 + bn], acc[:, dch, bb0:bb0 + bn], py[:, :bn], ALU.add,
                                         [acc, py], [acc])
                    for bb0 in range(0, cn, 512):
                        bn = min(512, cn - bb0)
                        t0 = s0 + c0 + bb0
                        k.dma(x1t[:, :, :bn], x1T[b][:, :, t0:t0 + bn].rearrange("k p t -> p k t"), [x1T], [x1t])
                        k.ts(x1t[:, :, :bn], x1t[:, :, :bn], ALPHA, ALU.mult, [x1t], [x1t], eng='pool')
                        for mo in range(8):
                            k.op('dve', lambda e, mo=mo: e.scalar_tensor_tensor(
                                out=y[:, mo, :bn], in0=acc[:, mo, bb0:bb0 + bn], scalar=modT[:, 40 + mo, r:r + 1],
                                in1=x1t[:, mo, :bn], op0=ALU.mult, op1=ALU.add), [acc, modT, x1t], [x1t])
                        layer_norm_cm(k, y, ysq, mean, rstd, bn, vecs[:, l, VC['ln2g']:VC['ln2g'] + 8],
                                      vecs[:, l, VC['ln2b']:VC['ln2b'] + 8], vecs, ONE, cst, nps, xo,
                                      lambda mo, b=b, t0=t0, bn=bn: dstT[b][mo][:, t0 - dst_off:t0 - dst_off + bn], dstT)
        k.barrier()


def tok_blocks():
    bl = [(0, NCTX, True)]
    for i in range(NLAT // 512):
        bl.append((NCTX + i * 512, 512, False))
    return bl


def build(stop_after=None, dbg=None):
    nc = bass.Bass("TRN2", target_bir_lowering=False)
    es = ExitStack()
    with es:
        k = KB(nc, es)
        IN = {}

        def inp(name, shape, dt=F32):
            IN[name] = k.dram(name, shape, dt, kind='ExternalInput')
            return IN[name]
        xT0 = inp('xT0', [NB, 8, 128, T])
        cT = inp('cT', [128, 8, 3])
        w_mod = inp('w_mod', [L, D, 6 * D])
        b_modT = inp('b_modT', [L, 128, 48])
        w_in = inp('w_in', [L, D, DIN])
        out = k.dram('out', [NB, 8, 128, NLAT], F32, kind='ExternalOutput')
        dbg_out = None
        if dbg is not None:
            dbg_out = k.dram('dbg', dbg, F32, kind='ExternalOutput')

        zT = k.dram('zT', [NB, NZC, 128, T])
        PS = [k.ps([128, 512], F32, name='psb%d' % i) for i in range(8)]
        psi = [0]

        def nps():
            psi[0] = psi[0] % 7 + 1
            return PS[psi[0]]

        CST = {'PS0': PS[0]}
        for nm_, shp in (('vecs', [128, L, NVC]), ('cst', [128, NCST]), ('rope', [128, 2, NLAT])):
            inp(nm_, shp)
            CST[nm_] = k.sb(shp, F32, name=nm_ + '_sb')
            k.dma(CST[nm_][:], IN[nm_][:], [IN[nm_]], [CST[nm_]])
        CST['cstb'] = k.sb([128, NCST], BF16, name='cstb')
        k.cp(CST['cstb'][:], CST['cst'][:], [CST['cst']], [CST['cstb']])
        inp('rwkv_w2', [L, 2, 64, 256])
        inp('rwkv_a2', [L, 2, 64, 256])
        inp('rwkv_g2', [L, 128, 256])
        CST['SI'] = {nm_: k.dram('SI_' + nm_, [3, 2, NB, 2, 128, T]) for nm_ in ('q', 'k', 'v', 'w', 'kk', 'b')}
        CST['YC'] = k.dram('YC', [3, 2, NB, 2, 128, T])
        CST['BR'] = k.dram('BR', [4, NB, 2, 128, T])
        CST['AUX'] = k.dram('AUX', [2, NB, 2, 128, T])
        CST['KA'] = k.dram('KA', [NB, 128, T])
        inp('w_gate', [L, 4, D, D])
        inp('w_branch', [L, 4, 256, D])
        inp('w_out', [L, D, D])
        x1T = k.dram('x1T', [NB, 8, 128, T])
        x2T = k.dram('x2T', [NB, 8, 128, T])
        inp('selm', [16, 2048])
        inp('w_router', [L, D, 16])
        inp('w_e1', [L, 16, D, D])
        inp('w_e3', [L, 16, D, D])
        inp('w_e2', [L, 16, D, D])
        modT = k.sb([128, 48, 3], F32, name='modT')
        silu_c = k.sb([128, 8, 3], F32, name='silu_c')
        bmod = k.sb([128, 48], F32, name='bmod')
        ctile = k.sb([128, 8, 3], F32, name='ctile')
        k.dma(ctile[:], cT[:], reads=[cT], writes=[ctile])
        k.act(silu_c[:], ctile[:], AF.Silu, [ctile], [silu_c])

        xT_cur = xT0
        for l in range(L):
            with ExitStack() as ph:
                k.es = ph
                k.dma(bmod[:], b_modT[l], reads=[b_modT], writes=[bmod])
                wst = [k.sb([128, 8, 768], F32, name='wmod_st%d_%d' % (l, i)) for i in range(2)]
                wv = w_mod.h.ap()[l].rearrange("(k p) n -> p k n", p=128)
                for jb in range(8):
                    w = wst[jb % 2]
                    k.dma(w[:], wv[:, :, jb * 768:(jb + 1) * 768], reads=[w_mod], writes=[w])
                    for jj in range(6):
                        j = jb * 6 + jj
                        p = nps()
                        for kk in range(8):
                            k.mm(p[:, 0:3], w[:, kk, jj * 128:(jj + 1) * 128], silu_c[:, kk, :], kk == 0, kk == 7,
                                 [w, silu_c], [p])
                        k.ts(modT[:, j, :], p[:, 0:3], bmod[:, j:j + 1], ALU.add, [p, bmod], [modT])
                for lo_ in (8, 32):
                    k.ts(modT[:, lo_:lo_ + 8, :], modT[:, lo_:lo_ + 8, :], 1.0, ALU.add, [modT], [modT])
                k.barrier()
            if stop_after == 'A':
                k.es = es
                t = k.sb([128, 144], F32, name='dbgt')
                k.cp(t[:], modT[:].rearrange("p a b -> p (a b)"), [modT], [t])
                k.dma(dbg_out[:], t[:], reads=[t], writes=[dbg_out])
                k.final_wait([dbg_out])
                return nc
            with ExitStack() as ph:
                k.es = ph
                wbf = k.sb([128, 8, DIN], BF16, name='win_bf%d' % l)
                wst = [k.sb([128, 8, 496], F32, name='win_st%d_%d' % (l, i)) for i in range(2)]
                wv = w_in.h.ap()[l].rearrange("(k p) n -> p k n", p=128)
                for cb in range(8):
                    w = wst[cb % 2]
                    k.dma(w[:], wv[:, :, cb * 496:(cb + 1) * 496], reads=[w_in], writes=[w])
                    k.cp(wbf[:, :, cb * 496:(cb + 1) * 496], w[:], [w], [wbf], eng='pool')
                xts = [k.sb([128, 8, 512], F32, name='xt%d_%d' % (l, i)) for i in range(2)]
                uts = [k.sb([128, 8, 512], BF16, name='ut%d_%d' % (l, i)) for i in range(2)]
                zsb = [k.sb([128, 512], F32, name='zsb%d_%d' % (l, i)) for i in range(4)]
                it = 0
                zi = 0
                for b in range(NB):
                    for (t0, n, isctx) in tok_blocks():
                        r = 2 if isctx else b
                        xt = xts[it % 2]
                        ut = uts[it % 2]
                        it += 1
                        k.dma(xt[:, :, :n], xT_cur[b][:, :, t0:t0 + n].rearrange("k p t -> p k t"),
                              reads=[xT_cur], writes=[xt])
                        for kk in range(8):
                            k.act(ut[:, kk, :n], xt[:, kk, :n], AF.Identity, [xt, modT], [ut],
                                  bias=modT[:, kk, r:r + 1], scale=modT[:, 8 + kk, r:r + 1])
                        for mc in range(NZC):
                            p = nps()
                            for kk in range(8):
                                k.mm(p[:, :n], wbf[:, kk, mc * 128:(mc + 1) * 128], ut[:, kk, :n], kk == 0, kk == 7,
                                     [wbf, ut], [p])
                            zs = zsb[zi % 4]
                            zi += 1
                            if zi % 2 == 0:
                                k.cp(zs[:, :n], p[:, :n], [p], [zs], eng='act')
                            else:
                                k.cp(zs[:, :n], p[:, :n], [p], [zs], eng='dve')
                            k.dma(zT[b][mc][:, t0:t0 + n], zs[:, :n], reads=[zs], writes=[zT])
                k.barrier()
            if stop_after == 'B':
                k.es = es
                t = k.sb([128, NZC, 512], F32, name='dbgt')
                k.dma(t[:], zT[0][:, :, 0:512].rearrange("c p t -> p c t"), reads=[zT], writes=[t])
                k.dma(dbg_out[:].rearrange("c p t -> p c t"), t[:], reads=[t], writes=[dbg_out])
                k.final_wait([dbg_out])
                return nc
            mixers_phase(k, l, zT, IN, CST, nps, stop_after, dbg_out)
            if stop_after != 'C':
                attn_phase(k, l, zT, CST, nps)
                merge_phase(k, l, xT_cur, x1T, modT, IN, CST, nps)
            if stop_after == 'D':
                k.es = es
                for b_ in range(NB):
                    t_ = k.sb([128, 8, T], F32, name='dbgD%d' % b_)
                    k.dma(t_[:], x1T[b_].rearrange("c p t -> p c t"), [x1T], [t_])
                    k.dma(dbg_out[b_].rearrange("c p t -> p c t"), t_[:], [t_], [dbg_out])
                k.final_wait([dbg_out])
                return nc
            if stop_after != 'C':
                if l < L - 1:
                    moe_phase(k, l, x1T, x2T, 0, modT, IN, CST, nps)
                    xT_cur = x2T
                    if stop_after == 'E':
                        k.es = es
                        for b_ in range(NB):
                            t_ = k.sb([128, 8, T], F32, name='dbgE%d' % b_)
                            k.dma(t_[:], x2T[b_].rearrange("c p t -> p c t"), [x2T], [t_])
                            k.dma(dbg_out[b_].rearrange("c p t -> p c t"), t_[:], [t_], [dbg_out])
                        k.final_wait([dbg_out])
                        return nc
                else:
                    moe_phase(k, l, x1T, out, NCTX, modT, IN, CST, nps)
            if stop_after == 'C':
                k.es = es
                BR = CST['BR']
                for m_ in range(3):
                    t_ = k.sb([128, 2, T], F32, name='dbgC%d' % m_)
                    k.dma(t_[:], BR[m_][0].rearrange("h p t -> p h t"), [BR], [t_])
                    k.dma(dbg_out[m_].rearrange("h p t -> p h t"), t_[:], [t_], [dbg_out])
                k.final_wait([dbg_out])
                return nc
        k.es = es
        k.final_wait([out])
    return nc


def make_inputs(inputs, core):
    b0 = core * NB
    x = inputs['x'][b0:b0 + NB]
    ctx = inputs['ctx'][b0:b0 + NB]
    xa = np.concatenate([ctx, x], axis=1)
    xT0 = np.ascontiguousarray(xa.transpose(0, 2, 1)).reshape(NB, 8, 128, T)
    c3 = np.stack([inputs['c'][b0], inputs['c'][b0 + 1], inputs['c_ctx']], axis=1)
    cT = np.ascontiguousarray(c3.reshape(8, 128, 3).transpose(1, 0, 2))
    b_modT = np.ascontiguousarray(inputs['b_mod'].reshape(L, 48, 128).transpose(0, 2, 1))
    d_ = {'xT0': xT0, 'cT': cT, 'w_mod': inputs['w_mod'], 'b_modT': b_modT, 'w_in': inputs['w_in']}
    d_.update(host_consts(inputs))
    selm = np.zeros((16, 2048), np.float32)
    for e_ in range(16):
        selm[e_, e_ * 128:(e_ + 1) * 128] = 1.0
    d_['selm'] = selm
    for nm_ in ('rwkv_w2', 'rwkv_a2', 'rwkv_g2', 'w_gate', 'w_branch', 'w_out', 'w_router', 'w_e1', 'w_e3', 'w_e2'):
        d_[nm_] = inputs[nm_]
    return d_


_HC = {}


def host_consts(inputs):
    if 'v' in _HC:
        return _HC['v']
    vecs = np.zeros((128, L, NVC), np.float32)

    def put(name, arr):
        a = np.asarray(arr, np.float32).reshape(L, -1, 128)
        for j in range(a.shape[1]):
            vecs[:, :, VC[name] + j] = a[:, j, :].T
    put('mu', inputs['rwkv_mu'])
    put('w0', inputs['rwkv_w0'].reshape(L, 512))
    put('a0', inputs['rwkv_a0'].reshape(L, 512))
    put('kk', inputs['rwkv_kk'])
    put('ka', inputs['rwkv_ka'])
    put('rk', inputs['rwkv_rk'])
    put('lng', inputs['rwkv_ln_g'])
    put('lnb', inputs['rwkv_ln_b'])
    put('retdec', np.repeat(inputs['ret_decay'].reshape(L, 8), 64, axis=1))
    put('retg', inputs['ret_norm_g'])
    put('retb', inputs['ret_norm_b'])
    hl = inputs['hgrn_lb']
    put('hlb', np.broadcast_to(hl.reshape(1, 2 * L * 256), (L, 2 * L * 256)))
    put('hng', inputs['hgrn_norm_g'])
    put('aqg', np.tile(inputs['attn_q_g'], (1, 2)))
    put('akg', np.tile(inputs['attn_k_g'], (1, 2)))
    put('ln1g', inputs['ln1_g'])
    put('ln1b', inputs['ln1_b'])
    put('ln2g', inputs['ln2_g'])
    put('ln2b', inputs['ln2_b'])
    cst = np.zeros((128, NCST), np.float32)
    p = np.arange(128)
    cst[:, C_BO:C_BO + 128] = (p[:, None] // 64 == p[None, :] // 64)
    cst[:, C_DM:C_DM + 64] = (p[:, None] % 64 == np.arange(64)[None, :])
    rot = np.zeros((128, 128), np.float32)
    for m_ in range(128):
        if m_ % 64 < 32:
            rot[m_ + 32, m_] = -1.0
        else:
            rot[m_ - 32, m_] = 1.0
    cst[:, C_ROT:C_ROT + 128] = rot
    cst[:, C_ID:C_ID + 128] = np.eye(128)
    cst[:, C_ONE:C_ONE + 128] = 1.0
    si_ = np.arange(128)[:, None]
    ti_ = np.arange(64)[None, :]
    for h_ in range(2):
        cst[:, C_MF + 64 * h_:C_MF + 64 * h_ + 64] = (ti_ >= si_)
        cst[:, C_MR + 64 * h_:C_MR + 64 * h_ + 64] = (ti_ <= si_)
        cst[:, C_MFS + 64 * h_:C_MFS + 64 * h_ + 64] = (ti_ > si_)
        cst[:, C_MRS + 64 * h_:C_MRS + 64 * h_ + 64] = (ti_ < si_)
    rows = NLAT // 64
    row = np.repeat(np.arange(rows), 64)
    col = np.tile(np.arange(64), rows)
    inv = (10000.0 ** (-np.arange(16, dtype=np.float32) / 16)).astype(np.float32)
    ang = np.concatenate([row[:, None] * inv, col[:, None] * inv], axis=-1).astype(np.float32)
    cosT = np.cos(ang).astype(np.float32).T
    sinT = np.sin(ang).astype(np.float32).T
    rope = np.stack([np.tile(cosT, (4, 1)), np.tile(sinT, (4, 1))], axis=1)
    _HC['v'] = {'vecs': vecs, 'cst': cst, 'rope': np.ascontiguousarray(rope, dtype=np.float32)}
    return _HC['v']


def kernel(**inputs):
    inputs = {k_: np.asarray(v) for k_, v in inputs.items()}
    nc = build()
    in_maps = [make_inputs(inputs, c) for c in range(8)]
    res = run_bass_kernel_spmd(nc, in_maps, core_ids=list(range(8)))
    outs = []
    for c in range(8):
        o = res.results[c]['out']
        outs.append(o.reshape(NB, D, NLAT).transpose(0, 2, 1))
    return np.ascontiguousarray(np.concatenate(outs, axis=0)).astype(np.float32)
```
